# Optimizing a Trainium2 kernel written in Bass

```python
import math
import jax
import jax.numpy as jnp
from jax import lax
import numpy as np

D_MODEL = 1024
BATCH = 32
SEQ = 256
DEPTH = 2
DEC_BATCH = 4
DEC_SEQ = 1024
PAST_LEN = 512

F32 = jnp.float32
GRID_W = 64
EPS = 1e-6
GN_EPS = 1e-5
S5_WIDTH = D_MODEL // 2
S5_GROUP_CH = 16
S5_GROUPS = S5_WIDTH // S5_GROUP_CH
S5_STATE = 64
RET_WIDTH = D_MODEL // 2
RET_HEADS = 4
RET_DK = RET_WIDTH // RET_HEADS
RET_DV = RET_WIDTH // RET_HEADS
RET_CHUNK = 128
ROPE_BASE = 10000.0
HY_WIDTH = D_MODEL // 2
HY_ORDER = 2
HY_BANDS = 16
HY_EMB = 1 + 2 * HY_BANDS
HY_HIDDEN = 64
HY_DECAY_MIN = -math.log(1e-2) / 1.5
HY_DECAY_MAX = -math.log(1e-2) / 0.3
N_BRANCH = 3
IN_COLS = S5_WIDTH + 4 * RET_WIDTH + 3 * HY_WIDTH + N_BRANCH * D_MODEL
D_FF = 256 * (-(-8 * D_MODEL // (3 * 256)))

kernel_name = 'hybrid_s5_retnet_hyena_diffusion_step'


def _rms_norm(x, g):
    xf = x.astype(F32)
    y = xf * lax.rsqrt(jnp.mean(xf * xf, axis=-1, keepdims=True) + EPS)
    return (y * g.astype(F32)).astype(x.dtype)


def _head_norm(x):
    xc = x - jnp.mean(x, axis=-1, keepdims=True)
    return xc * lax.rsqrt(jnp.mean(xc * xc, axis=-1, keepdims=True) + GN_EPS)


def _diag_scan(lam_bar, bu, h0):
    if h0 is not None:
        bu = bu.at[:, 0].add(lam_bar * h0)
    a = jnp.broadcast_to(lam_bar, bu.shape)

    def combine(left, right):
        a_l, b_l = left
        a_r, b_r = right
        return a_l * a_r, a_r * b_l + b_r

    _, h = lax.associative_scan(combine, (a, bu), axis=1)
    return h


def _s5(u, lp, h0):
    bsz, L, _ = u.shape
    uf = u.astype(F32).reshape(bsz, L, S5_GROUPS, S5_GROUP_CH)
    lam = lax.complex(lp['s5_lam_re'].astype(F32), lp['s5_lam_im'].astype(F32))
    dt = jnp.exp(lp['s5_log_dt'].astype(F32))[..., None]
    lam_bar = jnp.exp(lam * dt)
    b = lax.complex(lp['s5_b_re'].astype(F32), lp['s5_b_im'].astype(F32))
    b_bar = ((lam_bar - 1.0) / lam)[..., None] * b
    c = lax.complex(lp['s5_c_re'].astype(F32), lp['s5_c_im'].astype(F32))
    bu = jnp.einsum('blgc,rgpc->rblgp', uf, b_bar)
    h_f = _diag_scan(lam_bar[0], bu[0], None if h0 is None else h0[:, 0])
    h_b = jnp.flip(_diag_scan(lam_bar[1], jnp.flip(bu[1], axis=1), None if h0 is None else h0[:, 1]), axis=1)
    y = jnp.real(jnp.einsum('blgp,gcp->blgc', h_f, c[0]) + jnp.einsum('blgp,gcp->blgc', h_b, c[1]))
    y = y.reshape(bsz, L, S5_WIDTH) + lp['s5_d'].astype(F32) * u.astype(F32)
    final = jnp.stack([h_f[:, -1], h_b[:, 0]], axis=1)
    return y.astype(u.dtype), final


def _rope_2d(x):
    L, dk = x.shape[1], x.shape[-1]
    n_rows = L // GRID_W
    rows = jnp.repeat(jnp.arange(n_rows, dtype=F32), GRID_W)
    cols = jnp.tile(jnp.arange(GRID_W, dtype=F32), n_rows)
    half = dk // 2
    n_freq = half // 2
    inv = ROPE_BASE ** (-jnp.arange(n_freq, dtype=F32) / n_freq)
    ang = jnp.concatenate([rows[:, None] * inv, cols[:, None] * inv], axis=-1)
    cos = jnp.cos(ang)[None, :, None, :]
    sin = jnp.sin(ang)[None, :, None, :]
    x1, x2 = x[..., :half], x[..., half:]
    return jnp.concatenate([x1 * cos - x2 * sin, x1 * sin + x2 * cos], axis=-1)


def _retention_dir(q, k, v, log_g, inclusive, s0=None, q0=None):
    bsz, L, H, dk = q.shape
    dv = v.shape[-1]
    n = L // RET_CHUNK
    qc = q.reshape(bsz, n, RET_CHUNK, H, dk)
    kc = k.reshape(bsz, n, RET_CHUNK, H, dk)
    vc = v.reshape(bsz, n, RET_CHUNK, H, dv)
    pos = jnp.arange(RET_CHUNK, dtype=F32)
    diff = pos[:, None] - pos[None, :]
    mask = (diff >= 0) if inclusive else (diff > 0)
    decay = jnp.where(mask[None], jnp.exp(log_g[:, None, None] * jnp.maximum(diff, 0.0)[None]), 0.0)
    scores = jnp.einsum('bnihd,bnjhd->bnhij', qc, kc) * decay
    intra = jnp.einsum('bnhij,bnjhe->bnihe', scores, vc)
    k_decay = jnp.exp(log_g[:, None] * (RET_CHUNK - 1.0 - pos)[None])
    kv = jnp.einsum('bnjhd,hj,bnjhe->nbhde', kc, k_decay, vc)
    chunk_decay = jnp.exp(log_g * RET_CHUNK)[:, None, None]

    def step(s, kv_n):
        return chunk_decay * s + kv_n, s

    s_final, s_prev = lax.scan(step, jnp.zeros((bsz, H, dk, dv), F32), kv)
    q_decay = jnp.exp(log_g[:, None] * (pos + 1.0)[None])
    cross = jnp.einsum('bnihd,nbhde,hi->bnihe', qc, s_prev, q_decay)
    out = (intra + cross).reshape(bsz, L, H, dv)
    if s0 is not None:
        t = jnp.arange(L, dtype=F32) + (1.0 if inclusive else 0.0)
        out = out + jnp.einsum('blhd,bhde,hl->blhe', q0, s0, jnp.exp(log_g[:, None] * t[None]))
    return out, s_final


def _flip_seq(a):
    return jnp.flip(a, axis=1)


def _retention_bidir(q, k, v, log_g, q_ctx=None, s0=None):
    fwd, s_f = _retention_dir(q, k, v, log_g[0], True,
                              None if s0 is None else s0[:, 0], q_ctx)
    bwd, s_b = _retention_dir(_flip_seq(q), _flip_seq(k), _flip_seq(v), log_g[1], False,
                              None if s0 is None else s0[:, 1],
                              None if q_ctx is None else _flip_seq(q_ctx))
    return fwd + _flip_seq(bwd), jnp.stack([s_f, s_b], axis=1)


def _short_conv(x, w, b):
    L = x.shape[1]
    xp = jnp.pad(x, ((0, 0), (1, 1), (0, 0)))
    return xp[:, :L] * w[0] + xp[:, 1:L + 1] * w[1] + xp[:, 2:] * w[2] + b


def _hyena_filter_spectra(L, lp):
    t = jnp.arange(L, dtype=F32)
    t_norm = t / L
    bands = jnp.linspace(1e-4, HY_BANDS - 1, HY_BANDS, dtype=F32)
    ang = (2.0 * math.pi / L) * t[:, None] * bands[None, :]
    z = jnp.concatenate([t_norm[:, None], jnp.cos(ang), -jnp.sin(ang)], axis=-1)
    freq = lp['hy_freq'].astype(F32)
    hid = jnp.sin(freq[0] * (z @ lp['hy_w1'].astype(F32) + lp['hy_b1'].astype(F32)))
    hid = jnp.sin(freq[1] * (hid @ lp['hy_w2'].astype(F32) + lp['hy_b2'].astype(F32)))
    filt = (hid @ lp['hy_w3'].astype(F32)).reshape(L, 2, HY_ORDER, HY_WIDTH)
    rate = jnp.linspace(HY_DECAY_MIN, HY_DECAY_MAX, HY_WIDTH, dtype=F32)
    filt = filt * jnp.exp(-t_norm[:, None, None, None] * rate)
    fwd, bwd = filt[:, 0], filt[:, 1]
    kern = jnp.concatenate([fwd, jnp.zeros((1, HY_ORDER, HY_WIDTH), F32), bwd[:0:-1]], axis=0)
    kern = kern * lax.rsqrt(jnp.sum(kern * kern, axis=0, keepdims=True) + EPS)
    return jnp.fft.rfft(kern, axis=0)


def _hyena(hy, lp):
    L = hy.shape[1]
    z = _short_conv(hy.astype(F32), lp['hy_conv_w'].astype(F32), lp['hy_conv_b'].astype(F32))
    x1, x2, v = jnp.split(z, 3, axis=-1)
    spec = _hyena_filter_spectra(L, lp)
    bias = lp['hy_bias'].astype(F32)
    out = v
    for o, gate in enumerate((x1, x2)):
        conv = jnp.fft.irfft(jnp.fft.rfft(out, n=2 * L, axis=1) * spec[None, :, o], n=2 * L, axis=1)[:, :L]
        out = gate * (conv + bias[o] * out)
    return out.astype(hy.dtype)


def _mixer(h, lp, latent, s5_h0, ret_s0):
    bsz, L, _ = h.shape
    widths = [S5_WIDTH, RET_WIDTH, RET_WIDTH, RET_WIDTH, RET_WIDTH, 3 * HY_WIDTH]
    cuts = [int(cv) for cv in np.cumsum(widths)]
    u, q, k, v, g, hy, gate_logits = jnp.split(h @ lp['w_in'], cuts, axis=-1)
    y_s5, s5_state = _s5(u, lp, s5_h0)
    a, b = jnp.split(jax.nn.gelu(y_s5) @ lp['w_s5_glu'], 2, axis=-1)
    br_s5 = a * jax.nn.sigmoid(b)
    q = q.astype(F32).reshape(bsz, L, RET_HEADS, RET_DK)
    k = k.astype(F32).reshape(bsz, L, RET_HEADS, RET_DK) * (RET_DK ** -0.5)
    v = v.astype(F32).reshape(bsz, L, RET_HEADS, RET_DV)
    log_g = jnp.log1p(-jnp.exp(lp['ret_decay'].astype(F32)))
    if latent:
        ret, ret_state = _retention_bidir(_rope_2d(q), _rope_2d(k), v, log_g, q, ret_s0)
    else:
        ret, ret_state = _retention_bidir(q, k, v, log_g)
    ret = _head_norm(ret).reshape(bsz, L, RET_WIDTH) * jax.nn.silu(g.astype(F32))
    br_ret = ret.astype(h.dtype) @ lp['w_ret_o']
    br_hy = _hyena(hy, lp) @ lp['w_hy_o']
    gates = jax.nn.sigmoid(gate_logits.astype(F32)).astype(h.dtype).reshape(bsz, L, N_BRANCH, D_MODEL)
    merged = gates[:, :, 0] * br_s5 + gates[:, :, 1] * br_ret + gates[:, :, 2] * br_hy
    return merged @ lp['w_out'], s5_state, ret_state


def _swiglu(h, w_in, w_out):
    a, b = jnp.split(h @ w_in, 2, axis=-1)
    return (jax.nn.silu(a) * b) @ w_out


def _layer(x, cond, lp, latent, s5_h0, ret_s0):
    mod = (cond @ lp['w_mod'] + lp['b_mod'])[:, None, :]
    sh1, sc1, g1, sh2, sc2, g2 = jnp.split(mod, 6, axis=-1)
    h = _rms_norm(x, lp['norm1']) * (1.0 + sc1) + sh1
    mix, s5_state, ret_state = _mixer(h, lp, latent, s5_h0, ret_s0)
    x = x + g1 * mix
    h = _rms_norm(x, lp['norm2']) * (1.0 + sc2) + sh2
    x = x + g2 * _swiglu(h, lp['w_ffn_in'], lp['w_ffn_out'])
    return x, s5_state, ret_state


def setup_inputs(seed: int = 0) -> dict:
    key = jax.random.key(seed)
    ks = jax.random.split(key, 36)

    def nrm(i, shape, scale):
        return scale * jax.random.normal(ks[i], shape, F32)

    s5_shape = (DEPTH, 2, S5_GROUPS, S5_STATE)
    lam_im = jnp.pi * jnp.arange(S5_STATE, dtype=F32) + nrm(12, s5_shape, 0.01)
    ret_decay = -(5.0 + jnp.arange(RET_HEADS, dtype=F32)) * math.log(2.0) + nrm(20, (DEPTH, 2, RET_HEADS), 0.05)
    return {
        'x_prompt': nrm(0, (BATCH, SEQ, D_MODEL), 1.0),
        'x_sample': nrm(1, (DEC_BATCH, DEC_SEQ, D_MODEL), 1.0),
        'state_s5': nrm(2, (DEC_BATCH, DEPTH, 2, S5_GROUPS, S5_STATE, 2), 0.1),
        'state_ret': nrm(3, (DEC_BATCH, DEPTH, 2, RET_HEADS, RET_DK, RET_DV), 0.5),
        'c': nrm(4, (DEC_BATCH, D_MODEL), 1.0),
        'c_ctx': nrm(5, (D_MODEL,), 1.0),
        'w_mod': nrm(6, (DEPTH, D_MODEL, 6 * D_MODEL), 0.5 * D_MODEL ** -0.5),
        'b_mod': nrm(7, (DEPTH, 6 * D_MODEL), 0.02),
        'norm1': 1.0 + nrm(8, (DEPTH, D_MODEL), 0.02),
        'norm2': 1.0 + nrm(9, (DEPTH, D_MODEL), 0.02),
        'w_in': nrm(10, (DEPTH, D_MODEL, IN_COLS), D_MODEL ** -0.5),
        's5_lam_re': -0.5 + nrm(11, s5_shape, 0.01),
        's5_lam_im': lam_im,
        's5_log_dt': jax.random.uniform(ks[13], (DEPTH, 2, S5_GROUPS), F32, math.log(1e-3), math.log(1e-1)),
        's5_b_re': nrm(14, (DEPTH, 2, S5_GROUPS, S5_STATE, S5_GROUP_CH), (2.0 * S5_GROUP_CH) ** -0.5),
        's5_b_im': nrm(15, (DEPTH, 2, S5_GROUPS, S5_STATE, S5_GROUP_CH), (2.0 * S5_GROUP_CH) ** -0.5),
        's5_c_re': nrm(16, (DEPTH, 2, S5_GROUPS, S5_GROUP_CH, S5_STATE), S5_STATE ** -0.5),
        's5_c_im': nrm(17, (DEPTH, 2, S5_GROUPS, S5_GROUP_CH, S5_STATE), S5_STATE ** -0.5),
        's5_d': nrm(18, (DEPTH, S5_WIDTH), 1.0),
        'w_s5_glu': nrm(19, (DEPTH, S5_WIDTH, 2 * D_MODEL), S5_WIDTH ** -0.5),
        'ret_decay': ret_decay,
        'w_ret_o': nrm(21, (DEPTH, RET_WIDTH, D_MODEL), RET_WIDTH ** -0.5),
        'hy_conv_w': nrm(22, (DEPTH, 3, 3 * HY_WIDTH), 3.0 ** -0.5),
        'hy_conv_b': nrm(23, (DEPTH, 3 * HY_WIDTH), 0.02),
        'hy_w1': nrm(24, (DEPTH, HY_EMB, HY_HIDDEN), HY_EMB ** -0.5),
        'hy_b1': nrm(25, (DEPTH, HY_HIDDEN), 0.1),
        'hy_w2': nrm(26, (DEPTH, HY_HIDDEN, HY_HIDDEN), HY_HIDDEN ** -0.5),
        'hy_b2': nrm(27, (DEPTH, HY_HIDDEN), 0.1),
        'hy_freq': 1.0 + nrm(28, (DEPTH, 2, HY_HIDDEN), 0.02),
        'hy_w3': nrm(29, (DEPTH, HY_HIDDEN, 2 * HY_ORDER * HY_WIDTH), HY_HIDDEN ** -0.5),
        'hy_bias': nrm(30, (DEPTH, HY_ORDER, HY_WIDTH), 1.0),
        'w_hy_o': nrm(31, (DEPTH, HY_WIDTH, D_MODEL), HY_WIDTH ** -0.5),
        'w_out': nrm(32, (DEPTH, D_MODEL, D_MODEL), D_MODEL ** -0.5),
        'w_ffn_in': nrm(33, (DEPTH, D_MODEL, 2 * D_FF), D_MODEL ** -0.5),
        'w_ffn_out': nrm(34, (DEPTH, D_FF, D_MODEL), D_FF ** -0.5),
        'norm_f': 1.0 + nrm(35, (D_MODEL,), 0.02),
    }


def reference(x_prompt, x_sample, state_s5, state_ret, c, c_ctx, w_mod, b_mod, norm1, norm2, w_in,
              s5_lam_re, s5_lam_im, s5_log_dt, s5_b_re, s5_b_im, s5_c_re, s5_c_im, s5_d, w_s5_glu,
              ret_decay, w_ret_o, hy_conv_w, hy_conv_b, hy_w1, hy_b1, hy_w2, hy_b2, hy_freq, hy_w3, hy_bias,
              w_hy_o, w_out, w_ffn_in, w_ffn_out, norm_f):
    cond_ctx = jax.nn.silu(c_ctx)[None, :]
    cond_lat = jax.nn.silu(c)
    s5_cache = lax.complex(state_s5[..., 0].astype(F32), state_s5[..., 1].astype(F32))
    ret_cache = state_ret.astype(F32)
    xp, xs = x_prompt, x_sample
    s5_states, ret_states = [], []
    for l in range(DEPTH):
        lp = {
            'w_mod': w_mod[l], 'b_mod': b_mod[l], 'norm1': norm1[l], 'norm2': norm2[l], 'w_in': w_in[l],
            's5_lam_re': s5_lam_re[l], 's5_lam_im': s5_lam_im[l], 's5_log_dt': s5_log_dt[l],
            's5_b_re': s5_b_re[l], 's5_b_im': s5_b_im[l], 's5_c_re': s5_c_re[l], 's5_c_im': s5_c_im[l],
            's5_d': s5_d[l], 'w_s5_glu': w_s5_glu[l], 'ret_decay': ret_decay[l], 'w_ret_o': w_ret_o[l],
            'hy_conv_w': hy_conv_w[l], 'hy_conv_b': hy_conv_b[l], 'hy_w1': hy_w1[l], 'hy_b1': hy_b1[l],
            'hy_w2': hy_w2[l], 'hy_b2': hy_b2[l], 'hy_freq': hy_freq[l], 'hy_w3': hy_w3[l],
            'hy_bias': hy_bias[l], 'w_hy_o': w_hy_o[l], 'w_out': w_out[l],
            'w_ffn_in': w_ffn_in[l], 'w_ffn_out': w_ffn_out[l],
        }
        xp, s5_st, ret_st = _layer(xp, cond_ctx, lp, False, None, None)
        s5_states.append(s5_st)
        ret_states.append(ret_st)
        xs, _, _ = _layer(xs, cond_lat, lp, True, s5_cache[:, l], ret_cache[:, l])
    y_prompt = _rms_norm(xp, norm_f)
    y_sample = _rms_norm(xs, norm_f)
    s5_new = jnp.stack(s5_states, axis=1)
    new_state_s5 = jnp.stack([jnp.real(s5_new), jnp.imag(s5_new)], axis=-1).astype(x_prompt.dtype)
    new_state_ret = jnp.stack(ret_states, axis=1).astype(x_prompt.dtype)
    return (y_prompt, y_sample, new_state_s5, new_state_ret)
```

```python
import contextlib
import math
import numpy as np
import concourse.bass as bass
import concourse.mybir as mybir
from concourse.bass_utils import run_bass_kernel_spmd

F32 = mybir.dt.float32
F32R = mybir.dt.float32r
ALU = mybir.AluOpType
AF = mybir.ActivationFunctionType

ENGS = ("pe", "dve", "act", "pool", "sp")
DEMOD_ENG = "dve"
SKIP_OLD_SAME_ENGINE = True

D = 1024
T = 1024
NBLK = 2
DEPTH = 2
DFF = 2816
EPS = 1e-6
GN_EPS = 1e-5


class _Rec:
    def __init__(self):
        self.calls = []

    def __getattr__(self, name):
        def f(*a, **kw):
            self.calls.append((name, a, kw))
            return None
        return f


class Prog:
    def __init__(self, nc):
        self.nc = nc
        self.es = contextlib.ExitStack()
        self.q = {e: [] for e in ENGS}
        self.cnt = {}
        self.lastw = {}
        self.readers = {}
        self.waited = {e: {} for e in ENGS}
        self.sems = {}
        self.alias = {}

    def _x(self, names):
        out = []
        for n in names:
            out.extend(self.alias.get(n, (n,)))
        return tuple(out)

    def sb(self, name, shape, dt=F32):
        return self.es.enter_context(self.nc.sbuf_tensor("sb_" + name, list(shape), dt))

    def ps(self, name, shape, dt=F32):
        return self.es.enter_context(self.nc.psum_tensor("pp_" + name, list(shape), dt))

    def _sem(self, key):
        if key not in self.sems:
            self.sems[key] = self.es.enter_context(self.nc.semaphore("s_" + key))
            self.cnt[key] = 0
        return self.sems[key]

    def _deps(self, eng, reads, writes):
        toks = []
        for r in reads:
            t = self.lastw.get(r)
            if t is not None:
                toks.append(t)
        for w in writes:
            t = self.lastw.get(w)
            if t is not None:
                toks.append(t)
            toks.extend(self.readers.get(w, ()))
        waits = {}
        for (k, v, e) in toks:
            if e == eng and eng == "pe":
                continue
            if e == eng and eng != "sp" and SKIP_OLD_SAME_ENGINE and v < self.cnt.get(eng, 0):
                continue
            if self.waited[eng].get(k, 0) >= v:
                continue
            if waits.get(k, 0) < v:
                waits[k] = v
        for k, v in waits.items():
            self.waited[eng][k] = v
        return waits

    def _commit(self, tok, reads, writes):
        for r in reads:
            self.readers.setdefault(r, []).append(tok)
        for w in writes:
            self.lastw[w] = tok
            self.readers[w] = []

    def op(self, eng, fn, reads=(), writes=()):
        reads = self._x(reads)
        writes = self._x(writes)
        waits = self._deps(eng, reads, writes)
        self._sem(eng)
        self.cnt[eng] += 1
        tok = (eng, self.cnt[eng], eng)
        rec = _Rec()
        fn(rec)
        assert len(rec.calls) == 1
        name, a, kw = rec.calls[0]
        fn = (lambda e, name=name, a=a, kw=kw: getattr(e, name)(*a, **kw))
        self.q[eng].append((fn, waits, [(eng, 1)]))
        self._commit(tok, reads, writes)
        return tok

    def dma(self, pairs, sem, reads=(), writes=(), eng="sp"):
        reads = self._x(reads)
        writes = self._x(writes)
        waits = self._deps(eng, reads, writes)
        key = "d_" + sem
        self._sem(key)
        self.cnt[key] += 16 * len(pairs)
        tok = (key, self.cnt[key], "dma")
        fns = [(lambda e, o=o, i=i: e.dma_start(out=o, in_=i)) for (o, i) in pairs]
        self.q[eng].append((fns, waits, [(key, 16)]))
        self._commit(tok, reads, writes)
        return tok

    def wait_all(self, eng="sp"):
        waits = {}
        for k, v in self.cnt.items():
            if v > 0 and self.waited[eng].get(k, 0) < v:
                waits[k] = v
        self.q[eng].append((None, waits, []))

    def emit(self):
        nc = self.nc
        with nc.Block() as block:
            def run(eng_name):
                def body(e):
                    for (fn, waits, incs) in self.q[eng_name]:
                        for k, v in waits.items():
                            e.wait_ge(self.sems[k], v)
                        if fn is None:
                            continue
                        for f in (fn if isinstance(fn, list) else [fn]):
                            ins = f(e)
                            for (k, a) in incs:
                                ins.then_inc(self.sems[k], a)
                return body
            for name, reg in (("sp", block.sync), ("pe", block.tensor), ("dve", block.vector),
                              ("act", block.scalar), ("pool", block.gpsimd)):
                if self.q[name]:
                    reg(run(name))

    def close(self):
        self.es.close()


def build_program(dbg=None, branches=("s5", "ret", "hy"), ntiles=2, nlayers=DEPTH):
    dbg = dbg or {}
    nc = bass.Bass("TRN2", target_bir_lowering=False)
    P = Prog(nc)

    def din(name, shape, dt=F32):
        return nc.dram_tensor(name, list(shape), dt, kind="ExternalInput").ap()

    def dout(name, shape, dt=F32):
        return nc.dram_tensor(name, list(shape), dt, kind="ExternalOutput").ap()

    xT = [din("xT_p", [128, 8, T]), din("xT_s", [128, 8, T])]
    yT = [dout("yT_p", [128, 8, T]), dout("yT_s", [128, 8, T])]
    cond_d = din("cond", [128, 8, 2])
    bmod_d = din("b_mod_t", [128, DEPTH, 48])
    n1_d = din("norm1_t", [128, DEPTH, 8])
    n2_d = din("norm2_t", [128, DEPTH, 8])
    nf_d = din("normf_t", [128, 8])
    ident_d = din("ident", [128, 128])
    w_mod = din("w_mod", [DEPTH, D, 6 * D])
    w_in = din("w_in", [DEPTH, D, 7168])
    w_s5_glu = din("w_s5_glu", [DEPTH, 512, 2048])
    w_ret_o = din("w_ret_o", [DEPTH, 512, 1024])
    w_hy_o = din("w_hy_o", [DEPTH, 512, 1024])
    w_out = din("w_out", [DEPTH, D, D])
    w_ffn_in = din("w_ffn_in", [DEPTH, D, 2 * DFF])
    w_ffn_out = din("w_ffn_out", [DEPTH, DFF, D])
    retdec_d = din("ret_decay_bc", [128, DEPTH * 8])
    ramp_d = [din("ramp_p", [128, 512]), din("ramp_s", [128, 2048])]
    rope_d = din("rope_cs", [2, 128, 1024])
    pos_d = din("pos12", [2, 128, 1024])
    posT_d = din("posT", [128, 2, 2])
    perm_d = din("perm", [128, 128])
    sret_d = din("state_ret_c", [DEPTH, 2, 4, 128, 128])
    nsret_d = dout("new_state_ret_c", [4, DEPTH, 2, 4, 128, 128])
    hy_w3 = din("hy_w3", [DEPTH, 64, 2048])
    hyc_d = din("hyc", [128, DEPTH, 4, 12])
    hyb_d = din("hyb", [128, DEPTH, 2, 4])
    hw1_d = din("hw1", [33, DEPTH, 64])
    hw2_d = din("hw2", [64, DEPTH, 64])
    hys_d = din("hys", [64, DEPTH, 4])
    zT_d = din("zT", [2, 33, 1024])
    ntn_d = din("ntn", [128, 2, 8])
    rate_d = din("rate_bc", [128, 512])
    dftF_d = [din("dftF_p", [2, 256, 256]), din("dftF_s", [2, 1024, 1024])]
    dftG_d = [din("dftG_p", [2, 256, 256]), din("dftG_s", [2, 1024, 1024])]
    s5sp_d = din("s5sp", [128, DEPTH, 3, 32])
    s5h0_d = din("s5h0", [128, DEPTH, 32, 2])
    s5bz_d = din("s5bz", [DEPTH, 128, 2048])
    s5cz_d = din("s5cz", [DEPTH, 128, 2048])
    s5d_d = din("s5d", [128, DEPTH, 4])
    tpos_d = din("tpos", [128, 512])
    ns5_d = dout("new_state_s5_c", [4, DEPTH, 2, 32, 64, 2])
    dbg_out = {k: dout("dbg_" + k, shp) for k, shp in dbg.items()}

    X = P.sb("X", [128, 8, T])
    H = P.sb("H", [128, 8, T], F32R)
    BIG = P.sb("BIG", [128, 22, T], F32R)
    WSF = [P.sb(f"WS{i}", [128, 4096], F32R) for i in range(2)]
    WS = [WSF[0], WSF[1], WSF[0][:, 0:2048], WSF[0][:, 2048:4096], WSF[1][:, 0:2048], WSF[1][:, 2048:4096]]
    P.alias = {"TMP0": ("TMP0h0", "TMP0h1"), "TMP1": ("TMP1h0", "TMP1h1"), "TMP2": ("TMP2h0", "TMP2h1"), "TMP3": ("TMP3h0", "TMP3h1"),
               "WS0": ("WSh0", "WSh1"), "WS1": ("WSh2", "WSh3"), "WS2": ("WSh0",), "WS3": ("WSh1",), "WS4": ("WSh2",), "WS5": ("WSh3",)}
    TMP = [P.sb(f"TMP{i}", [128, 512]) for i in range(4)]
    RSTD = P.sb("RSTD", [128, 512])
    ones = P.sb("ones", [128, 128], F32R)
    ident = P.sb("ident", [128, 128])
    cst = P.sb("cst", [128, 8])
    scond = P.sb("scond", [128, 8, 2], F32R)
    n1 = P.sb("n1", [128, DEPTH, 8])
    n2 = P.sb("n2", [128, DEPTH, 8])
    nf = P.sb("nf", [128, 8])
    MODP = P.sb("MODP", [128, DEPTH, 2, 6, 8])
    PSB = [P.ps(f"ps{i}", [128, 512]) for i in range(8)]
    modv = TMP[1][:, 0:DEPTH * 96].rearrange("p (l j c) -> p l j c", l=DEPTH, c=2)
    bmod = TMP[2][:, 0:DEPTH * 48].rearrange("p (l j) -> p l j", l=DEPTH)
    cond = TMP[3][:, 0:16].rearrange("p (k c) -> p k c", c=2)
    LG = P.sb("LG", [128, DEPTH * 8])
    NLG = P.sb("NLG", [128, DEPTH * 8])
    perm = P.sb("perm", [128, 128], F32R)
    posT = P.sb("posT", [128, 2, 2])
    KD = P.sb("KD", [128, 2, 2])
    PT = [P.sb(f"PT{i}", [128, 512], F32R) for i in range(2)]
    OS = [P.sb(f"OS{i}", [128, 512], F32R) for i in range(2)]
    S0T = P.sb("S0T", [128, 2, 128], F32R)
    KTS = PT[0][:, :].rearrange("p (d c e) -> p d c e", d=2, c=2)
    HYC = P.sb("HYC", [128, DEPTH, 4, 12])
    HYB = P.sb("HYB", [128, DEPTH, 2, 4])
    BST = P.sb("BST", [128, 2, 128], F32R)
    HYS = P.sb("HYS", [64, DEPTH, 4])
    HYF = P.sb("HYF", [64, DEPTH, 2])
    NTN = P.sb("NTN", [128, 2, 8])
    H0 = P.sb("H0", [128, 32, 2])

    ST5 = P.sb("ST5", [128, 8, 2])
    S5D = P.sb("S5D", [128, DEPTH, 4])
    print("sbuf bytes remaining", nc.sbuf_bytes_remaining)

    st = {"ws": 0, "ps": 0, "tmp": 0, "psn": 6, "wsh": 0}

    def next_ws():
        i = st["ws"]
        st["ws"] = (i + 1) % 2
        return i

    def next_ps():
        i = st["ps"]
        st["ps"] = (i + 1) % st["psn"]
        return i

    def next_tmp():
        i = st["tmp"]
        st["tmp"] = (i + 1) % 4
        return i

    def load_w(pairs_fn, half=False):
        if half:
            s = 2 + st["wsh"]
            st["wsh"] = (st["wsh"] + 1) % 4
        else:
            s = next_ws()
        P.dma(pairs_fn(WS[s]), f"ws{s}", writes=[f"WS{s}"], eng="pool")
        return s

    def wview(s, kc, n, off=0):
        return WS[s][:, off:off + kc * n].rearrange("p (k n) -> p k n", k=kc)

    def dump(key, ap, reads):
        if key in dbg_out:
            P.dma([(dbg_out[key], ap)], "dbg_" + key, reads=reads)

    ROWBASE = {"MG": 0, "U": 8, "Y": 12, "SQ": 14, "GA": 0}

    def xr(buf, ks, ns):
        if buf in ROWBASE:
            return [f"B{ROWBASE[buf] + k}_{n}" for k in ks for n in ns]
        return [f"{buf}{k}_{n}" for k in ks for n in ns]

    BLK = [slice(0, 512), slice(512, 1024)]
    R8 = range(8)

    P.dma([(ident[:], ident_d)], "c_ident", writes=["ident"])
    P.dma([(cond, cond_d)], "c_cond", writes=["TMP3"])
    P.dma([(bmod, bmod_d)], "c_bmod", writes=["TMP2"])
    P.dma([(n1[:], n1_d)], "c_n1", writes=["n1"])
    P.dma([(n2[:], n2_d)], "c_n2", writes=["n2"])
    P.dma([(nf[:], nf_d)], "c_nf", writes=["nf"])
    P.op("dve", lambda e: e.memset(TMP[0][:, 0:128], 1.0), writes=["TMP0"])
    P.op("dve", lambda e: e.tensor_copy(ones[:], TMP[0][:, 0:128]), reads=["TMP0"], writes=["ones"])
    P.op("dve", lambda e: e.memset(cst[:, 0:1], EPS), writes=["cst"])
    P.op("dve", lambda e: e.memset(cst[:, 1:2], GN_EPS), writes=["cst"])
    P.op("dve", lambda e: e.memset(cst[:, 2:3], 0.0), writes=["cst"])
    P.op("act", lambda e: e.activation(scond[:], cond, AF.Silu), reads=["TMP3"], writes=["scond"])
    P.dma([(LG[:], retdec_d)], "c_lg", writes=["LG"])
    P.dma([(HYC[:], hyc_d)], "c_hyc", writes=["HYC"])
    P.dma([(S5D[:], s5d_d)], "c_s5d", writes=["S5D"])
    P.op("dve", lambda e: e.memset(cst[:, 4:5], math.pi / 2.0), writes=["cst"])
    P.dma([(HYB[:], hyb_d)], "c_hyb", writes=["HYB"])
    P.dma([(HYS[:], hys_d)], "c_hys", writes=["HYS"])
    P.dma([(NTN[:], ntn_d)], "c_ntn", writes=["NTN"])
    P.op("dve", lambda e: e.memset(TMP[3][:, 0:256], 0.0), writes=["TMP3"])
    P.op("dve", lambda e: e.tensor_copy(BST[:, :, :], TMP[3][:, 0:256].rearrange("p (a b) -> p a b", a=2)), reads=["TMP3"], writes=["BST"])
    P.op("dve", lambda e: e.tensor_tensor(HYF[:], HYS[:, :, 0:2], HYS[:, :, 2:4], ALU.mult), reads=["HYS"], writes=["HYF"])
    P.dma([(posT[:], posT_d)], "c_posT", writes=["posT"])
    P.dma([(perm[:], perm_d)], "c_perm", writes=["perm"], eng="pool")
    P.op("act", lambda e: e.activation(LG[:], LG[:], AF.Exp), reads=["LG"], writes=["LG"])
    P.op("dve", lambda e: e.memset(cst[:, 3:4], 1.0), writes=["cst"])
    P.op("act", lambda e: e.activation(LG[:], LG[:], AF.Ln, bias=cst[:, 3:4], scale=-1.0), reads=["LG", "cst"], writes=["LG"])
    P.op("dve", lambda e: e.tensor_scalar(NLG[:], LG[:], -1.0, None, ALU.mult), reads=["LG"], writes=["NLG"])

    for l in range(nlayers):
        for cb in range(12):
            s = load_w(lambda w, l=l, cb=cb: [(w[:, 0:4096].rearrange("p (k n) -> p k n", k=8),
                                                 w_mod[l][:, cb * 512:(cb + 1) * 512].rearrange("(k p) n -> p k n", p=128))])
            wv = wview(s, 8, 512)
            pi = next_ps()
            for m in range(4):
                j = cb * 4 + m
                for k in range(8):
                    P.op("pe", lambda e, pi=pi, wv=wv, m=m, k=k: e.matmul(
                        PSB[pi][:, 2 * m:2 * m + 2], wv[:, k, m * 128:(m + 1) * 128], scond[:, k, :],
                        start=(k == 0), stop=(k == 7)), reads=[f"WS{s}", "scond"], writes=[f"ps{pi}"])
            P.op("dve", lambda e, pi=pi, l=l, cb=cb: e.tensor_tensor(
                modv[:, l, cb * 4:(cb + 1) * 4, :], PSB[pi][:, 0:8].rearrange("p (m c) -> p m c", c=2),
                bmod[:, l, cb * 4:(cb + 1) * 4].unsqueeze(2).to_broadcast([128, 4, 2]), ALU.add),
                reads=[f"ps{pi}", "TMP2"], writes=["TMP1"])
        for t in range(2):
            def mv(i, l=l, t=t):
                return modv[:, l, i * 8:(i + 1) * 8, t]
            P.op("dve", lambda e, l=l, t=t, mv=mv: e.scalar_tensor_tensor(
                MODP[:, l, t, 0, :], mv(1), 1.0, n1[:, l, :], ALU.add, ALU.mult), reads=["TMP1", "n1"], writes=["MODP"])
            P.op("dve", lambda e, l=l, t=t, mv=mv: e.tensor_copy(MODP[:, l, t, 1, :], mv(0)), reads=["TMP1"], writes=["MODP"])
            P.op("dve", lambda e, l=l, t=t, mv=mv: e.tensor_copy(MODP[:, l, t, 2, :], mv(2)), reads=["TMP1"], writes=["MODP"])
            P.op("dve", lambda e, l=l, t=t, mv=mv: e.scalar_tensor_tensor(
                MODP[:, l, t, 3, :], mv(4), 1.0, n2[:, l, :], ALU.add, ALU.mult), reads=["TMP1", "n2"], writes=["MODP"])
            P.op("dve", lambda e, l=l, t=t, mv=mv: e.tensor_copy(MODP[:, l, t, 4, :], mv(3)), reads=["TMP1"], writes=["MODP"])
            P.op("dve", lambda e, l=l, t=t, mv=mv: e.tensor_copy(MODP[:, l, t, 5, :], mv(5)), reads=["TMP1"], writes=["MODP"])
    dump("modp", MODP[:], ["MODP"])

    def rms_norm(dst, dst_name, a_ap, b_ap):
        sq = BIG[:, 14:22, :]
        for n in range(NBLK):
            for k in R8:
                P.op("act", lambda e, k=k, n=n: e.activation(sq[:, k, BLK[n]], X[:, k, BLK[n]], AF.Square),
                     reads=xr("X", [k], [n]), writes=xr("SQ", [k], [n]))
            pi = next_ps()
            for k in R8:
                P.op("pe", lambda e, pi=pi, k=k, n=n: e.matmul(PSB[pi][:], ones[:], sq[:, k, BLK[n]], start=(k == 0), stop=(k == 7)),
                     reads=["ones"] + xr("SQ", [k], [n]), writes=[f"ps{pi}"])
            P.op("act", lambda e, pi=pi: e.activation(RSTD[:], PSB[pi][:], AF.Sqrt, bias=cst[:, 0:1], scale=1.0 / D),
                 reads=[f"ps{pi}", "cst"], writes=["RSTD"])
            P.op("dve", lambda e: e.reciprocal(RSTD[:], RSTD[:]), reads=["RSTD"], writes=["RSTD"])
            for k in R8:
                ti = next_tmp()
                P.op("dve", lambda e, ti=ti, k=k, n=n: e.scalar_tensor_tensor(
                    TMP[ti][:], X[:, k, BLK[n]], a_ap[:, k:k + 1], RSTD[:], ALU.mult, ALU.mult),
                    reads=xr("X", [k], [n]) + ["RSTD", "MODP", "nf"], writes=[f"TMP{ti}"])
                if b_ap is not None:
                    P.op("act", lambda e, ti=ti, k=k, n=n: e.activation(dst[:, k, BLK[n]], TMP[ti][:], AF.Identity,
                                                                     bias=b_ap[:, k:k + 1], scale=1.0),
                         reads=[f"TMP{ti}", "MODP"], writes=xr(dst_name, [k], [n]))
                else:
                    P.op("act", lambda e, ti=ti, k=k, n=n: e.copy(dst[:, k, BLK[n]], TMP[ti][:]),
                         reads=[f"TMP{ti}"], writes=xr(dst_name, [k], [n]))

    def proj_fm(wsrc, col0, ncols, kc, rhs, rhs_name, epi, cols_per_load=512):
        for c0 in range(0, ncols, cols_per_load):
            nl = min(cols_per_load, ncols - c0)
            s = load_w(lambda w, c0=c0, nl=nl: [(w[:, 0:kc * nl].rearrange("p (k n) -> p k n", k=kc),
                                                 wsrc[:, col0 + c0:col0 + c0 + nl].rearrange("(k p) n -> p k n", p=128))])
            wv = wview(s, kc, nl)
            for m in range(nl // 128):
                for n in range(NBLK):
                    pi = next_ps()
                    for k in range(kc):
                        P.op("pe", lambda e, pi=pi, wv=wv, m=m, k=k, n=n: e.matmul(
                            PSB[pi][:], wv[:, k, m * 128:(m + 1) * 128], rhs[:, k, BLK[n]],
                            start=(k == 0), stop=(k == kc - 1)),
                            reads=[f"WS{s}"] + xr(rhs_name, [k], [n]), writes=[f"ps{pi}"])
                    epi((c0 // 128) + m, n, pi)

    MERGED = BIG[:, 0:8, :]
    SCR = BIG

    for tile in range(ntiles):
        P.dma([(X[:, k, :], xT[tile][:, k, :]) for k in R8], "x_in", writes=xr("X", R8, range(NBLK)))
        for l in range(nlayers):
            mp = lambda i, l=l, tile=tile: MODP[:, l, tile, i, :]
            wl = w_in[l]
            rms_norm(H, "H", mp(0), mp(1))
            if tile == 0 and l == 0:
                dump("h1", H[:].bitcast(F32), xr("H", R8, range(NBLK)))

            first = [True]

            def merge_branch(br_cols_fn, gate_idx, l=l, wl=wl):
                raise NotImplementedError

            def gate_merge(m, n, br_ap, br_reads, gate_pi, is_first):
                t1 = next_tmp()
                P.op("act", lambda e, t1=t1, gate_pi=gate_pi: e.activation(TMP[t1][:], PSB[gate_pi][:], AF.Sigmoid),
                     reads=[f"ps{gate_pi}"], writes=[f"TMP{t1}"])
                if is_first:
                    P.op("dve", lambda e, t1=t1, m=m, n=n: e.tensor_tensor(MERGED[:, m, BLK[n]], br_ap, TMP[t1][:], ALU.mult),
                         reads=br_reads + [f"TMP{t1}"], writes=xr("MG", [m], [n]))
                else:
                    t2 = next_tmp()
                    P.op("dve", lambda e, t1=t1, t2=t2: e.tensor_tensor(TMP[t2][:], br_ap, TMP[t1][:], ALU.mult),
                         reads=br_reads + [f"TMP{t1}"], writes=[f"TMP{t2}"])
                    P.op("dve", lambda e, t2=t2, m=m, n=n: e.tensor_tensor(
                        MERGED[:, m, BLK[n]], MERGED[:, m, BLK[n]].bitcast(F32), TMP[t2][:], ALU.add),
                        reads=[f"TMP{t2}"] + xr("MG", [m], [n]), writes=xr("MG", [m], [n]))

            def branch_out(wsrc, kc_b, src, src_name, gate_idx, is_first, glu):
                for m in range(8):
                    gcol = 4096 + gate_idx * 1024 + m * 128
                    ncb = 256 if glu else 128

                    def pairs(w, m=m, gcol=gcol):
                        pr = []
                        if glu:
                            pr.append((w[:, 0:kc_b * 128].rearrange("p (k n) -> p k n", k=kc_b),
                                       wsrc[:, m * 128:(m + 1) * 128].rearrange("(k p) n -> p k n", p=128)))
                            pr.append((w[:, kc_b * 128:kc_b * 256].rearrange("p (k n) -> p k n", k=kc_b),
                                       wsrc[:, 1024 + m * 128:1024 + (m + 1) * 128].rearrange("(k p) n -> p k n", p=128)))
                        else:
                            pr.append((w[:, 0:kc_b * 128].rearrange("p (k n) -> p k n", k=kc_b),
                                       wsrc[:, m * 128:(m + 1) * 128].rearrange("(k p) n -> p k n", p=128)))
                        pr.append((w[:, 1024:1024 + 1024].rearrange("p (k n) -> p k n", k=8),
                                   wl[:, gcol:gcol + 128].rearrange("(k p) n -> p k n", p=128)))
                        return pr
                    s = load_w(pairs, half=True)
                    wa = wview(s, kc_b, 128, 0)
                    wb = wview(s, kc_b, 128, kc_b * 128)
                    wg = wview(s, 8, 128, 1024)
                    for n in range(NBLK):
                        pa = next_ps()
                        for k in range(kc_b):
                            P.op("pe", lambda e, pa=pa, wa=wa, k=k, n=n: e.matmul(PSB[pa][:], wa[:, k, :], src[:, k, BLK[n]],
                                                                                  start=(k == 0), stop=(k == kc_b - 1)),
                                 reads=[f"WS{s}"] + xr(src_name, [k], [n]), writes=[f"ps{pa}"])
                        if glu:
                            pb = next_ps()
                            for k in range(kc_b):
                                P.op("pe", lambda e, pb=pb, wb=wb, k=k, n=n: e.matmul(PSB[pb][:], wb[:, k, :], src[:, k, BLK[n]],
                                                                                      start=(k == 0), stop=(k == kc_b - 1)),
                                     reads=[f"WS{s}"] + xr(src_name, [k], [n]), writes=[f"ps{pb}"])
                        pg = next_ps()
                        for k in range(8):
                            P.op("pe", lambda e, pg=pg, wg=wg, k=k, n=n: e.matmul(PSB[pg][:], wg[:, k, :], H[:, k, BLK[n]],
                                                                                  start=(k == 0), stop=(k == 7)),
                                 reads=[f"WS{s}"] + xr("H", [k], [n]), writes=[f"ps{pg}"])
                        if glu:
                            t0 = next_tmp()
                            P.op("act", lambda e, t0=t0, pb=pb: e.activation(TMP[t0][:], PSB[pb][:], AF.Sigmoid),
                                 reads=[f"ps{pb}"], writes=[f"TMP{t0}"])
                            t3 = next_tmp()
                            P.op("dve", lambda e, t0=t0, t3=t3, pa=pa: e.tensor_tensor(TMP[t3][:], PSB[pa][:], TMP[t0][:], ALU.mult),
                                 reads=[f"ps{pa}", f"TMP{t0}"], writes=[f"TMP{t3}"])
                            gate_merge(m, n, TMP[t3][:], [f"TMP{t3}"], pg, is_first)
                        else:
                            gate_merge(m, n, PSB[pa][:], [f"ps{pa}"], pg, is_first)

            U = SCR[:, 8:12, :]
            Y = SCR[:, 12:16, :]
            nb = 0
            if "s5" in branches or "s5stub" in branches:
                def epi_u(m, n, pi):
                    P.op("act", lambda e, m=m, n=n, pi=pi: e.copy(U[:, m, BLK[n]], PSB[pi][:]),
                         reads=[f"ps{pi}"], writes=xr("U", [m], [n]))
                proj_fm(wl, 0, 512, 8, H, "H", epi_u)
                if tile == 0 and l == 0:
                    dump("u", U.bitcast(F32), xr("U", range(4), range(NBLK)))
                if "s5" in branches:
                    latent5 = (tile == 1)
                    L5 = 1024 if latent5 else 256
                    nseq5 = T // L5
                    W = min(512, L5)
                    nm = L5 // W
                    TWO_PI5 = 2.0 * math.pi
                    MAGIC5 = 12582912.0
                    st["ps"] = 0
                    st["psn"] = 6
                    SPv = RSTD[:, :].rearrange("p (a b) -> p a b", a=16)
                    LRE, LIM, DT, TH, RM, CT, ST_, CR, CI, C512, S512, GIR, GII, T1, T2, T3 = range(16)

                    def c_(i, a=0, b=32):
                        return SPv[:, i, a:b]

                    def sp(fn):
                        P.op("dve", fn, reads=["RSTD", "H0"], writes=["RSTD"])

                    def spa(fn):
                        P.op("act", fn, reads=["RSTD", "cst"], writes=["RSTD"])

                    def range_reduce(src, dst, tmp):
                        sp(lambda e: e.tensor_scalar(c_(tmp), c_(src), 1.0 / TWO_PI5, MAGIC5, ALU.mult, ALU.add))
                        sp(lambda e: e.tensor_scalar(c_(tmp), c_(tmp), -MAGIC5, None, ALU.add))
                        sp(lambda e: e.scalar_tensor_tensor(c_(dst), c_(tmp), -TWO_PI5, c_(src), ALU.mult, ALU.add))
                        sp(lambda e: e.tensor_scalar(c_(dst), c_(dst), 3.141592, -3.141592, ALU.min, ALU.max))

                    def sincos(src_red, dsin, dcos):
                        spa(lambda e: e.activation(c_(dsin), c_(src_red), AF.Sin))
                        sp(lambda e: e.scalar_tensor_tensor(c_(src_red), c_(src_red), -1.0, c_(src_red), ALU.mult, ALU.max))
                        spa(lambda e: e.activation(c_(dcos), c_(src_red), AF.Sin, bias=cst[:, 4:5], scale=-1.0))

                    P.dma([(SPv[:, 0:3, :], s5sp_d[:, l, :, :])], "s5sp", writes=["RSTD"])
                    spa(lambda e: e.activation(c_(DT), c_(DT), AF.Exp))
                    sp(lambda e: e.tensor_tensor(c_(TH), c_(LIM), c_(DT), ALU.mult))
                    sp(lambda e: e.tensor_tensor(c_(T1), c_(LRE), c_(DT), ALU.mult))
                    spa(lambda e: e.activation(c_(RM), c_(T1), AF.Exp))
                    range_reduce(TH, T1, T2)
                    sincos(T1, ST_, CT)
                    sp(lambda e: e.tensor_tensor(c_(T1), c_(RM), c_(CT), ALU.mult))
                    sp(lambda e: e.tensor_tensor(c_(T2), c_(RM), c_(ST_), ALU.mult))
                    sp(lambda e: e.tensor_scalar(c_(T1), c_(T1), -1.0, None, ALU.add))
                    sp(lambda e: e.tensor_tensor(c_(T3), c_(LRE), c_(LRE), ALU.mult))
                    sp(lambda e: e.tensor_tensor(c_(CR), c_(LIM), c_(LIM), ALU.mult))
                    sp(lambda e: e.tensor_tensor(c_(T3), c_(T3), c_(CR), ALU.add))
                    sp(lambda e: e.reciprocal(c_(T3), c_(T3)))
                    sp(lambda e: e.tensor_tensor(c_(CR), c_(T1), c_(LRE), ALU.mult))
                    sp(lambda e: e.tensor_tensor(c_(CI), c_(T2), c_(LIM), ALU.mult))
                    sp(lambda e: e.tensor_tensor(c_(CR), c_(CR), c_(CI), ALU.add))
                    sp(lambda e: e.tensor_tensor(c_(CR), c_(CR), c_(T3), ALU.mult))
                    sp(lambda e: e.tensor_tensor(c_(CI), c_(T2), c_(LRE), ALU.mult))
                    sp(lambda e: e.tensor_tensor(c_(C512), c_(T1), c_(LIM), ALU.mult))
                    sp(lambda e: e.tensor_tensor(c_(CI), c_(CI), c_(C512), ALU.subtract))
                    sp(lambda e: e.tensor_tensor(c_(CI), c_(CI), c_(T3), ALU.mult))
                    if latent5:
                        sp(lambda e: e.tensor_scalar(c_(T3), c_(TH), 512.0, None, ALU.mult))
                        range_reduce(T3, T1, T2)
                        sincos(T1, S512, C512)
                        P.dma([(H0[:], s5h0_d[:, l, :, :])], "s5h0", writes=["H0"])
                        sp(lambda e: e.tensor_tensor(c_(T1), c_(CT), H0[:, :, 0], ALU.mult))
                        sp(lambda e: e.tensor_tensor(c_(T2), c_(ST_), H0[:, :, 1], ALU.mult))
                        sp(lambda e: e.tensor_tensor(c_(GIR), c_(T1), c_(T2), ALU.subtract))
                        sp(lambda e: e.tensor_tensor(c_(T1), c_(ST_), H0[:, :, 0], ALU.mult))
                        sp(lambda e: e.tensor_tensor(c_(T2), c_(CT), H0[:, :, 1], ALU.mult))
                        sp(lambda e: e.tensor_tensor(c_(GII), c_(T1), c_(T2), ALU.add))
                    BZw = BIG[:, 20:22, :].rearrange("p a t -> p (a t)")
                    BZ = BZw.bitcast(F32).rearrange("p (ri q c) -> p ri q c", ri=2, q=32)
                    RB = xr("B", [20, 21], range(NBLK)) if False else [f"B20_{n}" for n in range(NBLK)] + [f"B21_{n}" for n in range(NBLK)]
                    P.dma([(BZw, s5bz_d[l])], "s5bz", writes=RB, eng="pool")
                    BTw = BIG[:, 16:18, :].rearrange("p a t -> p (a t)").rearrange("p (r jq ri s) -> p r jq ri s", r=2, jq=4, ri=2)
                    RBT = [f"B16_{n}" for n in range(NBLK)] + [f"B17_{n}" for n in range(NBLK)]
                    for r in range(2):
                        crb = SPv[:, CR, 16 * r:16 * r + 16].unsqueeze(2).to_broadcast([128, 16, 32])
                        cib = SPv[:, CI, 16 * r:16 * r + 16].unsqueeze(2).to_broadcast([128, 16, 32])
                        bre = BZ[:, 0, 16 * r:16 * r + 16, :]
                        bim = BZ[:, 1, 16 * r:16 * r + 16, :]
                        tv = [TMP[i][:, :].rearrange("p (q c) -> p q c", q=16) for i in range(4)]
                        P.op("dve", lambda e: e.tensor_tensor(tv[0], bre, crb, ALU.mult), reads=RB + ["RSTD"], writes=["TMP0"])
                        P.op("dve", lambda e: e.tensor_tensor(tv[1], bim, cib, ALU.mult), reads=RB + ["RSTD"], writes=["TMP1"])
                        P.op("dve", lambda e: e.tensor_tensor(tv[0], tv[0], tv[1], ALU.subtract), reads=["TMP0", "TMP1"], writes=["TMP0"])
                        P.op("dve", lambda e: e.tensor_tensor(tv[2], bim, crb, ALU.mult), reads=RB + ["RSTD"], writes=["TMP2"])
                        P.op("dve", lambda e: e.tensor_tensor(tv[3], bre, cib, ALU.mult), reads=RB + ["RSTD"], writes=["TMP3"])
                        P.op("dve", lambda e: e.tensor_tensor(tv[2], tv[2], tv[3], ALU.add), reads=["TMP2", "TMP3"], writes=["TMP2"])
                        for ri, ti in ((0, 0), (1, 2)):
                            pt = next_ps()
                            for jq in range(4):
                                P.op("pe", lambda e, pt=pt, jq=jq, ti=ti: e.transpose(PSB[pt][:, jq * 128:(jq + 1) * 128], TMP[ti][:, jq * 128:(jq + 1) * 128], ident[:]),
                                     reads=[f"TMP{ti}", "ident"], writes=[f"ps{pt}"])
                            P.op("act", lambda e, pt=pt, r=r, ri=ri: e.copy(BTw[:, r, :, ri, :], PSB[pt][:].rearrange("p (jq s) -> p jq s", jq=4)),
                                 reads=[f"ps{pt}"], writes=RBT)
                    CZw = BIG[:, 18:20, :].rearrange("p a t -> p (a t)")
                    CZ = CZw.bitcast(F32).rearrange("p (q c) -> p q c", q=64)
                    RCZ = [f"B18_{n}" for n in range(NBLK)] + [f"B19_{n}" for n in range(NBLK)]
                    P.dma([(CZw, s5cz_d[l])], "s5cz", writes=RCZ, eng="pool")
                    P.op("dve", lambda e: e.memset(TMP[3][:], 0.0), writes=["TMP3"])
                    for i in range(2):
                        P.op("dve", lambda e, i=i: e.tensor_copy(PT[i][:], TMP[3][:]), reads=["TMP3"], writes=[f"PT{i}"])
                    P.dma([(TMP[0][:], tpos_d)], "tpos", writes=["TMP0"])
                    P.dma([(S0T[:, 0, :], ident_d)], "s0t", writes=["S0T"], eng="pool")
                    IDR = S0T[:, 0, :]
                    ZCv = [PT[jj // 2][:, (jj % 2) * 256:(jj % 2) * 256 + 256].rearrange("p (ri m) -> p ri m", ri=2) for jj in range(4)]
                    WB = {"ur": (OS[0], "OS0"), "ui": (OS[1], "OS1"),
                          "gr": (BIG[:, 20, 0:512], "B20_0"), "gi": (BIG[:, 20, 512:1024], "B20_1"),
                          "hr": (BIG[:, 21, 0:512], "B21_0"), "hi": (BIG[:, 21, 512:1024], "B21_1")}

                    def wb(k):
                        return WB[k][0][:, 0:W] if k in ("ur", "ui") else WB[k][0][:, 0:W]

                    st5i = [0]
                    s5it = [0]
                    s5pref = [False]
                    for jq in range(4):
                        accs = {("n", n): 6 + n for n in range(NBLK)}
                        started = set()
                        for jj in range(4):
                            j = 4 * jq + jj
                            psl = slice(jj * 32, (jj + 1) * 32)
                            for r in range(2):
                                rj = r * 16 + j
                                P.op("act", lambda e: e.copy(ZCv[jj][:, 0, psl], CZ[:, rj * 2, :]), reads=RCZ, writes=[f"PT{jj // 2}"])
                                P.op("act", lambda e: e.mul(ZCv[jj][:, 1, psl], CZ[:, rj * 2 + 1, :], -1.0), reads=RCZ, writes=[f"PT{jj // 2}"])
                                if jj == 3:
                                    for ri in range(2):
                                        P.op("act", lambda e, ri=ri: e.copy(BST[96:128, ri, :], BTw[96:128, r, jq, ri, :].bitcast(F32)), reads=RBT, writes=["BST"])
                                def gen_tab(rj_, c0, nm3, parts):
                                    cs_ = slice(c0, c0 + W)
                                    th_ = SPv[:, TH, rj_:rj_ + 1]
                                    n1_, n2_, n3_ = nm3
                                    if "A" in parts:
                                        P.op("dve", lambda e: e.tensor_scalar(TMP[3][:, cs_], TMP[0][:, 0:W], th_, None, ALU.mult), reads=["TMP0", "RSTD"], writes=[n3_])
                                        P.op("dve", lambda e: e.tensor_scalar(TMP[1][:, cs_], TMP[3][:, cs_], 1.0 / TWO_PI5, MAGIC5, ALU.mult, ALU.add), reads=[n3_], writes=[n1_])
                                        P.op("dve", lambda e: e.tensor_scalar(TMP[1][:, cs_], TMP[1][:, cs_], -MAGIC5, None, ALU.add), reads=[n1_], writes=[n1_])
                                        P.op("dve", lambda e: e.scalar_tensor_tensor(TMP[3][:, cs_], TMP[1][:, cs_], -TWO_PI5, TMP[3][:, cs_], ALU.mult, ALU.add), reads=[n1_, n3_], writes=[n3_])
                                        P.op("dve", lambda e: e.tensor_scalar(TMP[3][:, cs_], TMP[3][:, cs_], 3.141592, -3.141592, ALU.min, ALU.max), reads=[n3_], writes=[n3_])
                                        P.op("act", lambda e: e.activation(TMP[2][:, cs_], TMP[3][:, cs_], AF.Sin), reads=[n3_], writes=[n2_])
                                    if "B" in parts:
                                        P.op("dve", lambda e: e.scalar_tensor_tensor(TMP[3][:, cs_], TMP[3][:, cs_], -1.0, TMP[3][:, cs_], ALU.mult, ALU.max), reads=[n3_, n2_], writes=[n3_])
                                        P.op("act", lambda e: e.activation(TMP[1][:, cs_], TMP[3][:, cs_], AF.Sin, bias=cst[:, 4:5], scale=-1.0), reads=[n3_, "cst"], writes=[n1_])
                                        P.op("act", lambda e: e.mul(TMP[3][:, cs_], TMP[2][:, cs_], -1.0), reads=[n2_, n1_], writes=[n3_])

                                if latent5:
                                    hsel = 0
                                    NM3 = ("TMP1", "TMP2", "TMP3")
                                    gen_tab(rj, 0, NM3, "AB")
                                else:
                                    hsel = s5it[0] % 2
                                    NM3 = (f"TMP1h{hsel}", f"TMP2h{hsel}", f"TMP3h{hsel}")
                                    if not s5pref[0]:
                                        gen_tab(rj, hsel * 256, NM3, "AB")
                                    s5pref[0] = False
                                    s5it[0] += 1
                                    last_in_jq = (jj == 3 and r == 1)
                                    if r == 0:
                                        rj_next = 16 + j
                                    else:
                                        rj_next = j + 1
                                    NM3n = (f"TMP1h{1 - hsel}", f"TMP2h{1 - hsel}", f"TMP3h{1 - hsel}")
                                tcs = slice(hsel * W, hsel * W + W) if not latent5 else slice(0, W)
                                cosT5 = TMP[1][:, tcs]
                                sinT5 = TMP[2][:, tcs]
                                rmb = SPv[:, RM, rj:rj + 1].to_broadcast([128, W])
                                for pr in (range(2) if not latent5 else ()):
                                    tsl = slice(512 * pr, 512 * pr + 512)
                                    nblk = pr
                                    pre = next_ps()
                                    pim = next_ps()
                                    for ri, pp in ((0, pre), (1, pim)):
                                        if jj < 3:
                                            P.op("pe", lambda e, ri=ri, pp=pp: e.matmul(PSB[pp][:, 0:512], BTw[psl, r, jq, ri, :], U[psl, jq, tsl], start=True, stop=True),
                                                 reads=RBT + xr("U", [jq], [nblk]), writes=[f"ps{pp}"])
                                        else:
                                            P.op("pe", lambda e, ri=ri, pp=pp: e.matmul(PSB[pp][:, 0:512], BST[64:128, ri, :], U[64:128, jq, tsl], start=True, stop=True),
                                                 reads=["BST"] + xr("U", [jq], [nblk]), writes=[f"ps{pp}"])

                                    def v3(ap):
                                        return ap.rearrange("p (s t) -> p s t", s=2)

                                    def rv3(ap3):
                                        return ap3 if r == 0 else ap3[:, :, ::-1]
                                    bre3 = rv3(v3(PSB[pre][:, 0:512]))
                                    bim3 = rv3(v3(PSB[pim][:, 0:512]))
                                    c3 = cosT5.unsqueeze(1).to_broadcast([128, 2, 256])
                                    s3 = sinT5.unsqueeze(1).to_broadcast([128, 2, 256])
                                    ns3 = TMP[3][:, tcs].unsqueeze(1).to_broadcast([128, 2, 256])
                                    A0, A1 = OS[0][:, 0:512], OS[1][:, 0:512]
                                    B0, B1 = BIG[:, 20, 0:512], BIG[:, 20, 512:1024]
                                    gr, gi = BIG[:, 21, 0:512], BIG[:, 21, 512:1024]
                                    grf, gif = gr.bitcast(F32), gi.bitcast(F32)
                                    P.op("dve", lambda e: e.tensor_tensor(v3(A0), bre3, c3, ALU.mult), reads=[f"ps{pre}", NM3[0]], writes=["OS0"])
                                    P.op("dve", lambda e: e.tensor_tensor(v3(A1), bim3, s3, ALU.mult), reads=[f"ps{pim}", NM3[1]], writes=["OS1"])
                                    P.op("dve", lambda e: e.tensor_tensor(v3(B0), bim3, c3, ALU.mult), reads=[f"ps{pim}", NM3[0]], writes=["B20_0"])
                                    P.op("dve", lambda e: e.tensor_tensor(v3(B1), bre3, ns3, ALU.mult), reads=[f"ps{pre}", NM3[2]], writes=["B20_1"])
                                    sre = next_ps()
                                    sim = next_ps()
                                    P.op("pe", lambda e: e.matmul(PSB[sre][:, 0:512], IDR, A0, start=True, stop=False), reads=["S0T", "OS0"], writes=[f"ps{sre}"])
                                    P.op("pe", lambda e: e.matmul(PSB[sre][:, 0:512], IDR, A1, start=False, stop=True), reads=["S0T", "OS1"], writes=[f"ps{sre}"])
                                    P.op("pe", lambda e: e.matmul(PSB[sim][:, 0:512], IDR, B0, start=True, stop=False), reads=["S0T", "B20_0"], writes=[f"ps{sim}"])
                                    P.op("pe", lambda e: e.matmul(PSB[sim][:, 0:512], IDR, B1, start=False, stop=True), reads=["S0T", "B20_1"], writes=[f"ps{sim}"])
                                    if not last_in_jq:
                                        gen_tab(rj_next, (1 - hsel) * 256, NM3n, "A" if pr == 0 else "B")
                                        if pr == 1:
                                            s5pref[0] = True
                                    for sl in range(2):
                                        cs = slice(sl * 256, (sl + 1) * 256)
                                        P.op("dve", lambda e: e.tensor_tensor_scan(gr[:, cs], rmb, PSB[sre][:, cs], 0.0, ALU.mult, ALU.add), reads=[f"ps{sre}", "RSTD"], writes=["B21_0"])
                                        P.op("dve", lambda e: e.tensor_tensor_scan(gi[:, cs], rmb, PSB[sim][:, cs], 0.0, ALU.mult, ALU.add), reads=[f"ps{sim}", "RSTD"], writes=["B21_1"])
                                    P.op("dve", lambda e: e.tensor_tensor(rv3(v3(A0)), v3(grf), c3, ALU.mult), reads=["B21_0", NM3[0]], writes=["OS0"])
                                    P.op("dve", lambda e: e.tensor_tensor(rv3(v3(A1)), v3(gif), ns3, ALU.mult), reads=["B21_1", NM3[2]], writes=["OS1"])
                                    P.op("dve", lambda e: e.tensor_tensor(rv3(v3(B0)), v3(grf), s3, ALU.mult), reads=["B21_0", NM3[1]], writes=["B20_0"])
                                    P.op("dve", lambda e: e.tensor_tensor(rv3(v3(B1)), v3(gif), c3, ALU.mult), reads=["B21_1", NM3[0]], writes=["B20_1"])
                                    for sl in range(2):
                                        sq = 2 * pr + sl
                                        slot = st5i[0] % 8
                                        st5i[0] += 1
                                        lastc = sl * 256 + (255 if r == 0 else 0)
                                        P.op("dve", lambda e: e.tensor_tensor(ST5[:, slot, 0:1], A0.bitcast(F32)[:, lastc:lastc + 1], A1.bitcast(F32)[:, lastc:lastc + 1], ALU.add),
                                             reads=["OS0", "OS1"], writes=[f"ST5_{slot}"])
                                        P.op("dve", lambda e: e.tensor_tensor(ST5[:, slot, 1:2], B0.bitcast(F32)[:, lastc:lastc + 1], B1.bitcast(F32)[:, lastc:lastc + 1], ALU.add),
                                             reads=["B20_0", "B20_1"], writes=[f"ST5_{slot}"])
                                        dst = ns5_d[sq, l, r].rearrange("(j gl) p c -> j (gl p) c", gl=2)[j]
                                        P.dma([(dst, ST5[:, slot, :])], f"st5_{slot}", reads=[f"ST5_{slot}"])
                                    key = ("n", nblk)
                                    ab = accs[key]
                                    for qi, (hb, hn, ri) in enumerate(((A0, "OS0", 0), (A1, "OS1", 0), (B0, "B20_0", 1), (B1, "B20_1", 1))):
                                        is_first = (key not in started) and qi == 0
                                        is_last = (jj == 3 and r == 1 and qi == 3)
                                        P.op("pe", lambda e, ri=ri, hb=hb, is_first=is_first, is_last=is_last: e.matmul(PSB[ab][:, 0:512], ZCv[jj][:, ri, :], hb, start=is_first, stop=is_last),
                                             reads=[f"PT{jj // 2}", hn], writes=[f"ps{ab}"])
                                    started.add(key)
                                for sq in (range(nseq5) if latent5 else ()):
                                    s0 = sq * L5
                                    for m in range(nm):
                                        if m == 1:
                                            c5 = SPv[:, C512, rj:rj + 1]
                                            s5_ = SPv[:, S512, rj:rj + 1]
                                            P.op("dve", lambda e: e.tensor_scalar(TMP[3][:, 0:W], sinT5, s5_, None, ALU.mult), reads=["TMP2", "RSTD"], writes=["TMP3"])
                                            P.op("dve", lambda e: e.scalar_tensor_tensor(TMP[3][:, 0:W], cosT5, c5, TMP[3][:, 0:W], ALU.mult, ALU.subtract), reads=["TMP1", "TMP3", "RSTD"], writes=["TMP3"])
                                            P.op("dve", lambda e: e.tensor_scalar(sinT5, sinT5, c5, None, ALU.mult), reads=["TMP2", "RSTD"], writes=["TMP2"])
                                            P.op("dve", lambda e: e.scalar_tensor_tensor(sinT5, cosT5, s5_, sinT5, ALU.mult, ALU.add), reads=["TMP1", "TMP2", "RSTD"], writes=["TMP2"])
                                            P.op("dve", lambda e: e.tensor_copy(cosT5, TMP[3][:, 0:W]), reads=["TMP3"], writes=["TMP1"])
                                            P.op("act", lambda e: e.mul(TMP[3][:, 0:W], sinT5, -1.0), reads=["TMP2"], writes=["TMP3"])
                                        if r == 0:
                                            lo = s0 + m * W
                                        else:
                                            lo = s0 + L5 - (m + 1) * W
                                        tsl = slice(lo, lo + W)
                                        nblk = lo // 512
                                        pre = next_ps()
                                        pim = next_ps()
                                        for ri, pp in ((0, pre), (1, pim)):
                                            if jj < 3:
                                                P.op("pe", lambda e, ri=ri, pp=pp: e.matmul(PSB[pp][:, 0:W], BTw[psl, r, jq, ri, :], U[psl, jq, tsl], start=True, stop=True),
                                                     reads=RBT + xr("U", [jq], [nblk]), writes=[f"ps{pp}"])
                                            else:
                                                P.op("pe", lambda e, ri=ri, pp=pp: e.matmul(PSB[pp][:, 0:W], BST[64:128, ri, :], U[64:128, jq, tsl], start=True, stop=True),
                                                     reads=["BST"] + xr("U", [jq], [nblk]), writes=[f"ps{pp}"])
                                        bre_ = PSB[pre][:, 0:W]
                                        bim_ = PSB[pim][:, 0:W]
                                        if r == 1:
                                            bre_ = bre_[:, ::-1]
                                            bim_ = bim_[:, ::-1]
                                        nsinT5 = TMP[3][:, 0:W]
                                        A0, A1 = OS[0][:, 0:W], OS[1][:, 0:W]
                                        B0, B1 = BIG[:, 20, 0:W], BIG[:, 20, 512:512 + W]
                                        gr, gi = BIG[:, 21, 0:W], BIG[:, 21, 512:512 + W]
                                        grf, gif = gr.bitcast(F32), gi.bitcast(F32)
                                        P.op("dve", lambda e: e.tensor_tensor(A0, bre_, cosT5, ALU.mult), reads=[f"ps{pre}", "TMP1"], writes=["OS0"])
                                        P.op("dve", lambda e: e.tensor_tensor(A1, bim_, sinT5, ALU.mult), reads=[f"ps{pim}", "TMP2"], writes=["OS1"])
                                        P.op("dve", lambda e: e.tensor_tensor(B0, bim_, cosT5, ALU.mult), reads=[f"ps{pim}", "TMP1"], writes=["B20_0"])
                                        P.op("dve", lambda e: e.tensor_tensor(B1, bre_, nsinT5, ALU.mult), reads=[f"ps{pre}", "TMP3"], writes=["B20_1"])
                                        sre = next_ps()
                                        sim = next_ps()
                                        P.op("pe", lambda e: e.matmul(PSB[sre][:, 0:W], IDR, A0, start=True, stop=False), reads=["S0T", "OS0"], writes=[f"ps{sre}"])
                                        P.op("pe", lambda e: e.matmul(PSB[sre][:, 0:W], IDR, A1, start=False, stop=True), reads=["S0T", "OS1"], writes=[f"ps{sre}"])
                                        P.op("pe", lambda e: e.matmul(PSB[sim][:, 0:W], IDR, B0, start=True, stop=False), reads=["S0T", "B20_0"], writes=[f"ps{sim}"])
                                        P.op("pe", lambda e: e.matmul(PSB[sim][:, 0:W], IDR, B1, start=False, stop=True), reads=["S0T", "B20_1"], writes=[f"ps{sim}"])
                                        if m == 0:
                                            inir = SPv[:, GIR, rj:rj + 1] if latent5 else 0.0
                                            inii = SPv[:, GII, rj:rj + 1] if latent5 else 0.0
                                        else:
                                            inir = SPv[:, T1, 0:1]
                                            inii = SPv[:, T2, 0:1]
                                        P.op("dve", lambda e: e.tensor_tensor_scan(gr, rmb, PSB[sre][:, 0:W], inir, ALU.mult, ALU.add), reads=[f"ps{sre}", "RSTD"], writes=["B21_0"])
                                        P.op("dve", lambda e: e.tensor_tensor_scan(gi, rmb, PSB[sim][:, 0:W], inii, ALU.mult, ALU.add), reads=[f"ps{sim}", "RSTD"], writes=["B21_1"])
                                        if m < nm - 1:
                                            P.op("dve", lambda e: e.tensor_copy(SPv[:, T1, 0:1], grf[:, W - 1:W]), reads=["B21_0"], writes=["RSTD"])
                                            P.op("dve", lambda e: e.tensor_copy(SPv[:, T2, 0:1], gif[:, W - 1:W]), reads=["B21_1"], writes=["RSTD"])
                                        rv = (lambda ap: ap) if r == 0 else (lambda ap: ap[:, ::-1])
                                        P.op("dve", lambda e: e.tensor_tensor(rv(A0), grf, cosT5, ALU.mult), reads=["B21_0", "TMP1"], writes=["OS0"])
                                        P.op("dve", lambda e: e.tensor_tensor(rv(A1), gif, nsinT5, ALU.mult), reads=["B21_1", "TMP3"], writes=["OS1"])
                                        P.op("dve", lambda e: e.tensor_tensor(rv(B0), grf, sinT5, ALU.mult), reads=["B21_0", "TMP2"], writes=["B20_0"])
                                        P.op("dve", lambda e: e.tensor_tensor(rv(B1), gif, cosT5, ALU.mult), reads=["B21_1", "TMP1"], writes=["B20_1"])
                                        if (not latent5) and m == nm - 1:
                                            slot = st5i[0] % 8
                                            st5i[0] += 1
                                            lastc = (W - 1) if r == 0 else 0
                                            P.op("dve", lambda e: e.tensor_tensor(ST5[:, slot, 0:1], A0.bitcast(F32)[:, lastc:lastc + 1], A1.bitcast(F32)[:, lastc:lastc + 1], ALU.add),
                                                 reads=["OS0", "OS1"], writes=[f"ST5_{slot}"])
                                            P.op("dve", lambda e: e.tensor_tensor(ST5[:, slot, 1:2], B0.bitcast(F32)[:, lastc:lastc + 1], B1.bitcast(F32)[:, lastc:lastc + 1], ALU.add),
                                                 reads=["B20_0", "B20_1"], writes=[f"ST5_{slot}"])
                                            dst = ns5_d[sq, l, r].rearrange("(j gl) p c -> j (gl p) c", gl=2)[j]
                                            P.dma([(dst, ST5[:, slot, :])], f"st5_{slot}", reads=[f"ST5_{slot}"])
                                        key = ("n", nblk) if latent5 else ("s", sq)
                                        ab = accs[key]
                                        for qi, (hb, hn, ri) in enumerate(((A0, "OS0", 0), (A1, "OS1", 0), (B0, "B20_0", 1), (B1, "B20_1", 1))):
                                            is_first = (key not in started) and qi == 0
                                            is_last = (jj == 3 and r == 1 and qi == 3)
                                            P.op("pe", lambda e, ri=ri, hb=hb, is_first=is_first, is_last=is_last: e.matmul(PSB[ab][:, 0:W], ZCv[jj][:, ri, :], hb, start=is_first, stop=is_last),
                                                 reads=[f"PT{jj // 2}", hn], writes=[f"ps{ab}"])
                                        started.add(key)
                        dd = S5D[:, l, jq:jq + 1]
                        for n in range(NBLK):
                            t3 = TMP[3][:]
                            if True:
                                P.op("dve", lambda e: e.scalar_tensor_tensor(t3, U[:, jq, BLK[n]].bitcast(F32), dd, PSB[6 + n][:], ALU.mult, ALU.add),
                                     reads=xr("U", [jq], [n]) + [f"ps{6 + n}", "S5D"], writes=["TMP3"])
                            else:
                                for hf in range(2):
                                    sq = 2 * n + hf
                                    P.op("dve", lambda e: e.scalar_tensor_tensor(TMP[3][:, hf * 256:(hf + 1) * 256], U[:, jq, sq * 256:(sq + 1) * 256].bitcast(F32), dd,
                                                                                 PSB[4 + sq][:, 0:256], ALU.mult, ALU.add),
                                         reads=xr("U", [jq], [n]) + [f"ps{4 + sq}", "S5D"], writes=["TMP3"])
                            o0 = OS[0][:]
                            o0f = o0.bitcast(F32)
                            P.op("dve", lambda e: e.tensor_tensor(o0, t3, t3, ALU.mult), reads=["TMP3"], writes=["OS0"])
                            P.op("dve", lambda e: e.tensor_scalar(o0, o0f, 0.044715 * 1.5957691216, 1.5957691216, ALU.mult, ALU.add), reads=["OS0"], writes=["OS0"])
                            P.op("dve", lambda e: e.tensor_tensor(o0, o0f, t3, ALU.mult), reads=["OS0", "TMP3"], writes=["OS0"])
                            P.op("act", lambda e: e.activation(o0, o0f, AF.Sigmoid), reads=["OS0"], writes=["OS0"])
                            P.op("dve", lambda e: e.tensor_tensor(Y[:, jq, BLK[n]], o0f, t3, ALU.mult), reads=["OS0", "TMP3"], writes=xr("Y", [jq], [n]))
                    st["psn"] = 6
                    st["ps"] = 0
                    if tile == 0 and l == 0:
                        dump("y5", Y.bitcast(F32), xr("Y", range(4), range(NBLK)))
                else:
                    for m in range(4):
                        for n in range(NBLK):
                            t0 = next_tmp()
                            P.op("dve", lambda e, t0=t0, m=m, n=n: e.tensor_tensor(TMP[t0][:], U[:, m, BLK[n]].bitcast(F32), U[:, m, BLK[n]].bitcast(F32), ALU.mult),
                                 reads=xr("U", [m], [n]), writes=[f"TMP{t0}"])
                            P.op("dve", lambda e, t0=t0: e.tensor_scalar(TMP[t0][:], TMP[t0][:], 0.044715 * 1.5957691216, 1.5957691216, ALU.mult, ALU.add),
                                 reads=[f"TMP{t0}"], writes=[f"TMP{t0}"])
                            P.op("dve", lambda e, t0=t0, m=m, n=n: e.tensor_tensor(TMP[t0][:], TMP[t0][:], U[:, m, BLK[n]].bitcast(F32), ALU.mult),
                                 reads=[f"TMP{t0}"] + xr("U", [m], [n]), writes=[f"TMP{t0}"])
                            P.op("act", lambda e, t0=t0: e.activation(TMP[t0][:], TMP[t0][:], AF.Sigmoid), reads=[f"TMP{t0}"], writes=[f"TMP{t0}"])
                            P.op("dve", lambda e, t0=t0, m=m, n=n: e.tensor_tensor(Y[:, m, BLK[n]], TMP[t0][:], U[:, m, BLK[n]].bitcast(F32), ALU.mult),
                                 reads=[f"TMP{t0}"] + xr("U", [m], [n]), writes=xr("Y", [m], [n]))
                branch_out(w_s5_glu[l], 4, Y, "Y", 0, nb == 0, True)
                nb += 1
            if "ret" in branches:
                latent = (tile == 1)
                L = 1024 if latent else 256
                RET = SCR[:, 12:16, :]
                RQ, RK, RQ0, RV, RG, RT = 16, 17, 18, 19, 20, 21

                def rr(r, ns=range(NBLK)):
                    return [f"B{r}_{n}" for n in ns]
                Qrow = BIG[:, RQ, :]
                Krow = BIG[:, RK, :]
                Q0row = BIG[:, RQ0, :]
                Vrow = BIG[:, RV, :].rearrange("p (c e) -> p c e", c=8)
                Grow = BIG[:, RG, :].bitcast(F32)
                Trow = BIG[:, RT, :]
                RtabW = BIG[:, 8:10, :].rearrange("p a t -> p (a t)")
                Rtab = RtabW.bitcast(F32)
                GrowW = BIG[:, RG, :]
                RTR = rr(8) + rr(9)
                cosT = BIG[:, 10, :].bitcast(F32)
                sinT = BIG[:, 11, :].bitcast(F32)
                if latent:
                    P.dma([(BIG[:, 10, :], rope_d[0]), (BIG[:, 11, :], rope_d[1])], "rope", writes=rr(10) + rr(11), eng="pool")

                def pj(wv_, n):
                    pi = next_ps()
                    for k in R8:
                        P.op("pe", lambda e, pi=pi, wv_=wv_, k=k, n=n: e.matmul(PSB[pi][:], wv_[:, k, :], H[:, k, BLK[n]],
                                                                              start=(k == 0), stop=(k == 7)),
                             reads=[f"WS{s}"] + xr("H", [k], [n]), writes=[f"ps{pi}"])
                    return pi

                def rope(src_row, src_r, dst_row, dst_r, n):
                    pi = next_ps()
                    P.op("pe", lambda e, pi=pi, n=n: e.matmul(PSB[pi][:], perm[:], src_row[:, BLK[n]], start=True, stop=True),
                         reads=["perm", f"B{src_r}_{n}"], writes=[f"ps{pi}"])
                    ta = next_tmp()
                    P.op("dve", lambda e, ta=ta, n=n: e.tensor_tensor(TMP[ta][:], src_row[:, BLK[n]].bitcast(F32), cosT[:, BLK[n]], ALU.mult),
                         reads=[f"B{src_r}_{n}", f"B10_{n}"], writes=[f"TMP{ta}"])
                    tb = next_tmp()
                    P.op("dve", lambda e, tb=tb, pi=pi, n=n: e.tensor_tensor(TMP[tb][:], PSB[pi][:], sinT[:, BLK[n]], ALU.mult),
                         reads=[f"ps{pi}", f"B11_{n}"], writes=[f"TMP{tb}"])
                    P.op("dve", lambda e, ta=ta, tb=tb, n=n: e.tensor_tensor(dst_row[:, BLK[n]], TMP[ta][:], TMP[tb][:], ALU.add),
                         reads=[f"TMP{ta}", f"TMP{tb}"], writes=[f"B{dst_r}_{n}"])

                acc_i = [0]
                for h in range(4):
                    lgi = l * 8 + h
                    lgf = LG[:, lgi:lgi + 1]
                    lgb = LG[:, lgi + 4:lgi + 5]
                    nlgb = NLG[:, lgi + 4:lgi + 5]
                    s = load_w(lambda w, h=h: [(w[:, i * 1024:(i + 1) * 1024].rearrange("p (k n) -> p k n", k=8),
                                                wl[:, c0 + h * 128:c0 + (h + 1) * 128].rearrange("(k p) n -> p k n", p=128))
                                               for i, c0 in enumerate((512, 1024, 1536, 2048))])
                    wq, wk, wv, wg = [wview(s, 8, 128, i * 1024) for i in range(4)]
                    for n in range(NBLK):
                        pi = pj(wq, n)
                        if latent:
                            P.op("act", lambda e, pi=pi, n=n: e.copy(Q0row[:, BLK[n]], PSB[pi][:]), reads=[f"ps{pi}"], writes=[f"B{RQ0}_{n}"])
                            rope(Q0row, RQ0, Qrow, RQ, n)
                        else:
                            P.op("act", lambda e, pi=pi, n=n: e.copy(Qrow[:, BLK[n]], PSB[pi][:]), reads=[f"ps{pi}"], writes=[f"B{RQ}_{n}"])
                    for n in range(NBLK):
                        pi = pj(wk, n)
                        if latent:
                            P.op("act", lambda e, pi=pi, n=n: e.mul(Trow[:, BLK[n]], PSB[pi][:], 128.0 ** -0.5), reads=[f"ps{pi}"], writes=[f"B{RT}_{n}"])
                            rope(Trow, RT, Krow, RK, n)
                        else:
                            P.op("act", lambda e, pi=pi, n=n: e.mul(Krow[:, BLK[n]], PSB[pi][:], 128.0 ** -0.5), reads=[f"ps{pi}"], writes=[f"B{RK}_{n}"])
                    for n in range(NBLK):
                        pi = pj(wv, n)
                        P.op("act", lambda e, pi=pi, n=n: e.copy(Trow[:, BLK[n]], PSB[pi][:]), reads=[f"ps{pi}"], writes=[f"B{RT}_{n}"])
                        pt = next_ps()
                        for c in range(4):
                            P.op("pe", lambda e, pt=pt, c=c, n=n: e.transpose(PSB[pt][:, c * 128:(c + 1) * 128],
                                                                             Trow[:, n * 512 + c * 128:n * 512 + (c + 1) * 128].bitcast(F32), ident[:]),
                                 reads=[f"B{RT}_{n}", "ident"], writes=[f"ps{pt}"])
                        P.op("act", lambda e, pt=pt, n=n: e.copy(BIG[:, RV, BLK[n]], PSB[pt][:]), reads=[f"ps{pt}"], writes=[f"B{RV}_{n}"])
                    for n in range(NBLK):
                        pi = pj(wg, n)
                        P.op("act", lambda e, pi=pi, n=n: e.activation(GrowW[:, BLK[n]], PSB[pi][:], AF.Silu), reads=[f"ps{pi}"], writes=[f"B{RG}_{n}"])
                    P.dma([(BIG[:, 8:10, :].rearrange("p a t -> p (a t)")[:, 0:2 * L], ramp_d[1 if latent else 0])], "ramp", writes=RTR, eng="pool")
                    for pc in range(2 * L // 512):
                        ta = next_tmp()
                        sl = slice(pc * 512, (pc + 1) * 512)
                        P.op("act", lambda e, ta=ta, sl=sl, nlgb=nlgb: e.activation(TMP[ta][:], Rtab[:, sl], AF.Exp, scale=nlgb),
                             reads=RTR + ["NLG"], writes=[f"TMP{ta}"])
                        P.op("act", lambda e, sl=sl, lgf=lgf: e.activation(RtabW[:, sl], Rtab[:, sl], AF.Exp, scale=lgf),
                             reads=RTR + ["LG"], writes=RTR)
                        P.op("dve", lambda e, ta=ta, sl=sl: e.tensor_tensor(RtabW[:, sl], Rtab[:, sl], TMP[ta][:], ALU.min),
                             reads=RTR + [f"TMP{ta}"], writes=RTR)
                    if latent:
                        P.dma([(S0T[:, d, :], sret_d[l, d, h]) for d in range(2)], "s0t", writes=["S0T"], eng="pool")
                    if not latent:
                        P.op("act", lambda e, lgf=lgf: e.activation(KD[:, 0, :], posT[:, 0, :], AF.Exp, scale=lgf), reads=["posT", "LG"], writes=["KD"])
                        P.op("act", lambda e, lgb=lgb: e.activation(KD[:, 1, :], posT[:, 1, :], AF.Exp, scale=lgb), reads=["posT", "LG"], writes=["KD"])
                        def st_stages(sq, b):
                            n = sq // 2
                            stt = {}
                            KTSb = PT[b][:, :].rearrange("p (d c e) -> p d c e", d=2, c=2)

                            def s0():
                                stt["pt"] = next_ps()
                                for cl in range(2):
                                    c = sq * 2 + cl
                                    P.op("pe", lambda e: e.transpose(PSB[stt["pt"]][:, cl * 128:(cl + 1) * 128], Krow[:, c * 128:(c + 1) * 128].bitcast(F32), ident[:]),
                                         reads=[f"B{RK}_{n}", "ident"], writes=[f"ps{stt['pt']}"])

                            def s1():
                                for d in range(2):
                                    for cl in range(2):
                                        P.op("dve", lambda e: e.tensor_scalar(KTSb[:, d, cl, :], PSB[stt["pt"]][:, cl * 128:(cl + 1) * 128], KD[:, d, cl:cl + 1], None, ALU.mult),
                                             reads=[f"ps{stt['pt']}", "KD"], writes=[f"PT{b}"])

                            def s2():
                                for d in range(2):
                                    stt[("ps", d)] = next_ps()
                                    for cl in range(2):
                                        c = sq * 2 + cl
                                        P.op("pe", lambda e: e.matmul(PSB[stt[("ps", d)]][:, 0:128], KTSb[:, d, cl, :], Vrow[:, c, :], start=(cl == 0), stop=(cl == 1)),
                                             reads=[f"PT{b}", f"B{RV}_{n}"], writes=[f"ps{stt[('ps', d)]}"])

                            def s3():
                                for d in range(2):
                                    so = 2 * b + d
                                    P.op("act", lambda e: e.copy(TMP[so][:, 0:128], PSB[stt[("ps", d)]][:, 0:128]), reads=[f"ps{stt[('ps', d)]}"], writes=[f"TMP{so}"])
                                    P.dma([(nsret_d[sq, l, d, h], TMP[so][:, 0:128])], f"sto{so}", reads=[f"TMP{so}"])
                            return [s0, s1, s2, s3]

                        for pr_ in range(2):
                            for fa, fb in zip(st_stages(2 * pr_, 0), st_stages(2 * pr_ + 1, 1)):
                                fa()
                                fb()
                    if latent:
                        groups = [(0, 512, list(range(8)), 0), (512, 512, list(range(8)), 0)]
                    else:
                        groups = []

                        def grp_stages(sq, b):
                            i0 = sq * 256
                            n = i0 // 512
                            jcs = [2 * sq, 2 * sq + 1]
                            acc = 6 + b
                            ta, tb = 2 * b, 2 * b + 1
                            stt = {}
                            TA = TMP[ta][:, 0:256]
                            TB = TMP[tb][:, 0:256]
                            O1 = OS[b][:, 0:256]
                            O2 = OS[b][:, 256:512]

                            def s0():
                                stt["ps"] = next_ps()
                                for ji, jc in enumerate(jcs):
                                    P.op("pe", lambda e: e.matmul(PSB[stt["ps"]][:, ji * 256:(ji + 1) * 256], Krow[:, jc * 128:(jc + 1) * 128], Qrow[:, i0:i0 + 256],
                                                                  start=True, stop=True), reads=[f"B{RK}_{n}", f"B{RQ}_{n}"], writes=[f"ps{stt['ps']}"])

                            def s1():
                                for ji, jc in enumerate(jcs):
                                    off = i0 - jc * 128 + L
                                    P.op("dve", lambda e: e.tensor_tensor(PT[b][:, ji * 256:(ji + 1) * 256], PSB[stt["ps"]][:, ji * 256:(ji + 1) * 256], Rtab[:, off:off + 256], ALU.mult),
                                         reads=[f"ps{stt['ps']}"] + RTR, writes=[f"PT{b}"])

                            def s2():
                                for ji, jc in enumerate(jcs):
                                    P.op("pe", lambda e: e.matmul(PSB[acc][:, 0:256], Vrow[:, jc, :], PT[b][:, ji * 256:(ji + 1) * 256], start=(ji == 0), stop=(ji == 1)),
                                         reads=[f"B{RV}_{n}", f"PT{b}"], writes=[f"ps{acc}"])

                            def s3():
                                P.op("act", lambda e: e.copy(O1, PSB[acc][:, 0:256]), reads=[f"ps{acc}"], writes=[f"OS{b}"])
                                P.op("act", lambda e: e.activation(O2, PSB[acc][:, 0:256], AF.Square), reads=[f"ps{acc}"], writes=[f"OS{b}"])

                            def s4():
                                stt["pm"] = next_ps()
                                P.op("pe", lambda e: e.matmul(PSB[stt["pm"]][:, 0:256], ones[:], O1, start=True, stop=True), reads=["ones", f"OS{b}"], writes=[f"ps{stt['pm']}"])
                                P.op("pe", lambda e: e.matmul(PSB[stt["pm"]][:, 256:512], ones[:], O2, start=True, stop=True), reads=["ones", f"OS{b}"], writes=[f"ps{stt['pm']}"])

                            def s5():
                                P.op("dve", lambda e: e.tensor_scalar(TA, PSB[stt["pm"]][:, 0:256], 1.0 / 128, None, ALU.mult), reads=[f"ps{stt['pm']}"], writes=[f"TMP{ta}"])

                            def s6():
                                P.op("dve", lambda e: e.tensor_tensor(TB, TA, TA, ALU.mult), reads=[f"TMP{ta}"], writes=[f"TMP{tb}"])

                            def s7():
                                P.op("dve", lambda e: e.scalar_tensor_tensor(TB, PSB[stt["pm"]][:, 256:512], 1.0 / 128, TB, ALU.mult, ALU.subtract),
                                     reads=[f"ps{stt['pm']}", f"TMP{tb}"], writes=[f"TMP{tb}"])

                            def s8():
                                P.op("act", lambda e: e.activation(TB, TB, AF.Sqrt, bias=cst[:, 1:2], scale=1.0), reads=[f"TMP{tb}", "cst"], writes=[f"TMP{tb}"])

                            def s9():
                                P.op("dve", lambda e: e.reciprocal(TB, TB), reads=[f"TMP{tb}"], writes=[f"TMP{tb}"])

                            def s10():
                                P.op("dve", lambda e: e.tensor_tensor(TA, O1.bitcast(F32), TA, ALU.subtract), reads=[f"OS{b}", f"TMP{ta}"], writes=[f"TMP{ta}"])

                            def s11():
                                P.op("dve", lambda e: e.tensor_tensor(TA, TA, TB, ALU.mult), reads=[f"TMP{ta}", f"TMP{tb}"], writes=[f"TMP{ta}"])

                            def s12():
                                P.op("dve", lambda e: e.tensor_tensor(RET[:, h, i0:i0 + 256], TA, Grow[:, i0:i0 + 256], ALU.mult),
                                     reads=[f"TMP{ta}", f"B{RG}_{n}"], writes=xr("Y", [h], [n]))
                            return [s0, s1, s2, s3, s4, s5, s6, s7, s8, s9, s10, s11, s12]

                        for pr_ in range(2):
                            sa = grp_stages(2 * pr_, 0)
                            sb_ = grp_stages(2 * pr_ + 1, 1)
                            for fa, fb in zip(sa, sb_):
                                fa()
                                fb()
                    for (i0, N, jcs, seq0) in groups:
                        n = i0 // 512
                        acc = 6 + (acc_i[0] % 2)
                        acc_i[0] += 1
                        for ji, jc in enumerate(jcs):
                            nj = (jc * 128) // 512
                            pi = next_ps()
                            P.op("pe", lambda e, pi=pi, jc=jc, i0=i0, N=N: e.matmul(PSB[pi][:, 0:N], Krow[:, jc * 128:(jc + 1) * 128], Qrow[:, i0:i0 + N],
                                                                                   start=True, stop=True),
                                 reads=[f"B{RK}_{nj}", f"B{RQ}_{n}"], writes=[f"ps{pi}"])
                            off = (i0 - seq0) - (jc * 128 - seq0) + L
                            pt_i = (acc_i[0] + ji) % 2
                            P.op("dve", lambda e, pi=pi, pt_i=pt_i, off=off, N=N: e.tensor_tensor(PT[pt_i][:, 0:N], PSB[pi][:, 0:N], Rtab[:, off:off + N], ALU.mult),
                                 reads=[f"ps{pi}"] + RTR, writes=[f"PT{pt_i}"])
                            last = (ji == len(jcs) - 1) and not latent
                            P.op("pe", lambda e, acc=acc, pt_i=pt_i, jc=jc, N=N, ji=ji, last=last: e.matmul(PSB[acc][:, 0:N], Vrow[:, jc, :], PT[pt_i][:, 0:N],
                                                                                                         start=(ji == 0), stop=last),
                                 reads=[f"B{RV}_{nj}", f"PT{pt_i}"], writes=[f"ps{acc}"])
                        if latent:
                            for d in range(2):
                                pb = next_tmp()
                                P.dma([(TMP[pb][:], pos_d[d][:, i0:i0 + N])], f"posb{pb}", writes=[f"TMP{pb}"])
                                lgd = lgf if d == 0 else lgb
                                P.op("act", lambda e, lgd=lgd, pb=pb: e.activation(TMP[pb][:], TMP[pb][:], AF.Exp, scale=lgd), reads=[f"TMP{pb}", "LG"], writes=[f"TMP{pb}"])
                                pt_i = d
                                P.op("dve", lambda e, pt_i=pt_i, i0=i0, N=N, pb=pb: e.tensor_tensor(PT[pt_i][:, 0:N], Q0row[:, i0:i0 + N].bitcast(F32), TMP[pb][:, 0:N], ALU.mult),
                                     reads=[f"B{RQ0}_{n}", f"TMP{pb}"], writes=[f"PT{pt_i}"])
                                P.op("pe", lambda e, acc=acc, pt_i=pt_i, d=d, N=N: e.matmul(PSB[acc][:, 0:N], S0T[:, d, :], PT[pt_i][:, 0:N],
                                                                                        start=False, stop=(d == 1)),
                                     reads=["S0T", f"PT{pt_i}"], writes=[f"ps{acc}"])
                        P.op("act", lambda e, acc=acc, N=N: e.copy(OS[0][:, 0:N], PSB[acc][:, 0:N]), reads=[f"ps{acc}"], writes=["OS0"])
                        P.op("act", lambda e, acc=acc, N=N: e.activation(OS[1][:, 0:N], PSB[acc][:, 0:N], AF.Square), reads=[f"ps{acc}"], writes=["OS1"])
                        p1 = next_ps()
                        P.op("pe", lambda e, p1=p1, N=N: e.matmul(PSB[p1][:, 0:N], ones[:], OS[0][:, 0:N], start=True, stop=True),
                             reads=["ones", "OS0"], writes=[f"ps{p1}"])
                        p2 = next_ps()
                        P.op("pe", lambda e, p2=p2, N=N: e.matmul(PSB[p2][:, 0:N], ones[:], OS[1][:, 0:N], start=True, stop=True),
                             reads=["ones", "OS1"], writes=[f"ps{p2}"])
                        ta = next_tmp()
                        tb = next_tmp()
                        P.op("dve", lambda e, ta=ta, p1=p1, N=N: e.tensor_scalar(TMP[ta][:, 0:N], PSB[p1][:, 0:N], 1.0 / 128, None, ALU.mult),
                             reads=[f"ps{p1}"], writes=[f"TMP{ta}"])
                        P.op("dve", lambda e, ta=ta, tb=tb, N=N: e.tensor_tensor(TMP[tb][:, 0:N], TMP[ta][:, 0:N], TMP[ta][:, 0:N], ALU.mult),
                             reads=[f"TMP{ta}"], writes=[f"TMP{tb}"])
                        P.op("dve", lambda e, tb=tb, p2=p2, N=N: e.scalar_tensor_tensor(TMP[tb][:, 0:N], PSB[p2][:, 0:N], 1.0 / 128, TMP[tb][:, 0:N], ALU.mult, ALU.subtract),
                             reads=[f"ps{p2}", f"TMP{tb}"], writes=[f"TMP{tb}"])
                        P.op("act", lambda e, tb=tb, N=N: e.activation(TMP[tb][:, 0:N], TMP[tb][:, 0:N], AF.Sqrt, bias=cst[:, 1:2], scale=1.0),
                             reads=[f"TMP{tb}", "cst"], writes=[f"TMP{tb}"])
                        P.op("dve", lambda e, tb=tb, N=N: e.reciprocal(TMP[tb][:, 0:N], TMP[tb][:, 0:N]), reads=[f"TMP{tb}"], writes=[f"TMP{tb}"])
                        P.op("dve", lambda e, ta=ta, N=N: e.tensor_tensor(TMP[ta][:, 0:N], OS[0][:, 0:N].bitcast(F32), TMP[ta][:, 0:N], ALU.subtract),
                             reads=["OS0", f"TMP{ta}"], writes=[f"TMP{ta}"])
                        P.op("dve", lambda e, ta=ta, tb=tb, N=N: e.tensor_tensor(TMP[ta][:, 0:N], TMP[ta][:, 0:N], TMP[tb][:, 0:N], ALU.mult),
                             reads=[f"TMP{ta}", f"TMP{tb}"], writes=[f"TMP{ta}"])
                        P.op("dve", lambda e, ta=ta, h=h, i0=i0, N=N: e.tensor_tensor(RET[:, h, i0:i0 + N], TMP[ta][:, 0:N], Grow[:, i0:i0 + N], ALU.mult),
                             reads=[f"TMP{ta}", f"B{RG}_{n}"], writes=xr("Y", [h], [n]))
                if tile == 0 and l == 0:
                    dump("ret", RET.bitcast(F32), xr("Y", range(4), range(NBLK)))
                branch_out(w_ret_o[l], 4, RET, "Y", 1, nb == 0, False)
                nb += 1
            if "hy" in branches:
                latent = (tile == 1)
                L = 1024 if latent else 256
                li = 1 if latent else 0
                nseq = T // L
                nfc = L // 128
                NB = 256
                YOUT = SCR[:, 12:16, :]
                TWO_PI = 2.0 * math.pi
                MAGIC = 12582912.0

                def rr(r, ns=range(NBLK)):
                    return [f"B{r}_{n}" for n in ns]

                def rowpair_tm(r0):
                    return BIG[:, r0:r0 + 2, :].rearrange("p a t -> p (a t)").rearrange("p (c e) -> p c e", c=8)

                nzb = (L + 511) // 512
                P.dma([(PT[zb][0:33, 0:min(512, L)], zT_d[li][:, zb * 512:zb * 512 + min(512, L)]) for zb in range(nzb)],
                      "zt", writes=[f"PT{zb}" for zb in range(nzb)], eng="pool")
                shw = load_w(lambda w: [(w[0:33, 0:64], hw1_d[:, l, :]), (w[0:64, 64:128], hw2_d[:, l, :])], half=True)
                for layer_i in range(2):
                    src = PT if layer_i == 0 else OS
                    dst = OS if layer_i == 0 else PT
                    sn = "PT" if layer_i == 0 else "OS"
                    dn = "OS" if layer_i == 0 else "PT"
                    kk = 33 if layer_i == 0 else 64
                    wmat = WS[shw][0:33, 0:64] if layer_i == 0 else WS[shw][0:64, 64:128]
                    fq = HYS[:, l, 2 + layer_i:3 + layer_i]
                    fb = HYF[:, l, layer_i:layer_i + 1]
                    for zb in range(nzb):
                        wdt = min(512, L)
                        pi = next_ps()
                        P.op("pe", lambda e, pi=pi, wmat=wmat, src=src, zb=zb, kk=kk, wdt=wdt: e.matmul(
                            PSB[pi][0:64, 0:wdt], wmat, src[zb][0:kk, 0:wdt], start=True, stop=True),
                            reads=[f"WS{shw}", f"{sn}{zb}"], writes=[f"ps{pi}"])
                        P.op("dve", lambda e, pi=pi, fq=fq, fb=fb, wdt=wdt: e.tensor_scalar(TMP[0][0:64, 0:wdt], PSB[pi][0:64, 0:wdt], fq, fb, ALU.mult, ALU.add),
                             reads=[f"ps{pi}", "HYS", "HYF"], writes=["TMP0"])
                        P.op("dve", lambda e, wdt=wdt: e.tensor_scalar(TMP[1][0:64, 0:wdt], TMP[0][0:64, 0:wdt], 1.0 / TWO_PI, MAGIC, ALU.mult, ALU.add),
                             reads=["TMP0"], writes=["TMP1"])
                        P.op("dve", lambda e, wdt=wdt: e.tensor_scalar(TMP[1][0:64, 0:wdt], TMP[1][0:64, 0:wdt], -MAGIC, None, ALU.add),
                             reads=["TMP1"], writes=["TMP1"])
                        P.op("dve", lambda e, wdt=wdt: e.scalar_tensor_tensor(TMP[0][0:64, 0:wdt], TMP[1][0:64, 0:wdt], -TWO_PI, TMP[0][0:64, 0:wdt], ALU.mult, ALU.add),
                             reads=["TMP0", "TMP1"], writes=["TMP0"])
                        P.op("dve", lambda e, wdt=wdt: e.tensor_scalar(TMP[0][0:64, 0:wdt], TMP[0][0:64, 0:wdt], 3.141592, -3.141592, ALU.min, ALU.max),
                             reads=["TMP0"], writes=["TMP0"])
                        P.op("act", lambda e, dst=dst, zb=zb, wdt=wdt: e.activation(dst[zb][0:64, 0:wdt], TMP[0][0:64, 0:wdt], AF.Sin),
                             reads=["TMP0"], writes=[f"{dn}{zb}"])
                HID = PT

                def hy_proj_conv(col0, jch0, dst_rows, raw_row):
                    s_ = load_w(lambda w: [(w[:, 0:2048].rearrange("p (k n) -> p k n", k=8),
                                            wl[:, col0:col0 + 256].rearrange("(k p) n -> p k n", p=128))], half=True)
                    wv_ = wview(s_, 8, 256)
                    raw = BIG[:, raw_row, :]
                    rawf = raw.bitcast(F32)
                    for cc in range(2):
                        j = jch0 + cc
                        zrow = BIG[:, dst_rows + cc, :]
                        zrowf = zrow.bitcast(F32)
                        for n in range(NBLK):
                            pi = next_ps()
                            for k in R8:
                                P.op("pe", lambda e, pi=pi, wv_=wv_, k=k, n=n, cc=cc: e.matmul(PSB[pi][:], wv_[:, k, cc * 128:(cc + 1) * 128], H[:, k, BLK[n]],
                                                                                           start=(k == 0), stop=(k == 7)),
                                     reads=[f"WS{s_}"] + xr("H", [k], [n]), writes=[f"ps{pi}"])
                            P.op("act", lambda e, pi=pi, n=n, raw=raw: e.copy(raw[:, BLK[n]], PSB[pi][:]), reads=[f"ps{pi}"], writes=[f"B{raw_row}_{n}"])
                        w0 = HYC[:, l, 0, j:j + 1]
                        w1 = HYC[:, l, 1, j:j + 1]
                        w2 = HYC[:, l, 2, j:j + 1]
                        bb = HYC[:, l, 3, j:j + 1]
                        P.op("act", lambda e, zrow=zrow, rawf=rawf, w1=w1, bb=bb: e.activation(zrow, rawf, AF.Identity, bias=bb, scale=w1),
                             reads=rr(raw_row) + ["HYC"], writes=rr(dst_rows + cc))
                        for sq in range(nseq):
                            a0 = sq * L
                            P.op("dve", lambda e, zrow=zrow, zrowf=zrowf, rawf=rawf, w0=w0, a0=a0: e.scalar_tensor_tensor(
                                zrow[:, a0 + 1:a0 + L], rawf[:, a0:a0 + L - 1], w0, zrowf[:, a0 + 1:a0 + L], ALU.mult, ALU.add),
                                reads=rr(raw_row) + rr(dst_rows + cc) + ["HYC"], writes=rr(dst_rows + cc))
                            P.op("dve", lambda e, zrow=zrow, zrowf=zrowf, rawf=rawf, w2=w2, a0=a0: e.scalar_tensor_tensor(
                                zrow[:, a0:a0 + L - 1], rawf[:, a0 + 1:a0 + L], w2, zrowf[:, a0:a0 + L - 1], ALU.mult, ALU.add),
                                reads=rr(raw_row) + rr(dst_rows + cc) + ["HYC"], writes=rr(dst_rows + cc))

                def to_tm(src_rows, tm_r0):
                    tmv = rowpair_tm(tm_r0)
                    for cc in range(2):
                        srcf = BIG[:, src_rows + cc, :].bitcast(F32)
                        for n in range(NBLK):
                            pt = next_ps()
                            for c in range(4):
                                P.op("pe", lambda e, pt=pt, c=c, n=n, srcf=srcf: e.transpose(PSB[pt][:, c * 128:(c + 1) * 128],
                                                                                          srcf[:, n * 512 + c * 128:n * 512 + (c + 1) * 128], ident[:]),
                                     reads=[f"B{src_rows + cc}_{n}", "ident"], writes=[f"ps{pt}"])
                            P.op("act", lambda e, pt=pt, n=n, cc=cc, tmv=tmv: e.copy(tmv[:, n * 4:(n + 1) * 4, cc * 128:(cc + 1) * 128],
                                                                                   PSB[pt][:].rearrange("p (c e) -> p c e", c=4)),
                                 reads=[f"ps{pt}"], writes=rr(tm_r0) + rr(tm_r0 + 1))

                for cb in range(2):
                    pairs3 = [(16, 18, 20), (20, 16, 18)]
                    hy_proj_conv(2560 + 1024 + cb * 256, 8 + cb * 2, 18, 20)
                    to_tm(18, 16)
                    for o in range(2):
                        tm_in, r_sum, r_diff = pairs3[o]
                        tm_out = 20
                        tmv = rowpair_tm(tm_in)
                        tsum = rowpair_tm(r_sum)
                        tdiff = rowpair_tm(r_diff)
                        TM_IN = rr(tm_in) + rr(tm_in + 1)
                        TSUM = rr(r_sum) + rr(r_sum + 1)
                        TDIFF = rr(r_diff) + rr(r_diff + 1)
                        YHre = rowpair_tm(8)
                        YHim = rowpair_tm(10)
                        YRE = rr(8) + rr(9)
                        YIM = rr(10) + rr(11)
                        s3 = load_w(lambda w, o=o, cb=cb: [(w[0:64, 0:256], hy_w3[l][:, o * 512 + cb * 256:o * 512 + (cb + 1) * 256]),
                                                            (w[0:64, 256:512], hy_w3[l][:, 1024 + o * 512 + cb * 256:1024 + o * 512 + (cb + 1) * 256])], half=True)
                        P.dma([(TMP[3][:, 0:256], rate_d[:, cb * 256:(cb + 1) * 256])], "rate", writes=["TMP3"])
                        NRM = RSTD[:, 0:256]
                        def tap_stages(tc, b):
                            cs_ = slice(b * 256, (b + 1) * 256)
                            stt = {}
                            F_, G_, Wn_ = TMP[0][:, cs_], TMP[1][:, cs_], TMP[2][:, cs_]
                            nF, nG, nW = f"TMP0h{b}", f"TMP1h{b}", f"TMP2h{b}"

                            def s0():
                                stt["pi"] = next_ps()
                                P.op("pe", lambda e: e.matmul(PSB[stt["pi"]][:], HID[tc // 4][0:64, (tc % 4) * 128:(tc % 4 + 1) * 128], WS[s3][0:64, 0:512], start=True, stop=True),
                                     reads=[f"PT{tc // 4}", f"WS{s3}"], writes=[f"ps{stt['pi']}"])
                                P.op("act", lambda e: e.activation(Wn_, TMP[3][:, 0:256], AF.Exp, scale=NTN[:, li, tc:tc + 1]), reads=["TMP3", "NTN"], writes=[nW])

                            def s1():
                                P.op("dve", lambda e: e.tensor_tensor(F_, PSB[stt["pi"]][:, 0:256], Wn_, ALU.mult), reads=[f"ps{stt['pi']}", nW], writes=[nF])
                                P.op("dve", lambda e: e.tensor_tensor(G_, PSB[stt["pi"]][:, 256:512], Wn_, ALU.mult), reads=[f"ps{stt['pi']}", nW], writes=[nG])
                                if tc == 0:
                                    P.op("dve", lambda e: e.memset(TMP[1][0:1, cs_], 0.0), reads=[nG], writes=[nG])

                            def s2():
                                P.op("dve", lambda e: e.tensor_tensor(tsum[:, tc, :], F_, G_, ALU.add), reads=[nF, nG], writes=TSUM)
                                P.op("dve", lambda e: e.tensor_tensor(tdiff[:, tc, :], F_, G_, ALU.subtract), reads=[nF, nG], writes=TDIFF)

                            def s3_():
                                P.op("act", lambda e: e.activation(OS[b][:, 0:256], F_, AF.Square), reads=[nF], writes=[f"OS{b}"])
                                P.op("act", lambda e: e.activation(OS[b][:, 256:512], G_, AF.Square), reads=[nG], writes=[f"OS{b}"])

                            def s4():
                                P.op("pe", lambda e: e.matmul(PSB[6][:], ones[:], OS[b][:], start=(tc == 0), stop=(tc == nfc - 1)), reads=["ones", f"OS{b}"], writes=["ps6"])
                            return [s0, s1, s2, s3_, s4]

                        for tp_ in range(nfc // 2):
                            for fa, fb in zip(tap_stages(2 * tp_, 0), tap_stages(2 * tp_ + 1, 1)):
                                fa()
                                fb()
                        P.op("dve", lambda e: e.tensor_copy(TMP[0][:, 0:256], PSB[6][:, 0:256]), reads=["ps6"], writes=["TMP0"])
                        P.op("dve", lambda e: e.tensor_tensor(TMP[0][:, 0:256], TMP[0][:, 0:256], PSB[6][:, 256:512], ALU.add), reads=["ps6", "TMP0"], writes=["TMP0"])
                        P.op("act", lambda e: e.activation(NRM, TMP[0][:, 0:256], AF.Sqrt, bias=cst[:, 0:1], scale=1.0), reads=["TMP0", "cst"], writes=["RSTD"])
                        P.op("dve", lambda e: e.reciprocal(NRM, NRM), reads=["RSTD"], writes=["RSTD"])
                        fpg = 2
                        for fg in range(nfc // fpg):
                            ncol = fpg * 128
                            sF = []
                            for ri in range(2):
                                sF.append(load_w(lambda w, ri=ri, fg=fg, ncol=ncol: [(w[:, 0:nfc * ncol].rearrange("p (k n) -> p k n", k=nfc),
                                                                                      dftF_d[li][ri][:, fg * ncol:(fg + 1) * ncol].rearrange("(k p) n -> p k n", p=128))], half=True))
                            Fv = [wview(sF[ri], nfc, ncol) for ri in range(2)]
                            for fl in range(fpg):
                                fc = fg * fpg + fl
                                fsl = slice(fl * 128, (fl + 1) * 128)
                                pk = [next_ps(), next_ps()]
                                for ri, (tab, TAB) in enumerate(((tsum, TSUM), (tdiff, TDIFF))):
                                    for tc in range(nfc):
                                        P.op("pe", lambda e, ri=ri, tc=tc, tab=tab, fsl=fsl, pk=pk: e.matmul(PSB[pk[ri]][:, 0:256], Fv[ri][:, tc, fsl], tab[:, tc, :],
                                                                                                          start=(tc == 0), stop=(tc == nfc - 1)),
                                             reads=[f"WS{sF[ri]}"] + TAB, writes=[f"ps{pk[ri]}"])
                                P.op("dve", lambda e, pk=pk: e.tensor_tensor(TMP[0][:, 0:256], PSB[pk[0]][:, 0:256], NRM, ALU.mult), reads=[f"ps{pk[0]}", "RSTD"], writes=["TMP0"])
                                P.op("dve", lambda e, pk=pk: e.tensor_tensor(TMP[1][:, 0:256], PSB[pk[1]][:, 0:256], NRM, ALU.mult), reads=[f"ps{pk[1]}", "RSTD"], writes=["TMP1"])
                                for sq in range(nseq):
                                    q = sq * nfc + fc
                                    px = [next_ps(), next_ps()]
                                    for ri in range(2):
                                        for sc in range(nfc):
                                            P.op("pe", lambda e, ri=ri, sc=sc, sq=sq, fsl=fsl, px=px: e.matmul(PSB[px[ri]][:, 0:256], Fv[ri][:, sc, fsl], tmv[:, sq * nfc + sc, :],
                                                                                                            start=(sc == 0), stop=(sc == nfc - 1)),
                                                 reads=[f"WS{sF[ri]}"] + TM_IN, writes=[f"ps{px[ri]}"])
                                    P.op("dve", lambda e, px=px: e.tensor_tensor(TMP[2][:, 0:256], PSB[px[0]][:, 0:256], TMP[0][:, 0:256], ALU.mult), reads=[f"ps{px[0]}", "TMP0"], writes=["TMP2"])
                                    P.op("dve", lambda e, px=px: e.tensor_tensor(TMP[3][:, 0:256], PSB[px[1]][:, 0:256], TMP[1][:, 0:256], ALU.mult), reads=[f"ps{px[1]}", "TMP1"], writes=["TMP3"])
                                    P.op("dve", lambda e, q=q: e.tensor_tensor(YHre[:, q, :], TMP[2][:, 0:256], TMP[3][:, 0:256], ALU.subtract), reads=["TMP2", "TMP3"], writes=YRE)
                                    P.op("dve", lambda e, px=px: e.tensor_tensor(TMP[2][:, 0:256], PSB[px[0]][:, 0:256], TMP[1][:, 0:256], ALU.mult), reads=[f"ps{px[0]}", "TMP1"], writes=["TMP2"])
                                    P.op("dve", lambda e, px=px: e.tensor_tensor(TMP[3][:, 0:256], PSB[px[1]][:, 0:256], TMP[0][:, 0:256], ALU.mult), reads=[f"ps{px[1]}", "TMP0"], writes=["TMP3"])
                                    P.op("dve", lambda e, q=q: e.tensor_tensor(YHim[:, q, :], TMP[2][:, 0:256], TMP[3][:, 0:256], ALU.add), reads=["TMP2", "TMP3"], writes=YIM)
                        hy_proj_conv(2560 + o * 512 + cb * 256, o * 4 + cb * 2, r_sum, r_diff)
                        for sq in range(nseq):
                            for tb in range(L // NB):
                                t0 = sq * L + tb * NB
                                n = t0 // 512
                                if not (L == NB and sq > 0):
                                    sG = []
                                    for ri in range(2):
                                        sG.append(load_w(lambda w, ri=ri, tb=tb: [(w[:, 0:nfc * NB].rearrange("p (k n) -> p k n", k=nfc),
                                                                                    dftG_d[li][ri][:, tb * NB:(tb + 1) * NB].rearrange("(k p) n -> p k n", p=128))], half=True))
                                    Gv = [wview(sG[ri], nfc, NB) for ri in range(2)]
                                def inv_stages(cc):
                                    csl = slice(cc * 128, (cc + 1) * 128)
                                    T0i, T1i = 2 * cc, 2 * cc + 1
                                    TA = TMP[T0i][:, 0:NB]
                                    TB = TMP[T1i][:, 0:NB]
                                    stt = {}
                                    hb = HYB[:, l, o, cb * 2 + cc:cb * 2 + cc + 1]
                                    grow = BIG[:, r_sum + cc, t0:t0 + NB].bitcast(F32)

                                    def s0():
                                        stt["pc"] = next_ps()
                                        cnt = 0
                                        for ri, (YH, YN) in enumerate(((YHre, YRE), (YHim, YIM))):
                                            for fc in range(nfc):
                                                P.op("pe", lambda e: e.matmul(PSB[stt["pc"]][:, 0:NB], YH[:, sq * nfc + fc, csl], Gv[ri][:, fc, :],
                                                                              start=(cnt == 0), stop=(cnt == 2 * nfc - 1)),
                                                     reads=[f"WS{sG[ri]}"] + YN, writes=[f"ps{stt['pc']}"])
                                                cnt += 1

                                    def s1():
                                        stt["pv"] = next_ps()
                                        for c in range(NB // 128):
                                            tcg = t0 // 128 + c
                                            P.op("pe", lambda e: e.transpose(PSB[stt["pv"]][:, c * 128:(c + 1) * 128], tmv[:, tcg, csl].bitcast(F32), ident[:]),
                                                 reads=TM_IN + ["ident"], writes=[f"ps{stt['pv']}"])

                                    def s2():
                                        P.op("act", lambda e: e.activation(TA, PSB[stt["pv"]][:, 0:NB], AF.Identity, scale=hb), reads=[f"ps{stt['pv']}", "HYB"], writes=[f"TMP{T0i}"])

                                    def s3():
                                        P.op("dve", lambda e: e.tensor_tensor(TA, PSB[stt["pc"]][:, 0:NB], TA, ALU.add), reads=[f"ps{stt['pc']}", f"TMP{T0i}"], writes=[f"TMP{T0i}"])

                                    def s4():
                                        if o == 1:
                                            P.op("dve", lambda e: e.tensor_tensor(YOUT[:, cb * 2 + cc, t0:t0 + NB], TA, grow, ALU.mult),
                                                 reads=[f"TMP{T0i}", f"B{r_sum + cc}_{n}"], writes=xr("Y", [cb * 2 + cc], [n]))
                                        else:
                                            P.op("dve", lambda e: e.tensor_tensor(TB, TA, grow, ALU.mult),
                                                 reads=[f"TMP{T0i}", f"B{r_sum + cc}_{n}"], writes=[f"TMP{T1i}"])

                                    def s5():
                                        if o == 0:
                                            stt["pz"] = next_ps()
                                            for c in range(NB // 128):
                                                P.op("pe", lambda e: e.transpose(PSB[stt["pz"]][:, c * 128:(c + 1) * 128], TMP[T1i][:, c * 128:(c + 1) * 128], ident[:]),
                                                     reads=[f"TMP{T1i}", "ident"], writes=[f"ps{stt['pz']}"])

                                    def s6():
                                        if o == 0:
                                            tmo = rowpair_tm(tm_out)
                                            tc0 = t0 // 128
                                            nt = NB // 128
                                            P.op("act", lambda e: e.copy(tmo[:, tc0:tc0 + nt, csl], PSB[stt["pz"]][:, 0:nt * 128].rearrange("p (c e) -> p c e", c=nt)),
                                                 reads=[f"ps{stt['pz']}"], writes=rr(tm_out) + rr(tm_out + 1))
                                    return [s0, s1, s2, s3, s4, s5, s6]

                                for fa, fb in zip(inv_stages(0), inv_stages(1)):
                                    fa()
                                    fb()
                if tile == 0 and l == 0:
                    dump("hy", YOUT.bitcast(F32), xr("Y", range(4), range(NBLK)))
                branch_out(w_hy_o[l], 4, YOUT, "Y", 2, nb == 0, False)
                nb += 1
            if nb == 0:
                for m in R8:
                    P.op("dve", lambda e, m=m: e.memset(MERGED[:, m, :], 0.0), writes=xr("MG", [m], range(NBLK)))

            def epi_res(gi, mp=mp):
                gap = mp(gi)

                def f(m, n, pi):
                    P.op("dve", lambda e, m=m, n=n, pi=pi: e.scalar_tensor_tensor(
                        X[:, m, BLK[n]], PSB[pi][:], gap[:, m:m + 1], X[:, m, BLK[n]], ALU.mult, ALU.add),
                        reads=[f"ps{pi}", "MODP"] + xr("X", [m], [n]), writes=xr("X", [m], [n]))
                return f
            proj_fm(w_out[l], 0, 1024, 8, MERGED, "MG", epi_res(2))
            if tile == 0 and l == 0:
                dump("x1", X[:], xr("X", R8, range(NBLK)))

            rms_norm(H, "H", mp(3), mp(4))
            GA = BIG
            for i in range(11):
                def pairs(w, i=i, l=l):
                    return [(w[:, 0:2048].rearrange("p (k n) -> p k n", k=8),
                             w_ffn_in[l][:, i * 256:(i + 1) * 256].rearrange("(k p) n -> p k n", p=128)),
                            (w[:, 2048:4096].rearrange("p (k n) -> p k n", k=8),
                             w_ffn_in[l][:, DFF + i * 256:DFF + (i + 1) * 256].rearrange("(k p) n -> p k n", p=128))]
                s = load_w(pairs)
                wa = wview(s, 8, 256, 0)
                wb = wview(s, 8, 256, 2048)
                for mm in range(2):
                    for n in range(NBLK):
                        pa = next_ps()
                        for k in R8:
                            P.op("pe", lambda e, pa=pa, wa=wa, k=k, n=n, mm=mm: e.matmul(PSB[pa][:], wa[:, k, mm * 128:(mm + 1) * 128], H[:, k, BLK[n]],
                                                                                     start=(k == 0), stop=(k == 7)),
                                 reads=[f"WS{s}"] + xr("H", [k], [n]), writes=[f"ps{pa}"])
                        pb = next_ps()
                        for k in R8:
                            P.op("pe", lambda e, pb=pb, wb=wb, k=k, n=n, mm=mm: e.matmul(PSB[pb][:], wb[:, k, mm * 128:(mm + 1) * 128], H[:, k, BLK[n]],
                                                                                     start=(k == 0), stop=(k == 7)),
                                 reads=[f"WS{s}"] + xr("H", [k], [n]), writes=[f"ps{pb}"])
                        t0 = next_tmp()
                        P.op("act", lambda e, t0=t0, pa=pa: e.activation(TMP[t0][:], PSB[pa][:], AF.Silu),
                             reads=[f"ps{pa}"], writes=[f"TMP{t0}"])
                        j = i * 2 + mm
                        P.op("dve", lambda e, t0=t0, pb=pb, j=j, n=n: e.tensor_tensor(GA[:, j, BLK[n]], TMP[t0][:], PSB[pb][:], ALU.mult),
                             reads=[f"TMP{t0}", f"ps{pb}"], writes=xr("GA", [j], [n]))
            for m in R8:
                def pairs(w, m=m, l=l):
                    return [(w[:, 0:22 * 128].rearrange("p (k n) -> p k n", k=22),
                             w_ffn_out[l][:, m * 128:(m + 1) * 128].rearrange("(k p) n -> p k n", p=128))]
                s = load_w(pairs)
                wv = wview(s, 22, 128)
                for n in range(NBLK):
                    pi = next_ps()
                    for k in range(22):
                        P.op("pe", lambda e, pi=pi, wv=wv, k=k, n=n: e.matmul(PSB[pi][:], wv[:, k, :], GA[:, k, BLK[n]],
                                                                              start=(k == 0), stop=(k == 21)),
                             reads=[f"WS{s}"] + xr("GA", [k], [n]), writes=[f"ps{pi}"])
                    epi_res(5)(m, n, pi)
            if tile == 0 and l == 0:
                dump("x2", X[:], xr("X", R8, range(NBLK)))
        rms_norm(H, "H", nf, None)
        P.dma([(yT[tile][:, k, :], H[:, k, :].bitcast(F32)) for k in R8], "y_out", reads=xr("H", R8, range(NBLK)))

    P.wait_all("sp")
    P.emit()
    P.close()
    return nc


def _fm(x2d):
    t = x2d.shape[0]
    return np.ascontiguousarray(x2d.T.reshape(8, 128, t).transpose(1, 0, 2))


def _unfm(y):
    t = y.shape[2]
    return np.ascontiguousarray(y.transpose(1, 0, 2).reshape(1024, t).T)


def _vec_fm(v, nch):
    return np.ascontiguousarray(v.reshape(nch, 128).T)


def prep_core_inputs(inp, core):
    f = np.float32
    b = core % 4
    m = {}
    m["xT_p"] = _fm(inp["x_prompt"][core * 4:(core + 1) * 4].reshape(1024, 1024))
    m["xT_s"] = _fm(inp["x_sample"][b])
    m["cond"] = np.ascontiguousarray(np.stack([_vec_fm(inp["c_ctx"], 8), _vec_fm(inp["c"][b], 8)], axis=-1))
    m["b_mod_t"] = np.ascontiguousarray(np.stack([_vec_fm(inp["b_mod"][l], 48) for l in range(DEPTH)], axis=1))
    m["norm1_t"] = np.ascontiguousarray(np.stack([_vec_fm(inp["norm1"][l], 8) for l in range(DEPTH)], axis=1))
    m["norm2_t"] = np.ascontiguousarray(np.stack([_vec_fm(inp["norm2"][l], 8) for l in range(DEPTH)], axis=1))
    m["normf_t"] = _vec_fm(inp["norm_f"], 8)
    m["ident"] = np.eye(128, dtype=f)
    m.update(_consts())
    m["hy_w3"] = inp["hy_w3"]
    hyc = np.zeros((128, DEPTH, 4, 12), f)
    for l in range(DEPTH):
        for tp in range(3):
            hyc[:, l, tp, :] = _vec_fm(inp["hy_conv_w"][l, tp], 12)
        hyc[:, l, 3, :] = _vec_fm(inp["hy_conv_b"][l], 12)
    m["hyc"] = hyc
    hyb = np.zeros((128, DEPTH, 2, 4), f)
    for l in range(DEPTH):
        for o in range(2):
            hyb[:, l, o, :] = _vec_fm(inp["hy_bias"][l, o], 4)
    m["hyb"] = hyb
    m["hw1"] = np.ascontiguousarray(inp["hy_w1"].transpose(1, 0, 2))
    m["hw2"] = np.ascontiguousarray(inp["hy_w2"].transpose(1, 0, 2))
    hys = np.zeros((64, DEPTH, 4), f)
    for l in range(DEPTH):
        hys[:, l, 0] = inp["hy_b1"][l]
        hys[:, l, 1] = inp["hy_b2"][l]
        hys[:, l, 2] = inp["hy_freq"][l, 0]
        hys[:, l, 3] = inp["hy_freq"][l, 1]
    m["hys"] = hys
    def sp_layout(a):
        return np.ascontiguousarray(a.reshape(2, 16, 2, 64).transpose(2, 3, 0, 1).reshape(128, 32))
    s5sp = np.zeros((128, DEPTH, 3, 32), f)
    for l in range(DEPTH):
        s5sp[:, l, 0] = sp_layout(inp["s5_lam_re"][l])
        s5sp[:, l, 1] = sp_layout(inp["s5_lam_im"][l])
        s5sp[:, l, 2] = sp_layout(np.broadcast_to(inp["s5_log_dt"][l][:, :, None], (2, 32, 64)))
    m["s5sp"] = s5sp
    h0 = inp["state_s5"][b]
    s5h0 = np.zeros((128, DEPTH, 32, 2), f)
    for l in range(DEPTH):
        for ri in range(2):
            s5h0[:, l, :, ri] = sp_layout(h0[l, :, :, :, ri])
    m["s5h0"] = s5h0
    s5bz = np.zeros((DEPTH, 128, 2, 32, 2, 16), f)
    s5cz = np.zeros((DEPTH, 128, 32, 2, 2, 16), f)
    for l in range(DEPTH):
        for ri, key in enumerate(("s5_b_re", "s5_b_im")):
            Bq = inp[key][l].reshape(2, 16, 2, 64, 16)
            for gl in range(2):
                s5bz[l, gl * 64:(gl + 1) * 64, ri, :, gl, :] = Bq[:, :, gl].transpose(2, 0, 1, 3).reshape(64, 32, 16)
        for ri, key in enumerate(("s5_c_re", "s5_c_im")):
            Cq = inp[key][l].reshape(2, 16, 2, 16, 64)
            for gl in range(2):
                s5cz[l, gl * 64:(gl + 1) * 64, :, ri, gl, :] = Cq[:, :, gl].transpose(3, 0, 1, 2).reshape(64, 32, 16)
    m["s5bz"] = s5bz.reshape(DEPTH, 128, 2048)
    m["s5cz"] = s5cz.reshape(DEPTH, 128, 2048)
    m["s5d"] = np.ascontiguousarray(np.stack([_vec_fm(inp["s5_d"][l], 4) for l in range(DEPTH)], axis=1))
    m["ret_decay_bc"] = np.ascontiguousarray(np.broadcast_to(inp["ret_decay"].reshape(1, DEPTH * 8), (128, DEPTH * 8)))
    m["state_ret_c"] = np.ascontiguousarray(inp["state_ret"][b])
    for k in ("w_mod", "w_in", "w_s5_glu", "w_ret_o", "w_hy_o", "w_out", "w_ffn_in", "w_ffn_out"):
        m[k] = np.ascontiguousarray(inp[k])
    return {k: np.asarray(v, dtype=f) for k, v in m.items()}


_CONST_CACHE = {}


def _consts():
    if _CONST_CACHE:
        return _CONST_CACHE
    f = np.float32
    c = {}
    p = np.arange(128, dtype=f)[:, None]
    c["ramp_p"] = (np.arange(512, dtype=f)[None, :] - p - 256.0).astype(f)
    c["ramp_s"] = (np.arange(2048, dtype=f)[None, :] - p - 1024.0).astype(f)
    L = 1024
    rows = np.repeat(np.arange(L // 64, dtype=f), 64)
    cols = np.tile(np.arange(64, dtype=f), L // 64)
    inv = (f(10000.0) ** (-np.arange(32, dtype=f) / f(32))).astype(f)
    ang = np.concatenate([rows[:, None] * inv, cols[:, None] * inv], axis=-1).astype(f)
    cos = np.cos(ang).astype(f).T
    sin = np.sin(ang).astype(f).T
    c["rope_cs"] = np.ascontiguousarray(np.stack([np.concatenate([cos, cos], 0), np.concatenate([-sin, sin], 0)], 0))
    i = np.arange(1024, dtype=f)
    c["pos12"] = np.ascontiguousarray(np.stack([np.broadcast_to(i + 1.0, (128, 1024)), np.broadcast_to(1023.0 - i, (128, 1024))], 0)).astype(f)
    pp = np.arange(128, dtype=f)
    posT = np.zeros((128, 2, 2), f)
    for cl in range(2):
        posT[:, 0, cl] = 255.0 - (cl * 128 + pp)
        posT[:, 1, cl] = cl * 128 + pp
    c["posT"] = posT
    perm = np.zeros((128, 128), f)
    for mcol in range(128):
        perm[(mcol + 64) % 128, mcol] = 1.0
    c["perm"] = perm
    zT = np.zeros((2, 33, 1024), f)
    ntn = np.zeros((128, 2, 8), f)
    bands = np.linspace(1e-4, 15, 16, dtype=f)
    for li, Lh in enumerate((256, 1024)):
        t = np.arange(Lh, dtype=f)
        tn = (t / f(Lh)).astype(f)
        angz = (f(2.0 * math.pi / Lh) * t[:, None] * bands[None, :]).astype(f)
        z = np.concatenate([tn[:, None], np.cos(angz), -np.sin(angz)], axis=-1).astype(f)
        zT[li, :, :Lh] = z.T
        for tc in range(Lh // 128):
            ntn[:, li, tc] = -(tc * 128 + np.arange(128, dtype=f)) / f(Lh)
        s64 = np.arange(Lh, dtype=np.float64)
        th = 2.0 * np.pi * (s64[None, :] + 0.5) / (2.0 * Lh)
        ang = s64[:, None] * th
        F = np.stack([np.cos(ang), -np.sin(ang)], 0).astype(f)
        G = np.stack([np.cos(ang).T / Lh, -np.sin(ang).T / Lh], 0).astype(f)
        key = "p" if Lh == 256 else "s"
        c["dftF_" + key] = np.ascontiguousarray(F)
        c["dftG_" + key] = np.ascontiguousarray(G)
    c["zT"] = zT
    c["ntn"] = ntn
    dmin = -math.log(1e-2) / 1.5
    dmax = -math.log(1e-2) / 0.3
    rate = np.linspace(dmin, dmax, 512, dtype=f)
    c["tpos"] = np.ascontiguousarray(np.broadcast_to(np.arange(512, dtype=f), (128, 512))).astype(f)
    c["rate_bc"] = np.ascontiguousarray(np.broadcast_to(rate, (128, 512))).astype(f)
    _CONST_CACHE.update(c)
    return _CONST_CACHE


_NC_CACHE = {}


def kernel(**inputs):
    inp = {k: np.asarray(v) for k, v in inputs.items()}
    if "nc" not in _NC_CACHE:
        _NC_CACHE["nc"] = build_program()
    nc = _NC_CACHE["nc"]
    in_maps = [prep_core_inputs(inp, c) for c in range(8)]
    res = run_bass_kernel_spmd(nc, in_maps, core_ids=list(range(8)))
    y_prompt = np.zeros((32, 256, 1024), np.float32)
    y_sample = np.zeros((4, 1024, 1024), np.float32)
    for c in range(8):
        r = res.results[c]
        y_prompt[c * 4:(c + 1) * 4] = _unfm(r["yT_p"]).reshape(4, 256, 1024)
        if c < 4:
            y_sample[c] = _unfm(r["yT_s"])
    s5 = np.zeros((32, DEPTH, 2, 32, 64, 2), np.float32)
    ret = np.zeros((32, DEPTH, 2, 4, 128, 128), np.float32)
    for c in range(8):
        ret[c * 4:(c + 1) * 4] = res.results[c]["new_state_ret_c"]
        s5[c * 4:(c + 1) * 4] = res.results[c]["new_state_s5_c"]
    return (y_prompt, y_sample, s5, ret)
```

```python
import contextlib
import math
import numpy as np
import concourse.bass as bass
import concourse.mybir as mybir
from concourse.bass_utils import run_bass_kernel_spmd

F32 = mybir.dt.float32
F32R = mybir.dt.float32r
ALU = mybir.AluOpType
AF = mybir.ActivationFunctionType

ENGS = ("pe", "dve", "act", "pool", "sp")
DEMOD_ENG = "dve"
SKIP_OLD_SAME_ENGINE = True

D = 1024
T = 1024
NBLK = 2
DEPTH = 2
DFF = 2816
EPS = 1e-6
GN_EPS = 1e-5


class _Rec:
    def __init__(self):
        self.calls = []

    def __getattr__(self, name):
        def f(*a, **kw):
            self.calls.append((name, a, kw))
            return None
        return f


class Prog:
    def __init__(self, nc):
        self.nc = nc
        self.es = contextlib.ExitStack()
        self.q = {e: [] for e in ENGS}
        self.cnt = {}
        self.lastw = {}
        self.readers = {}
        self.waited = {e: {} for e in ENGS}
        self.sems = {}
        self.alias = {}

    def _x(self, names):
        out = []
        for n in names:
            out.extend(self.alias.get(n, (n,)))
        return tuple(out)

    def sb(self, name, shape, dt=F32):
        return self.es.enter_context(self.nc.sbuf_tensor("sb_" + name, list(shape), dt))

    def ps(self, name, shape, dt=F32):
        return self.es.enter_context(self.nc.psum_tensor("pp_" + name, list(shape), dt))

    def _sem(self, key):
        if key not in self.sems:
            self.sems[key] = self.es.enter_context(self.nc.semaphore("s_" + key))
            self.cnt[key] = 0
        return self.sems[key]

    def _deps(self, eng, reads, writes):
        toks = []
        for r in reads:
            t = self.lastw.get(r)
            if t is not None:
                toks.append(t)
        for w in writes:
            t = self.lastw.get(w)
            if t is not None:
                toks.append(t)
            toks.extend(self.readers.get(w, ()))
        waits = {}
        for (k, v, e) in toks:
            if e == eng and eng == "pe":
                continue
            if e == eng and eng != "sp" and SKIP_OLD_SAME_ENGINE and v < self.cnt.get(eng, 0):
                continue
            if self.waited[eng].get(k, 0) >= v:
                continue
            if waits.get(k, 0) < v:
                waits[k] = v
        for k, v in waits.items():
            self.waited[eng][k] = v
        return waits

    def _commit(self, tok, reads, writes):
        for r in reads:
            self.readers.setdefault(r, []).append(tok)
        for w in writes:
            self.lastw[w] = tok
            self.readers[w] = []

    def op(self, eng, fn, reads=(), writes=()):
        reads = self._x(reads)
        writes = self._x(writes)
        waits = self._deps(eng, reads, writes)
        self._sem(eng)
        self.cnt[eng] += 1
        tok = (eng, self.cnt[eng], eng)
        rec = _Rec()
        fn(rec)
        assert len(rec.calls) == 1
        name, a, kw = rec.calls[0]
        fn = (lambda e, name=name, a=a, kw=kw: getattr(e, name)(*a, **kw))
        self.q[eng].append((fn, waits, [(eng, 1)]))
        self._commit(tok, reads, writes)
        return tok

    def dma(self, pairs, sem, reads=(), writes=(), eng="sp"):
        reads = self._x(reads)
        writes = self._x(writes)
        waits = self._deps(eng, reads, writes)
        key = "d_" + sem
        self._sem(key)
        self.cnt[key] += 16 * len(pairs)
        tok = (key, self.cnt[key], "dma")
        fns = [(lambda e, o=o, i=i: e.dma_start(out=o, in_=i)) for (o, i) in pairs]
        self.q[eng].append((fns, waits, [(key, 16)]))
        self._commit(tok, reads, writes)
        return tok

    def wait_all(self, eng="sp"):
        waits = {}
        for k, v in self.cnt.items():
            if v > 0 and self.waited[eng].get(k, 0) < v:
                waits[k] = v
        self.q[eng].append((None, waits, []))

    def emit(self):
        nc = self.nc
        with nc.Block() as block:
            def run(eng_name):
                def body(e):
                    for (fn, waits, incs) in self.q[eng_name]:
                        for k, v in waits.items():
                            e.wait_ge(self.sems[k], v)
                        if fn is None:
                            continue
                        for f in (fn if isinstance(fn, list) else [fn]):
                            ins = f(e)
                            for (k, a) in incs:
                                ins.then_inc(self.sems[k], a)
                return body
            for name, reg in (("sp", block.sync), ("pe", block.tensor), ("dve", block.vector),
                              ("act", block.scalar), ("pool", block.gpsimd)):
                if self.q[name]:
                    reg(run(name))

    def close(self):
        self.es.close()


def build_program(dbg=None, branches=("s5", "ret", "hy"), ntiles=2, nlayers=DEPTH):
    dbg = dbg or {}
    nc = bass.Bass("TRN2", target_bir_lowering=False)
    P = Prog(nc)

    def din(name, shape, dt=F32):
        return nc.dram_tensor(name, list(shape), dt, kind="ExternalInput").ap()

    def dout(name, shape, dt=F32):
        return nc.dram_tensor(name, list(shape), dt, kind="ExternalOutput").ap()

    xT = [din("xT_p", [128, 8, T]), din("xT_s", [128, 8, T])]
    yT = [dout("yT_p", [128, 8, T]), dout("yT_s", [128, 8, T])]
    cond_d = din("cond", [128, 8, 2])
    bmod_d = din("b_mod_t", [128, DEPTH, 48])
    n1_d = din("norm1_t", [128, DEPTH, 8])
    n2_d = din("norm2_t", [128, DEPTH, 8])
    nf_d = din("normf_t", [128, 8])
    ident_d = din("ident", [128, 128])
    w_mod = din("w_mod", [DEPTH, D, 6 * D])
    w_in = din("w_in", [DEPTH, D, 7168])
    w_s5_glu = din("w_s5_glu", [DEPTH, 512, 2048])
    w_ret_o = din("w_ret_o", [DEPTH, 512, 1024])
    w_hy_o = din("w_hy_o", [DEPTH, 512, 1024])
    w_out = din("w_out", [DEPTH, D, D])
    w_ffn_in = din("w_ffn_in", [DEPTH, D, 2 * DFF])
    w_ffn_out = din("w_ffn_out", [DEPTH, DFF, D])
    retdec_d = din("ret_decay_bc", [128, DEPTH * 8])
    ramp_d = [din("ramp_p", [128, 512]), din("ramp_s", [128, 2048])]
    rope_d = din("rope_cs", [2, 128, 1024])
    pos_d = din("pos12", [2, 128, 1024])
    posT_d = din("posT", [128, 2, 2])
    perm_d = din("perm", [128, 128])
    sret_d = din("state_ret_c", [DEPTH, 2, 4, 128, 128])
    nsret_d = dout("new_state_ret_c", [4, DEPTH, 2, 4, 128, 128])
    hy_w3 = din("hy_w3", [DEPTH, 64, 2048])
    hyc_d = din("hyc", [128, DEPTH, 4, 12])
    hyb_d = din("hyb", [128, DEPTH, 2, 4])
    hw1_d = din("hw1", [33, DEPTH, 64])
    hw2_d = din("hw2", [64, DEPTH, 64])
    hys_d = din("hys", [64, DEPTH, 4])
    zT_d = din("zT", [2, 33, 1024])
    ntn_d = din("ntn", [128, 2, 8])
    rate_d = din("rate_bc", [128, 512])
    dftF_d = [din("dftF_p", [2, 256, 256]), din("dftF_s", [2, 1024, 1024])]
    dftG_d = [din("dftG_p", [2, 256, 256]), din("dftG_s", [2, 1024, 1024])]
    s5sp_d = din("s5sp", [128, DEPTH, 3, 32])
    s5h0_d = din("s5h0", [128, DEPTH, 32, 2])
    s5bz_d = din("s5bz", [DEPTH, 128, 2048])
    s5cz_d = din("s5cz", [DEPTH, 128, 2048])
    s5d_d = din("s5d", [128, DEPTH, 4])
    tpos_d = din("tpos", [128, 512])
    ns5_d = dout("new_state_s5_c", [4, DEPTH, 2, 32, 64, 2])
    dbg_out = {k: dout("dbg_" + k, shp) for k, shp in dbg.items()}

    X = P.sb("X", [128, 8, T])
    H = P.sb("H", [128, 8, T], F32R)
    BIG = P.sb("BIG", [128, 22, T], F32R)
    WSF = [P.sb(f"WS{i}", [128, 4096], F32R) for i in range(2)]
    WS = [WSF[0], WSF[1], WSF[0][:, 0:2048], WSF[0][:, 2048:4096], WSF[1][:, 0:2048], WSF[1][:, 2048:4096]]
    P.alias = {"TMP0": ("TMP0h0", "TMP0h1"), "TMP1": ("TMP1h0", "TMP1h1"), "TMP2": ("TMP2h0", "TMP2h1"), "TMP3": ("TMP3h0", "TMP3h1"),
               "WS0": ("WSh0", "WSh1"), "WS1": ("WSh2", "WSh3"), "WS2": ("WSh0",), "WS3": ("WSh1",), "WS4": ("WSh2",), "WS5": ("WSh3",)}
    TMP = [P.sb(f"TMP{i}", [128, 512]) for i in range(4)]
    RSTD = P.sb("RSTD", [128, 512])
    ones = P.sb("ones", [128, 128], F32R)
    ident = P.sb("ident", [128, 128])
    cst = P.sb("cst", [128, 8])
    scond = P.sb("scond", [128, 8, 2], F32R)
    n1 = P.sb("n1", [128, DEPTH, 8])
    n2 = P.sb("n2", [128, DEPTH, 8])
    nf = P.sb("nf", [128, 8])
    MODP = P.sb("MODP", [128, DEPTH, 2, 6, 8])
    PSB = [P.ps(f"ps{i}", [128, 512]) for i in range(8)]
    modv = TMP[1][:, 0:DEPTH * 96].rearrange("p (l j c) -> p l j c", l=DEPTH, c=2)
    bmod = TMP[2][:, 0:DEPTH * 48].rearrange("p (l j) -> p l j", l=DEPTH)
    cond = TMP[3][:, 0:16].rearrange("p (k c) -> p k c", c=2)
    LG = P.sb("LG", [128, DEPTH * 8])
    NLG = P.sb("NLG", [128, DEPTH * 8])
    perm = P.sb("perm", [128, 128], F32R)
    posT = P.sb("posT", [128, 2, 2])
    KD = P.sb("KD", [128, 2, 2])
    PT = [P.sb(f"PT{i}", [128, 512], F32R) for i in range(2)]
    OS = [P.sb(f"OS{i}", [128, 512], F32R) for i in range(2)]
    S0T = P.sb("S0T", [128, 2, 128], F32R)
    KTS = PT[0][:, :].rearrange("p (d c e) -> p d c e", d=2, c=2)
    HYC = P.sb("HYC", [128, DEPTH, 4, 12])
    HYB = P.sb("HYB", [128, DEPTH, 2, 4])
    BST = P.sb("BST", [128, 2, 128], F32R)
    HYS = P.sb("HYS", [64, DEPTH, 4])
    HYF = P.sb("HYF", [64, DEPTH, 2])
    NTN = P.sb("NTN", [128, 2, 8])
    H0 = P.sb("H0", [128, 32, 2])

    ST5 = P.sb("ST5", [128, 8, 2])
    S5D = P.sb("S5D", [128, DEPTH, 4])
    print("sbuf bytes remaining", nc.sbuf_bytes_remaining)

    st = {"ws": 0, "ps": 0, "tmp": 0, "psn": 6, "wsh": 0}

    def next_ws():
        i = st["ws"]
        st["ws"] = (i + 1) % 2
        return i

    def next_ps():
        i = st["ps"]
        st["ps"] = (i + 1) % st["psn"]
        return i

    def next_tmp():
        i = st["tmp"]
        st["tmp"] = (i + 1) % 4
        return i

    def load_w(pairs_fn, half=False):
        if half:
            s = 2 + st["wsh"]
            st["wsh"] = (st["wsh"] + 1) % 4
        else:
            s = next_ws()
        P.dma(pairs_fn(WS[s]), f"ws{s}", writes=[f"WS{s}"], eng="pool")
        return s

    def wview(s, kc, n, off=0):
        return WS[s][:, off:off + kc * n].rearrange("p (k n) -> p k n", k=kc)

    def dump(key, ap, reads):
        if key in dbg_out:
            P.dma([(dbg_out[key], ap)], "dbg_" + key, reads=reads)

    ROWBASE = {"MG": 0, "U": 8, "Y": 12, "SQ": 14, "GA": 0}

    def xr(buf, ks, ns):
        if buf in ROWBASE:
            return [f"B{ROWBASE[buf] + k}_{n}" for k in ks for n in ns]
        return [f"{buf}{k}_{n}" for k in ks for n in ns]

    BLK = [slice(0, 512), slice(512, 1024)]
    R8 = range(8)

    P.dma([(ident[:], ident_d)], "c_ident", writes=["ident"])
    P.dma([(cond, cond_d)], "c_cond", writes=["TMP3"])
    P.dma([(bmod, bmod_d)], "c_bmod", writes=["TMP2"])
    P.dma([(n1[:], n1_d)], "c_n1", writes=["n1"])
    P.dma([(n2[:], n2_d)], "c_n2", writes=["n2"])
    P.dma([(nf[:], nf_d)], "c_nf", writes=["nf"])
    P.op("dve", lambda e: e.memset(TMP[0][:, 0:128], 1.0), writes=["TMP0"])
    P.op("dve", lambda e: e.tensor_copy(ones[:], TMP[0][:, 0:128]), reads=["TMP0"], writes=["ones"])
    P.op("dve", lambda e: e.memset(cst[:, 0:1], EPS), writes=["cst"])
    P.op("dve", lambda e: e.memset(cst[:, 1:2], GN_EPS), writes=["cst"])
    P.op("dve", lambda e: e.memset(cst[:, 2:3], 0.0), writes=["cst"])
    P.op("act", lambda e: e.activation(scond[:], cond, AF.Silu), reads=["TMP3"], writes=["scond"])
    P.dma([(LG[:], retdec_d)], "c_lg", writes=["LG"])
    P.dma([(HYC[:], hyc_d)], "c_hyc", writes=["HYC"])
    P.dma([(S5D[:], s5d_d)], "c_s5d", writes=["S5D"])
    P.op("dve", lambda e: e.memset(cst[:, 4:5], math.pi / 2.0), writes=["cst"])
    P.dma([(HYB[:], hyb_d)], "c_hyb", writes=["HYB"])
    P.dma([(HYS[:], hys_d)], "c_hys", writes=["HYS"])
    P.dma([(NTN[:], ntn_d)], "c_ntn", writes=["NTN"])
    P.op("dve", lambda e: e.memset(TMP[3][:, 0:256], 0.0), writes=["TMP3"])
    P.op("dve", lambda e: e.tensor_copy(BST[:, :, :], TMP[3][:, 0:256].rearrange("p (a b) -> p a b", a=2)), reads=["TMP3"], writes=["BST"])
    P.op("dve", lambda e: e.tensor_tensor(HYF[:], HYS[:, :, 0:2], HYS[:, :, 2:4], ALU.mult), reads=["HYS"], writes=["HYF"])
    P.dma([(posT[:], posT_d)], "c_posT", writes=["posT"])
    P.dma([(perm[:], perm_d)], "c_perm", writes=["perm"], eng="pool")
    P.op("act", lambda e: e.activation(LG[:], LG[:], AF.Exp), reads=["LG"], writes=["LG"])
    P.op("dve", lambda e: e.memset(cst[:, 3:4], 1.0), writes=["cst"])
    P.op("act", lambda e: e.activation(LG[:], LG[:], AF.Ln, bias=cst[:, 3:4], scale=-1.0), reads=["LG", "cst"], writes=["LG"])
    P.op("dve", lambda e: e.tensor_scalar(NLG[:], LG[:], -1.0, None, ALU.mult), reads=["LG"], writes=["NLG"])

    for l in range(nlayers):
        for cb in range(12):
            s = load_w(lambda w, l=l, cb=cb: [(w[:, 0:4096].rearrange("p (k n) -> p k n", k=8),
                                                 w_mod[l][:, cb * 512:(cb + 1) * 512].rearrange("(k p) n -> p k n", p=128))])
            wv = wview(s, 8, 512)
            pi = next_ps()
            for m in range(4):
                j = cb * 4 + m
                for k in range(8):
                    P.op("pe", lambda e, pi=pi, wv=wv, m=m, k=k: e.matmul(
                        PSB[pi][:, 2 * m:2 * m + 2], wv[:, k, m * 128:(m + 1) * 128], scond[:, k, :],
                        start=(k == 0), stop=(k == 7)), reads=[f"WS{s}", "scond"], writes=[f"ps{pi}"])
            P.op("dve", lambda e, pi=pi, l=l, cb=cb: e.tensor_tensor(
                modv[:, l, cb * 4:(cb + 1) * 4, :], PSB[pi][:, 0:8].rearrange("p (m c) -> p m c", c=2),
                bmod[:, l, cb * 4:(cb + 1) * 4].unsqueeze(2).to_broadcast([128, 4, 2]), ALU.add),
                reads=[f"ps{pi}", "TMP2"], writes=["TMP1"])
        for t in range(2):
            def mv(i, l=l, t=t):
                return modv[:, l, i * 8:(i + 1) * 8, t]
            P.op("dve", lambda e, l=l, t=t, mv=mv: e.scalar_tensor_tensor(
                MODP[:, l, t, 0, :], mv(1), 1.0, n1[:, l, :], ALU.add, ALU.mult), reads=["TMP1", "n1"], writes=["MODP"])
            P.op("dve", lambda e, l=l, t=t, mv=mv: e.tensor_copy(MODP[:, l, t, 1, :], mv(0)), reads=["TMP1"], writes=["MODP"])
            P.op("dve", lambda e, l=l, t=t, mv=mv: e.tensor_copy(MODP[:, l, t, 2, :], mv(2)), reads=["TMP1"], writes=["MODP"])
            P.op("dve", lambda e, l=l, t=t, mv=mv: e.scalar_tensor_tensor(
                MODP[:, l, t, 3, :], mv(4), 1.0, n2[:, l, :], ALU.add, ALU.mult), reads=["TMP1", "n2"], writes=["MODP"])
            P.op("dve", lambda e, l=l, t=t, mv=mv: e.tensor_copy(MODP[:, l, t, 4, :], mv(3)), reads=["TMP1"], writes=["MODP"])
            P.op("dve", lambda e, l=l, t=t, mv=mv: e.tensor_copy(MODP[:, l, t, 5, :], mv(5)), reads=["TMP1"], writes=["MODP"])
    dump("modp", MODP[:], ["MODP"])

    def rms_norm(dst, dst_name, a_ap, b_ap):
        sq = BIG[:, 14:22, :]
        for n in range(NBLK):
            for k in R8:
                P.op("act", lambda e, k=k, n=n: e.activation(sq[:, k, BLK[n]], X[:, k, BLK[n]], AF.Square),
                     reads=xr("X", [k], [n]), writes=xr("SQ", [k], [n]))
            pi = next_ps()
            for k in R8:
                P.op("pe", lambda e, pi=pi, k=k, n=n: e.matmul(PSB[pi][:], ones[:], sq[:, k, BLK[n]], start=(k == 0), stop=(k == 7)),
                     reads=["ones"] + xr("SQ", [k], [n]), writes=[f"ps{pi}"])
            P.op("act", lambda e, pi=pi: e.activation(RSTD[:], PSB[pi][:], AF.Sqrt, bias=cst[:, 0:1], scale=1.0 / D),
                 reads=[f"ps{pi}", "cst"], writes=["RSTD"])
            P.op("dve", lambda e: e.reciprocal(RSTD[:], RSTD[:]), reads=["RSTD"], writes=["RSTD"])
            for k in R8:
                ti = next_tmp()
                P.op("dve", lambda e, ti=ti, k=k, n=n: e.scalar_tensor_tensor(
                    TMP[ti][:], X[:, k, BLK[n]], a_ap[:, k:k + 1], RSTD[:], ALU.mult, ALU.mult),
                    reads=xr("X", [k], [n]) + ["RSTD", "MODP", "nf"], writes=[f"TMP{ti}"])
                if b_ap is not None:
                    P.op("act", lambda e, ti=ti, k=k, n=n: e.activation(dst[:, k, BLK[n]], TMP[ti][:], AF.Identity,
                                                                     bias=b_ap[:, k:k + 1], scale=1.0),
                         reads=[f"TMP{ti}", "MODP"], writes=xr(dst_name, [k], [n]))
                else:
                    P.op("act", lambda e, ti=ti, k=k, n=n: e.copy(dst[:, k, BLK[n]], TMP[ti][:]),
                         reads=[f"TMP{ti}"], writes=xr(dst_name, [k], [n]))

    def proj_fm(wsrc, col0, ncols, kc, rhs, rhs_name, epi, cols_per_load=512):
        for c0 in range(0, ncols, cols_per_load):
            nl = min(cols_per_load, ncols - c0)
            s = load_w(lambda w, c0=c0, nl=nl: [(w[:, 0:kc * nl].rearrange("p (k n) -> p k n", k=kc),
                                                 wsrc[:, col0 + c0:col0 + c0 + nl].rearrange("(k p) n -> p k n", p=128))])
            wv = wview(s, kc, nl)
            for m in range(nl // 128):
                for n in range(NBLK):
                    pi = next_ps()
                    for k in range(kc):
                        P.op("pe", lambda e, pi=pi, wv=wv, m=m, k=k, n=n: e.matmul(
                            PSB[pi][:], wv[:, k, m * 128:(m + 1) * 128], rhs[:, k, BLK[n]],
                            start=(k == 0), stop=(k == kc - 1)),
                            reads=[f"WS{s}"] + xr(rhs_name, [k], [n]), writes=[f"ps{pi}"])
                    epi((c0 // 128) + m, n, pi)

    MERGED = BIG[:, 0:8, :]
    SCR = BIG

    for tile in range(ntiles):
        P.dma([(X[:, k, :], xT[tile][:, k, :]) for k in R8], "x_in", writes=xr("X", R8, range(NBLK)))
        for l in range(nlayers):
            mp = lambda i, l=l, tile=tile: MODP[:, l, tile, i, :]
            wl = w_in[l]
            rms_norm(H, "H", mp(0), mp(1))
            if tile == 0 and l == 0:
                dump("h1", H[:].bitcast(F32), xr("H", R8, range(NBLK)))

            first = [True]

            def merge_branch(br_cols_fn, gate_idx, l=l, wl=wl):
                raise NotImplementedError

            def gate_merge(m, n, br_ap, br_reads, gate_pi, is_first):
                t1 = next_tmp()
                P.op("act", lambda e, t1=t1, gate_pi=gate_pi: e.activation(TMP[t1][:], PSB[gate_pi][:], AF.Sigmoid),
                     reads=[f"ps{gate_pi}"], writes=[f"TMP{t1}"])
                if is_first:
                    P.op("dve", lambda e, t1=t1, m=m, n=n: e.tensor_tensor(MERGED[:, m, BLK[n]], br_ap, TMP[t1][:], ALU.mult),
                         reads=br_reads + [f"TMP{t1}"], writes=xr("MG", [m], [n]))
                else:
                    t2 = next_tmp()
                    P.op("dve", lambda e, t1=t1, t2=t2: e.tensor_tensor(TMP[t2][:], br_ap, TMP[t1][:], ALU.mult),
                         reads=br_reads + [f"TMP{t1}"], writes=[f"TMP{t2}"])
                    P.op("dve", lambda e, t2=t2, m=m, n=n: e.tensor_tensor(
                        MERGED[:, m, BLK[n]], MERGED[:, m, BLK[n]].bitcast(F32), TMP[t2][:], ALU.add),
                        reads=[f"TMP{t2}"] + xr("MG", [m], [n]), writes=xr("MG", [m], [n]))

            def branch_out(wsrc, kc_b, src, src_name, gate_idx, is_first, glu):
                for m in range(8):
                    gcol = 4096 + gate_idx * 1024 + m * 128
                    ncb = 256 if glu else 128

                    def pairs(w, m=m, gcol=gcol):
                        pr = []
                        if glu:
                            pr.append((w[:, 0:kc_b * 128].rearrange("p (k n) -> p k n", k=kc_b),
                                       wsrc[:, m * 128:(m + 1) * 128].rearrange("(k p) n -> p k n", p=128)))
                            pr.append((w[:, kc_b * 128:kc_b * 256].rearrange("p (k n) -> p k n", k=kc_b),
                                       wsrc[:, 1024 + m * 128:1024 + (m + 1) * 128].rearrange("(k p) n -> p k n", p=128)))
                        else:
                            pr.append((w[:, 0:kc_b * 128].rearrange("p (k n) -> p k n", k=kc_b),
                                       wsrc[:, m * 128:(m + 1) * 128].rearrange("(k p) n -> p k n", p=128)))
                        pr.append((w[:, 1024:1024 + 1024].rearrange("p (k n) -> p k n", k=8),
                                   wl[:, gcol:gcol + 128].rearrange("(k p) n -> p k n", p=128)))
                        return pr
                    s = load_w(pairs, half=True)
                    wa = wview(s, kc_b, 128, 0)
                    wb = wview(s, kc_b, 128, kc_b * 128)
                    wg = wview(s, 8, 128, 1024)
                    for n in range(NBLK):
                        pa = next_ps()
                        for k in range(kc_b):
                            P.op("pe", lambda e, pa=pa, wa=wa, k=k, n=n: e.matmul(PSB[pa][:], wa[:, k, :], src[:, k, BLK[n]],
                                                                                  start=(k == 0), stop=(k == kc_b - 1)),
                                 reads=[f"WS{s}"] + xr(src_name, [k], [n]), writes=[f"ps{pa}"])
                        if glu:
                            pb = next_ps()
                            for k in range(kc_b):
                                P.op("pe", lambda e, pb=pb, wb=wb, k=k, n=n: e.matmul(PSB[pb][:], wb[:, k, :], src[:, k, BLK[n]],
                                                                                      start=(k == 0), stop=(k == kc_b - 1)),
                                     reads=[f"WS{s}"] + xr(src_name, [k], [n]), writes=[f"ps{pb}"])
                        pg = next_ps()
                        for k in range(8):
                            P.op("pe", lambda e, pg=pg, wg=wg, k=k, n=n: e.matmul(PSB[pg][:], wg[:, k, :], H[:, k, BLK[n]],
                                                                                  start=(k == 0), stop=(k == 7)),
                                 reads=[f"WS{s}"] + xr("H", [k], [n]), writes=[f"ps{pg}"])
                        if glu:
                            t0 = next_tmp()
                            P.op("act", lambda e, t0=t0, pb=pb: e.activation(TMP[t0][:], PSB[pb][:], AF.Sigmoid),
                                 reads=[f"ps{pb}"], writes=[f"TMP{t0}"])
                            t3 = next_tmp()
                            P.op("dve", lambda e, t0=t0, t3=t3, pa=pa: e.tensor_tensor(TMP[t3][:], PSB[pa][:], TMP[t0][:], ALU.mult),
                                 reads=[f"ps{pa}", f"TMP{t0}"], writes=[f"TMP{t3}"])
                            gate_merge(m, n, TMP[t3][:], [f"TMP{t3}"], pg, is_first)
                        else:
                            gate_merge(m, n, PSB[pa][:], [f"ps{pa}"], pg, is_first)

            U = SCR[:, 8:12, :]
            Y = SCR[:, 12:16, :]
            nb = 0
            if "s5" in branches or "s5stub" in branches:
                def epi_u(m, n, pi):
                    P.op("act", lambda e, m=m, n=n, pi=pi: e.copy(U[:, m, BLK[n]], PSB[pi][:]),
                         reads=[f"ps{pi}"], writes=xr("U", [m], [n]))
                proj_fm(wl, 0, 512, 8, H, "H", epi_u)
                if tile == 0 and l == 0:
                    dump("u", U.bitcast(F32), xr("U", range(4), range(NBLK)))
                if "s5" in branches:
                    latent5 = (tile == 1)
                    L5 = 1024 if latent5 else 256
                    nseq5 = T // L5
                    W = min(512, L5)
                    nm = L5 // W
                    TWO_PI5 = 2.0 * math.pi
                    MAGIC5 = 12582912.0
                    st["ps"] = 0
                    st["psn"] = 6
                    SPv = RSTD[:, :].rearrange("p (a b) -> p a b", a=16)
                    LRE, LIM, DT, TH, RM, CT, ST_, CR, CI, C512, S512, GIR, GII, T1, T2, T3 = range(16)

                    def c_(i, a=0, b=32):
                        return SPv[:, i, a:b]

                    def sp(fn):
                        P.op("dve", fn, reads=["RSTD", "H0"], writes=["RSTD"])

                    def spa(fn):
                        P.op("act", fn, reads=["RSTD", "cst"], writes=["RSTD"])

                    def range_reduce(src, dst, tmp):
                        sp(lambda e: e.tensor_scalar(c_(tmp), c_(src), 1.0 / TWO_PI5, MAGIC5, ALU.mult, ALU.add))
                        sp(lambda e: e.tensor_scalar(c_(tmp), c_(tmp), -MAGIC5, None, ALU.add))
                        sp(lambda e: e.scalar_tensor_tensor(c_(dst), c_(tmp), -TWO_PI5, c_(src), ALU.mult, ALU.add))
                        sp(lambda e: e.tensor_scalar(c_(dst), c_(dst), 3.141592, -3.141592, ALU.min, ALU.max))

                    def sincos(src_red, dsin, dcos):
                        spa(lambda e: e.activation(c_(dsin), c_(src_red), AF.Sin))
                        sp(lambda e: e.scalar_tensor_tensor(c_(src_red), c_(src_red), -1.0, c_(src_red), ALU.mult, ALU.max))
                        spa(lambda e: e.activation(c_(dcos), c_(src_red), AF.Sin, bias=cst[:, 4:5], scale=-1.0))

                    P.dma([(SPv[:, 0:3, :], s5sp_d[:, l, :, :])], "s5sp", writes=["RSTD"])
                    spa(lambda e: e.activation(c_(DT), c_(DT), AF.Exp))
                    sp(lambda e: e.tensor_tensor(c_(TH), c_(LIM), c_(DT), ALU.mult))
                    sp(lambda e: e.tensor_tensor(c_(T1), c_(LRE), c_(DT), ALU.mult))
                    spa(lambda e: e.activation(c_(RM), c_(T1), AF.Exp))
                    range_reduce(TH, T1, T2)
                    sincos(T1, ST_, CT)
                    sp(lambda e: e.tensor_tensor(c_(T1), c_(RM), c_(CT), ALU.mult))
                    sp(lambda e: e.tensor_tensor(c_(T2), c_(RM), c_(ST_), ALU.mult))
                    sp(lambda e: e.tensor_scalar(c_(T1), c_(T1), -1.0, None, ALU.add))
                    sp(lambda e: e.tensor_tensor(c_(T3), c_(LRE), c_(LRE), ALU.mult))
                    sp(lambda e: e.tensor_tensor(c_(CR), c_(LIM), c_(LIM), ALU.mult))
                    sp(lambda e: e.tensor_tensor(c_(T3), c_(T3), c_(CR), ALU.add))
                    sp(lambda e: e.reciprocal(c_(T3), c_(T3)))
                    sp(lambda e: e.tensor_tensor(c_(CR), c_(T1), c_(LRE), ALU.mult))
                    sp(lambda e: e.tensor_tensor(c_(CI), c_(T2), c_(LIM), ALU.mult))
                    sp(lambda e: e.tensor_tensor(c_(CR), c_(CR), c_(CI), ALU.add))
                    sp(lambda e: e.tensor_tensor(c_(CR), c_(CR), c_(T3), ALU.mult))
                    sp(lambda e: e.tensor_tensor(c_(CI), c_(T2), c_(LRE), ALU.mult))
                    sp(lambda e: e.tensor_tensor(c_(C512), c_(T1), c_(LIM), ALU.mult))
                    sp(lambda e: e.tensor_tensor(c_(CI), c_(CI), c_(C512), ALU.subtract))
                    sp(lambda e: e.tensor_tensor(c_(CI), c_(CI), c_(T3), ALU.mult))
                    if latent5:
                        sp(lambda e: e.tensor_scalar(c_(T3), c_(TH), 512.0, None, ALU.mult))
                        range_reduce(T3, T1, T2)
                        sincos(T1, S512, C512)
                        P.dma([(H0[:], s5h0_d[:, l, :, :])], "s5h0", writes=["H0"])
                        sp(lambda e: e.tensor_tensor(c_(T1), c_(CT), H0[:, :, 0], ALU.mult))
                        sp(lambda e: e.tensor_tensor(c_(T2), c_(ST_), H0[:, :, 1], ALU.mult))
                        sp(lambda e: e.tensor_tensor(c_(GIR), c_(T1), c_(T2), ALU.subtract))
                        sp(lambda e: e.tensor_tensor(c_(T1), c_(ST_), H0[:, :, 0], ALU.mult))
                        sp(lambda e: e.tensor_tensor(c_(T2), c_(CT), H0[:, :, 1], ALU.mult))
                        sp(lambda e: e.tensor_tensor(c_(GII), c_(T1), c_(T2), ALU.add))
                    BZw = BIG[:, 20:22, :].rearrange("p a t -> p (a t)")
                    BZ = BZw.bitcast(F32).rearrange("p (ri q c) -> p ri q c", ri=2, q=32)
                    RB = xr("B", [20, 21], range(NBLK)) if False else [f"B20_{n}" for n in range(NBLK)] + [f"B21_{n}" for n in range(NBLK)]
                    P.dma([(BZw, s5bz_d[l])], "s5bz", writes=RB, eng="pool")
                    BTw = BIG[:, 16:18, :].rearrange("p a t -> p (a t)").rearrange("p (r jq ri s) -> p r jq ri s", r=2, jq=4, ri=2)
                    RBT = [f"B16_{n}" for n in range(NBLK)] + [f"B17_{n}" for n in range(NBLK)]
                    for r in range(2):
                        crb = SPv[:, CR, 16 * r:16 * r + 16].unsqueeze(2).to_broadcast([128, 16, 32])
                        cib = SPv[:, CI, 16 * r:16 * r + 16].unsqueeze(2).to_broadcast([128, 16, 32])
                        bre = BZ[:, 0, 16 * r:16 * r + 16, :]
                        bim = BZ[:, 1, 16 * r:16 * r + 16, :]
                        tv = [TMP[i][:, :].rearrange("p (q c) -> p q c", q=16) for i in range(4)]
                        P.op("dve", lambda e: e.tensor_tensor(tv[0], bre, crb, ALU.mult), reads=RB + ["RSTD"], writes=["TMP0"])
                        P.op("dve", lambda e: e.tensor_tensor(tv[1], bim, cib, ALU.mult), reads=RB + ["RSTD"], writes=["TMP1"])
                        P.op("dve", lambda e: e.tensor_tensor(tv[0], tv[0], tv[1], ALU.subtract), reads=["TMP0", "TMP1"], writes=["TMP0"])
                        P.op("dve", lambda e: e.tensor_tensor(tv[2], bim, crb, ALU.mult), reads=RB + ["RSTD"], writes=["TMP2"])
                        P.op("dve", lambda e: e.tensor_tensor(tv[3], bre, cib, ALU.mult), reads=RB + ["RSTD"], writes=["TMP3"])
                        P.op("dve", lambda e: e.tensor_tensor(tv[2], tv[2], tv[3], ALU.add), reads=["TMP2", "TMP3"], writes=["TMP2"])
                        for ri, ti in ((0, 0), (1, 2)):
                            pt = next_ps()
                            for jq in range(4):
                                P.op("pe", lambda e, pt=pt, jq=jq, ti=ti: e.transpose(PSB[pt][:, jq * 128:(jq + 1) * 128], TMP[ti][:, jq * 128:(jq + 1) * 128], ident[:]),
                                     reads=[f"TMP{ti}", "ident"], writes=[f"ps{pt}"])
                            P.op("act", lambda e, pt=pt, r=r, ri=ri: e.copy(BTw[:, r, :, ri, :], PSB[pt][:].rearrange("p (jq s) -> p jq s", jq=4)),
                                 reads=[f"ps{pt}"], writes=RBT)
                    CZw = BIG[:, 18:20, :].rearrange("p a t -> p (a t)")
                    CZ = CZw.bitcast(F32).rearrange("p (q c) -> p q c", q=64)
                    RCZ = [f"B18_{n}" for n in range(NBLK)] + [f"B19_{n}" for n in range(NBLK)]
                    P.dma([(CZw, s5cz_d[l])], "s5cz", writes=RCZ, eng="pool")
                    P.op("dve", lambda e: e.memset(TMP[3][:], 0.0), writes=["TMP3"])
                    for i in range(2):
                        P.op("dve", lambda e, i=i: e.tensor_copy(PT[i][:], TMP[3][:]), reads=["TMP3"], writes=[f"PT{i}"])
                    P.dma([(TMP[0][:], tpos_d)], "tpos", writes=["TMP0"])
                    P.dma([(S0T[:, 0, :], ident_d)], "s0t", writes=["S0T"], eng="pool")
                    IDR = S0T[:, 0, :]
                    ZCv = [PT[jj // 2][:, (jj % 2) * 256:(jj % 2) * 256 + 256].rearrange("p (ri m) -> p ri m", ri=2) for jj in range(4)]
                    WB = {"ur": (OS[0], "OS0"), "ui": (OS[1], "OS1"),
                          "gr": (BIG[:, 20, 0:512], "B20_0"), "gi": (BIG[:, 20, 512:1024], "B20_1"),
                          "hr": (BIG[:, 21, 0:512], "B21_0"), "hi": (BIG[:, 21, 512:1024], "B21_1")}

                    def wb(k):
                        return WB[k][0][:, 0:W] if k in ("ur", "ui") else WB[k][0][:, 0:W]

                    st5i = [0]
                    s5it = [0]
                    s5pref = [False]
                    for jq in range(4):
                        accs = {("n", n): 6 + n for n in range(NBLK)}
                        started = set()
                        for jj in range(4):
                            j = 4 * jq + jj
                            psl = slice(jj * 32, (jj + 1) * 32)
                            for r in range(2):
                                rj = r * 16 + j
                                P.op("act", lambda e: e.copy(ZCv[jj][:, 0, psl], CZ[:, rj * 2, :]), reads=RCZ, writes=[f"PT{jj // 2}"])
                                P.op("act", lambda e: e.mul(ZCv[jj][:, 1, psl], CZ[:, rj * 2 + 1, :], -1.0), reads=RCZ, writes=[f"PT{jj // 2}"])
                                if jj == 3:
                                    for ri in range(2):
                                        P.op("act", lambda e, ri=ri: e.copy(BST[96:128, ri, :], BTw[96:128, r, jq, ri, :].bitcast(F32)), reads=RBT, writes=["BST"])
                                def gen_tab(rj_, c0, nm3, parts):
                                    cs_ = slice(c0, c0 + W)
                                    th_ = SPv[:, TH, rj_:rj_ + 1]
                                    n1_, n2_, n3_ = nm3
                                    if "A" in parts:
                                        P.op("dve", lambda e: e.tensor_scalar(TMP[3][:, cs_], TMP[0][:, 0:W], th_, None, ALU.mult), reads=["TMP0", "RSTD"], writes=[n3_])
                                        P.op("dve", lambda e: e.tensor_scalar(TMP[1][:, cs_], TMP[3][:, cs_], 1.0 / TWO_PI5, MAGIC5, ALU.mult, ALU.add), reads=[n3_], writes=[n1_])
                                        P.op("dve", lambda e: e.tensor_scalar(TMP[1][:, cs_], TMP[1][:, cs_], -MAGIC5, None, ALU.add), reads=[n1_], writes=[n1_])
                                        P.op("dve", lambda e: e.scalar_tensor_tensor(TMP[3][:, cs_], TMP[1][:, cs_], -TWO_PI5, TMP[3][:, cs_], ALU.mult, ALU.add), reads=[n1_, n3_], writes=[n3_])
                                        P.op("dve", lambda e: e.tensor_scalar(TMP[3][:, cs_], TMP[3][:, cs_], 3.141592, -3.141592, ALU.min, ALU.max), reads=[n3_], writes=[n3_])
                                        P.op("act", lambda e: e.activation(TMP[2][:, cs_], TMP[3][:, cs_], AF.Sin), reads=[n3_], writes=[n2_])
                                    if "B" in parts:
                                        P.op("dve", lambda e: e.scalar_tensor_tensor(TMP[3][:, cs_], TMP[3][:, cs_], -1.0, TMP[3][:, cs_], ALU.mult, ALU.max), reads=[n3_, n2_], writes=[n3_])
                                        P.op("act", lambda e: e.activation(TMP[1][:, cs_], TMP[3][:, cs_], AF.Sin, bias=cst[:, 4:5], scale=-1.0), reads=[n3_, "cst"], writes=[n1_])
                                        P.op("act", lambda e: e.mul(TMP[3][:, cs_], TMP[2][:, cs_], -1.0), reads=[n2_, n1_], writes=[n3_])

                                if latent5:
                                    hsel = 0
                                    NM3 = ("TMP1", "TMP2", "TMP3")
                                    gen_tab(rj, 0, NM3, "AB")
                                else:
                                    hsel = s5it[0] % 2
                                    NM3 = (f"TMP1h{hsel}", f"TMP2h{hsel}", f"TMP3h{hsel}")
                                    if not s5pref[0]:
                                        gen_tab(rj, hsel * 256, NM3, "AB")
                                    s5pref[0] = False
                                    s5it[0] += 1
                                    last_in_jq = (jj == 3 and r == 1)
                                    if r == 0:
                                        rj_next = 16 + j
                                    else:
                                        rj_next = j + 1
                                    NM3n = (f"TMP1h{1 - hsel}", f"TMP2h{1 - hsel}", f"TMP3h{1 - hsel}")
                                tcs = slice(hsel * W, hsel * W + W) if not latent5 else slice(0, W)
                                cosT5 = TMP[1][:, tcs]
                                sinT5 = TMP[2][:, tcs]
                                rmb = SPv[:, RM, rj:rj + 1].to_broadcast([128, W])
                                for pr in (range(2) if not latent5 else ()):
                                    tsl = slice(512 * pr, 512 * pr + 512)
                                    nblk = pr
                                    pre = next_ps()
                                    pim = next_ps()
                                    for ri, pp in ((0, pre), (1, pim)):
                                        if jj < 3:
                                            P.op("pe", lambda e, ri=ri, pp=pp: e.matmul(PSB[pp][:, 0:512], BTw[psl, r, jq, ri, :], U[psl, jq, tsl], start=True, stop=True),
                                                 reads=RBT + xr("U", [jq], [nblk]), writes=[f"ps{pp}"])
                                        else:
                                            P.op("pe", lambda e, ri=ri, pp=pp: e.matmul(PSB[pp][:, 0:512], BST[64:128, ri, :], U[64:128, jq, tsl], start=True, stop=True),
                                                 reads=["BST"] + xr("U", [jq], [nblk]), writes=[f"ps{pp}"])

                                    def v3(ap):
                                        return ap.rearrange("p (s t) -> p s t", s=2)

                                    def rv3(ap3):
                                        return ap3 if r == 0 else ap3[:, :, ::-1]
                                    bre3 = rv3(v3(PSB[pre][:, 0:512]))
                                    bim3 = rv3(v3(PSB[pim][:, 0:512]))
                                    c3 = cosT5.unsqueeze(1).to_broadcast([128, 2, 256])
                                    s3 = sinT5.unsqueeze(1).to_broadcast([128, 2, 256])
                                    ns3 = TMP[3][:, tcs].unsqueeze(1).to_broadcast([128, 2, 256])
                                    A0, A1 = OS[0][:, 0:512], OS[1][:, 0:512]
                                    B0, B1 = BIG[:, 20, 0:512], BIG[:, 20, 512:1024]
                                    gr, gi = BIG[:, 21, 0:512], BIG[:, 21, 512:1024]
                                    grf, gif = gr.bitcast(F32), gi.bitcast(F32)
                                    P.op("dve", lambda e: e.tensor_tensor(v3(A0), bre3, c3, ALU.mult), reads=[f"ps{pre}", NM3[0]], writes=["OS0"])
                                    P.op("dve", lambda e: e.tensor_tensor(v3(A1), bim3, s3, ALU.mult), reads=[f"ps{pim}", NM3[1]], writes=["OS1"])
                                    P.op("dve", lambda e: e.tensor_tensor(v3(B0), bim3, c3, ALU.mult), reads=[f"ps{pim}", NM3[0]], writes=["B20_0"])
                                    P.op("dve", lambda e: e.tensor_tensor(v3(B1), bre3, ns3, ALU.mult), reads=[f"ps{pre}", NM3[2]], writes=["B20_1"])
                                    sre = next_ps()
                                    sim = next_ps()
                                    P.op("pe", lambda e: e.matmul(PSB[sre][:, 0:512], IDR, A0, start=True, stop=False), reads=["S0T", "OS0"], writes=[f"ps{sre}"])
                                    P.op("pe", lambda e: e.matmul(PSB[sre][:, 0:512], IDR, A1, start=False, stop=True), reads=["S0T", "OS1"], writes=[f"ps{sre}"])
                                    P.op("pe", lambda e: e.matmul(PSB[sim][:, 0:512], IDR, B0, start=True, stop=False), reads=["S0T", "B20_0"], writes=[f"ps{sim}"])
                                    P.op("pe", lambda e: e.matmul(PSB[sim][:, 0:512], IDR, B1, start=False, stop=True), reads=["S0T", "B20_1"], writes=[f"ps{sim}"])
                                    if not last_in_jq:
                                        gen_tab(rj_next, (1 - hsel) * 256, NM3n, "A" if pr == 0 else "B")
                                        if pr == 1:
                                            s5pref[0] = True
                                    for sl in range(2):
                                        cs = slice(sl * 256, (sl + 1) * 256)
                                        P.op("dve", lambda e: e.tensor_tensor_scan(gr[:, cs], rmb, PSB[sre][:, cs], 0.0, ALU.mult, ALU.add), reads=[f"ps{sre}", "RSTD"], writes=["B21_0"])
                                        P.op("dve", lambda e: e.tensor_tensor_scan(gi[:, cs], rmb, PSB[sim][:, cs], 0.0, ALU.mult, ALU.add), reads=[f"ps{sim}", "RSTD"], writes=["B21_1"])
                                    P.op("dve", lambda e: e.tensor_tensor(rv3(v3(A0)), v3(grf), c3, ALU.mult), reads=["B21_0", NM3[0]], writes=["OS0"])
                                    P.op("dve", lambda e: e.tensor_tensor(rv3(v3(A1)), v3(gif), ns3, ALU.mult), reads=["B21_1", NM3[2]], writes=["OS1"])
                                    P.op("dve", lambda e: e.tensor_tensor(rv3(v3(B0)), v3(grf), s3, ALU.mult), reads=["B21_0", NM3[1]], writes=["B20_0"])
                                    P.op("dve", lambda e: e.tensor_tensor(rv3(v3(B1)), v3(gif), c3, ALU.mult), reads=["B21_1", NM3[0]], writes=["B20_1"])
                                    for sl in range(2):
                                        sq = 2 * pr + sl
                                        slot = st5i[0] % 8
                                        st5i[0] += 1
                                        lastc = sl * 256 + (255 if r == 0 else 0)
                                        P.op("dve", lambda e: e.tensor_tensor(ST5[:, slot, 0:1], A0.bitcast(F32)[:, lastc:lastc + 1], A1.bitcast(F32)[:, lastc:lastc + 1], ALU.add),
                                             reads=["OS0", "OS1"], writes=[f"ST5_{slot}"])
                                        P.op("dve", lambda e: e.tensor_tensor(ST5[:, slot, 1:2], B0.bitcast(F32)[:, lastc:lastc + 1], B1.bitcast(F32)[:, lastc:lastc + 1], ALU.add),
                                             reads=["B20_0", "B20_1"], writes=[f"ST5_{slot}"])
                                        dst = ns5_d[sq, l, r].rearrange("(j gl) p c -> j (gl p) c", gl=2)[j]
                                        P.dma([(dst, ST5[:, slot, :])], f"st5_{slot}", reads=[f"ST5_{slot}"])
                                    key = ("n", nblk)
                                    ab = accs[key]
                                    for qi, (hb, hn, ri) in enumerate(((A0, "OS0", 0), (A1, "OS1", 0), (B0, "B20_0", 1), (B1, "B20_1", 1))):
                                        is_first = (key not in started) and qi == 0
                                        is_last = (jj == 3 and r == 1 and qi == 3)
                                        P.op("pe", lambda e, ri=ri, hb=hb, is_first=is_first, is_last=is_last: e.matmul(PSB[ab][:, 0:512], ZCv[jj][:, ri, :], hb, start=is_first, stop=is_last),
                                             reads=[f"PT{jj // 2}", hn], writes=[f"ps{ab}"])
                                    started.add(key)
                                for sq in (range(nseq5) if latent5 else ()):
                                    s0 = sq * L5
                                    for m in range(nm):
                                        if m == 1:
                                            c5 = SPv[:, C512, rj:rj + 1]
                                            s5_ = SPv[:, S512, rj:rj + 1]
                                            P.op("dve", lambda e: e.tensor_scalar(TMP[3][:, 0:W], sinT5, s5_, None, ALU.mult), reads=["TMP2", "RSTD"], writes=["TMP3"])
                                            P.op("dve", lambda e: e.scalar_tensor_tensor(TMP[3][:, 0:W], cosT5, c5, TMP[3][:, 0:W], ALU.mult, ALU.subtract), reads=["TMP1", "TMP3", "RSTD"], writes=["TMP3"])
                                            P.op("dve", lambda e: e.tensor_scalar(sinT5, sinT5, c5, None, ALU.mult), reads=["TMP2", "RSTD"], writes=["TMP2"])
                                            P.op("dve", lambda e: e.scalar_tensor_tensor(sinT5, cosT5, s5_, sinT5, ALU.mult, ALU.add), reads=["TMP1", "TMP2", "RSTD"], writes=["TMP2"])
                                            P.op("dve", lambda e: e.tensor_copy(cosT5, TMP[3][:, 0:W]), reads=["TMP3"], writes=["TMP1"])
                                            P.op("act", lambda e: e.mul(TMP[3][:, 0:W], sinT5, -1.0), reads=["TMP2"], writes=["TMP3"])
                                        if r == 0:
                                            lo = s0 + m * W
                                        else:
                                            lo = s0 + L5 - (m + 1) * W
                                        tsl = slice(lo, lo + W)
                                        nblk = lo // 512
                                        pre = next_ps()
                                        pim = next_ps()
                                        for ri, pp in ((0, pre), (1, pim)):
                                            if jj < 3:
                                                P.op("pe", lambda e, ri=ri, pp=pp: e.matmul(PSB[pp][:, 0:W], BTw[psl, r, jq, ri, :], U[psl, jq, tsl], start=True, stop=True),
                                                     reads=RBT + xr("U", [jq], [nblk]), writes=[f"ps{pp}"])
                                            else:
                                                P.op("pe", lambda e, ri=ri, pp=pp: e.matmul(PSB[pp][:, 0:W], BST[64:128, ri, :], U[64:128, jq, tsl], start=True, stop=True),
                                                     reads=["BST"] + xr("U", [jq], [nblk]), writes=[f"ps{pp}"])
                                        bre_ = PSB[pre][:, 0:W]
                                        bim_ = PSB[pim][:, 0:W]
                                        if r == 1:
                                            bre_ = bre_[:, ::-1]
                                            bim_ = bim_[:, ::-1]
                                        nsinT5 = TMP[3][:, 0:W]
                                        A0, A1 = OS[0][:, 0:W], OS[1][:, 0:W]
                                        B0, B1 = BIG[:, 20, 0:W], BIG[:, 20, 512:512 + W]
                                        gr, gi = BIG[:, 21, 0:W], BIG[:, 21, 512:512 + W]
                                        grf, gif = gr.bitcast(F32), gi.bitcast(F32)
                                        P.op("dve", lambda e: e.tensor_tensor(A0, bre_, cosT5, ALU.mult), reads=[f"ps{pre}", "TMP1"], writes=["OS0"])
                                        P.op("dve", lambda e: e.tensor_tensor(A1, bim_, sinT5, ALU.mult), reads=[f"ps{pim}", "TMP2"], writes=["OS1"])
                                        P.op("dve", lambda e: e.tensor_tensor(B0, bim_, cosT5, ALU.mult), reads=[f"ps{pim}", "TMP1"], writes=["B20_0"])
                                        P.op("dve", lambda e: e.tensor_tensor(B1, bre_, nsinT5, ALU.mult), reads=[f"ps{pre}", "TMP3"], writes=["B20_1"])
                                        sre = next_ps()
                                        sim = next_ps()
                                        P.op("pe", lambda e: e.matmul(PSB[sre][:, 0:W], IDR, A0, start=True, stop=False), reads=["S0T", "OS0"], writes=[f"ps{sre}"])
                                        P.op("pe", lambda e: e.matmul(PSB[sre][:, 0:W], IDR, A1, start=False, stop=True), reads=["S0T", "OS1"], writes=[f"ps{sre}"])
                                        P.op("pe", lambda e: e.matmul(PSB[sim][:, 0:W], IDR, B0, start=True, stop=False), reads=["S0T", "B20_0"], writes=[f"ps{sim}"])
                                        P.op("pe", lambda e: e.matmul(PSB[sim][:, 0:W], IDR, B1, start=False, stop=True), reads=["S0T", "B20_1"], writes=[f"ps{sim}"])
                                        if m == 0:
                                            inir = SPv[:, GIR, rj:rj + 1] if latent5 else 0.0
                                            inii = SPv[:, GII, rj:rj + 1] if latent5 else 0.0
                                        else:
                                            inir = SPv[:, T1, 0:1]
                                            inii = SPv[:, T2, 0:1]
                                        P.op("dve", lambda e: e.tensor_tensor_scan(gr, rmb, PSB[sre][:, 0:W], inir, ALU.mult, ALU.add), reads=[f"ps{sre}", "RSTD"], writes=["B21_0"])
                                        P.op("dve", lambda e: e.tensor_tensor_scan(gi, rmb, PSB[sim][:, 0:W], inii, ALU.mult, ALU.add), reads=[f"ps{sim}", "RSTD"], writes=["B21_1"])
                                        if m < nm - 1:
                                            P.op("dve", lambda e: e.tensor_copy(SPv[:, T1, 0:1], grf[:, W - 1:W]), reads=["B21_0"], writes=["RSTD"])
                                            P.op("dve", lambda e: e.tensor_copy(SPv[:, T2, 0:1], gif[:, W - 1:W]), reads=["B21_1"], writes=["RSTD"])
                                        rv = (lambda ap: ap) if r == 0 else (lambda ap: ap[:, ::-1])
                                        P.op("dve", lambda e: e.tensor_tensor(rv(A0), grf, cosT5, ALU.mult), reads=["B21_0", "TMP1"], writes=["OS0"])
                                        P.op("dve", lambda e: e.tensor_tensor(rv(A1), gif, nsinT5, ALU.mult), reads=["B21_1", "TMP3"], writes=["OS1"])
                                        P.op("dve", lambda e: e.tensor_tensor(rv(B0), grf, sinT5, ALU.mult), reads=["B21_0", "TMP2"], writes=["B20_0"])
                                        P.op("dve", lambda e: e.tensor_tensor(rv(B1), gif, cosT5, ALU.mult), reads=["B21_1", "TMP1"], writes=["B20_1"])
                                        if (not latent5) and m == nm - 1:
                                            slot = st5i[0] % 8
                                            st5i[0] += 1
                                            lastc = (W - 1) if r == 0 else 0
                                            P.op("dve", lambda e: e.tensor_tensor(ST5[:, slot, 0:1], A0.bitcast(F32)[:, lastc:lastc + 1], A1.bitcast(F32)[:, lastc:lastc + 1], ALU.add),
                                                 reads=["OS0", "OS1"], writes=[f"ST5_{slot}"])
                                            P.op("dve", lambda e: e.tensor_tensor(ST5[:, slot, 1:2], B0.bitcast(F32)[:, lastc:lastc + 1], B1.bitcast(F32)[:, lastc:lastc + 1], ALU.add),
                                                 reads=["B20_0", "B20_1"], writes=[f"ST5_{slot}"])
                                            dst = ns5_d[sq, l, r].rearrange("(j gl) p c -> j (gl p) c", gl=2)[j]
                                            P.dma([(dst, ST5[:, slot, :])], f"st5_{slot}", reads=[f"ST5_{slot}"])
                                        key = ("n", nblk) if latent5 else ("s", sq)
                                        ab = accs[key]
                                        for qi, (hb, hn, ri) in enumerate(((A0, "OS0", 0), (A1, "OS1", 0), (B0, "B20_0", 1), (B1, "B20_1", 1))):
                                            is_first = (key not in started) and qi == 0
                                            is_last = (jj == 3 and r == 1 and qi == 3)
                                            P.op("pe", lambda e, ri=ri, hb=hb, is_first=is_first, is_last=is_last: e.matmul(PSB[ab][:, 0:W], ZCv[jj][:, ri, :], hb, start=is_first, stop=is_last),
                                                 reads=[f"PT{jj // 2}", hn], writes=[f"ps{ab}"])
                                        started.add(key)
                        dd = S5D[:, l, jq:jq + 1]
                        for n in range(NBLK):
                            t3 = TMP[3][:]
                            if True:
                                P.op("dve", lambda e: e.scalar_tensor_tensor(t3, U[:, jq, BLK[n]].bitcast(F32), dd, PSB[6 + n][:], ALU.mult, ALU.add),
                                     reads=xr("U", [jq], [n]) + [f"ps{6 + n}", "S5D"], writes=["TMP3"])
                            else:
                                for hf in range(2):
                                    sq = 2 * n + hf
                                    P.op("dve", lambda e: e.scalar_tensor_tensor(TMP[3][:, hf * 256:(hf + 1) * 256], U[:, jq, sq * 256:(sq + 1) * 256].bitcast(F32), dd,
                                                                                 PSB[4 + sq][:, 0:256], ALU.mult, ALU.add),
                                         reads=xr("U", [jq], [n]) + [f"ps{4 + sq}", "S5D"], writes=["TMP3"])
                            o0 = OS[0][:]
                            o0f = o0.bitcast(F32)
                            P.op("dve", lambda e: e.tensor_tensor(o0, t3, t3, ALU.mult), reads=["TMP3"], writes=["OS0"])
                            P.op("dve", lambda e: e.tensor_scalar(o0, o0f, 0.044715 * 1.5957691216, 1.5957691216, ALU.mult, ALU.add), reads=["OS0"], writes=["OS0"])
                            P.op("dve", lambda e: e.tensor_tensor(o0, o0f, t3, ALU.mult), reads=["OS0", "TMP3"], writes=["OS0"])
                            P.op("act", lambda e: e.activation(o0, o0f, AF.Sigmoid), reads=["OS0"], writes=["OS0"])
                            P.op("dve", lambda e: e.tensor_tensor(Y[:, jq, BLK[n]], o0f, t3, ALU.mult), reads=["OS0", "TMP3"], writes=xr("Y", [jq], [n]))
                    st["psn"] = 6
                    st["ps"] = 0
                    if tile == 0 and l == 0:
                        dump("y5", Y.bitcast(F32), xr("Y", range(4), range(NBLK)))
                else:
                    for m in range(4):
                        for n in range(NBLK):
                            t0 = next_tmp()
                            P.op("dve", lambda e, t0=t0, m=m, n=n: e.tensor_tensor(TMP[t0][:], U[:, m, BLK[n]].bitcast(F32), U[:, m, BLK[n]].bitcast(F32), ALU.mult),
                                 reads=xr("U", [m], [n]), writes=[f"TMP{t0}"])
                            P.op("dve", lambda e, t0=t0: e.tensor_scalar(TMP[t0][:], TMP[t0][:], 0.044715 * 1.5957691216, 1.5957691216, ALU.mult, ALU.add),
                                 reads=[f"TMP{t0}"], writes=[f"TMP{t0}"])
                            P.op("dve", lambda e, t0=t0, m=m, n=n: e.tensor_tensor(TMP[t0][:], TMP[t0][:], U[:, m, BLK[n]].bitcast(F32), ALU.mult),
                                 reads=[f"TMP{t0}"] + xr("U", [m], [n]), writes=[f"TMP{t0}"])
                            P.op("act", lambda e, t0=t0: e.activation(TMP[t0][:], TMP[t0][:], AF.Sigmoid), reads=[f"TMP{t0}"], writes=[f"TMP{t0}"])
                            P.op("dve", lambda e, t0=t0, m=m, n=n: e.tensor_tensor(Y[:, m, BLK[n]], TMP[t0][:], U[:, m, BLK[n]].bitcast(F32), ALU.mult),
                                 reads=[f"TMP{t0}"] + xr("U", [m], [n]), writes=xr("Y", [m], [n]))
                branch_out(w_s5_glu[l], 4, Y, "Y", 0, nb == 0, True)
                nb += 1
            if "ret" in branches:
                latent = (tile == 1)
                L = 1024 if latent else 256
                RET = SCR[:, 12:16, :]
                RQ, RK, RQ0, RV, RG, RT = 16, 17, 18, 19, 20, 21

                def rr(r, ns=range(NBLK)):
                    return [f"B{r}_{n}" for n in ns]
                Qrow = BIG[:, RQ, :]
                Krow = BIG[:, RK, :]
                Q0row = BIG[:, RQ0, :]
                Vrow = BIG[:, RV, :].rearrange("p (c e) -> p c e", c=8)
                Grow = BIG[:, RG, :].bitcast(F32)
                Trow = BIG[:, RT, :]
                RtabW = BIG[:, 8:10, :].rearrange("p a t -> p (a t)")
                Rtab = RtabW.bitcast(F32)
                GrowW = BIG[:, RG, :]
                RTR = rr(8) + rr(9)
                cosT = BIG[:, 10, :].bitcast(F32)
                sinT = BIG[:, 11, :].bitcast(F32)
                if latent:
                    P.dma([(BIG[:, 10, :], rope_d[0]), (BIG[:, 11, :], rope_d[1])], "rope", writes=rr(10) + rr(11), eng="pool")

                def pj(wv_, n):
                    pi = next_ps()
                    for k in R8:
                        P.op("pe", lambda e, pi=pi, wv_=wv_, k=k, n=n: e.matmul(PSB[pi][:], wv_[:, k, :], H[:, k, BLK[n]],
                                                                              start=(k == 0), stop=(k == 7)),
                             reads=[f"WS{s}"] + xr("H", [k], [n]), writes=[f"ps{pi}"])
                    return pi

                def rope(src_row, src_r, dst_row, dst_r, n):
                    pi = next_ps()
                    P.op("pe", lambda e, pi=pi, n=n: e.matmul(PSB[pi][:], perm[:], src_row[:, BLK[n]], start=True, stop=True),
                         reads=["perm", f"B{src_r}_{n}"], writes=[f"ps{pi}"])
                    ta = next_tmp()
                    P.op("dve", lambda e, ta=ta, n=n: e.tensor_tensor(TMP[ta][:], src_row[:, BLK[n]].bitcast(F32), cosT[:, BLK[n]], ALU.mult),
                         reads=[f"B{src_r}_{n}", f"B10_{n}"], writes=[f"TMP{ta}"])
                    tb = next_tmp()
                    P.op("dve", lambda e, tb=tb, pi=pi, n=n: e.tensor_tensor(TMP[tb][:], PSB[pi][:], sinT[:, BLK[n]], ALU.mult),
                         reads=[f"ps{pi}", f"B11_{n}"], writes=[f"TMP{tb}"])
                    P.op("dve", lambda e, ta=ta, tb=tb, n=n: e.tensor_tensor(dst_row[:, BLK[n]], TMP[ta][:], TMP[tb][:], ALU.add),
                         reads=[f"TMP{ta}", f"TMP{tb}"], writes=[f"B{dst_r}_{n}"])

                acc_i = [0]
                for h in range(4):
                    lgi = l * 8 + h
                    lgf = LG[:, lgi:lgi + 1]
                    lgb = LG[:, lgi + 4:lgi + 5]
                    nlgb = NLG[:, lgi + 4:lgi + 5]
                    s = load_w(lambda w, h=h: [(w[:, i * 1024:(i + 1) * 1024].rearrange("p (k n) -> p k n", k=8),
                                                wl[:, c0 + h * 128:c0 + (h + 1) * 128].rearrange("(k p) n -> p k n", p=128))
                                               for i, c0 in enumerate((512, 1024, 1536, 2048))])
                    wq, wk, wv, wg = [wview(s, 8, 128, i * 1024) for i in range(4)]
                    for n in range(NBLK):
                        pi = pj(wq, n)
                        if latent:
                            P.op("act", lambda e, pi=pi, n=n: e.copy(Q0row[:, BLK[n]], PSB[pi][:]), reads=[f"ps{pi}"], writes=[f"B{RQ0}_{n}"])
                            rope(Q0row, RQ0, Qrow, RQ, n)
                        else:
                            P.op("act", lambda e, pi=pi, n=n: e.copy(Qrow[:, BLK[n]], PSB[pi][:]), reads=[f"ps{pi}"], writes=[f"B{RQ}_{n}"])
                    for n in range(NBLK):
                        pi = pj(wk, n)
                        if latent:
                            P.op("act", lambda e, pi=pi, n=n: e.mul(Trow[:, BLK[n]], PSB[pi][:], 128.0 ** -0.5), reads=[f"ps{pi}"], writes=[f"B{RT}_{n}"])
                            rope(Trow, RT, Krow, RK, n)
                        else:
                            P.op("act", lambda e, pi=pi, n=n: e.mul(Krow[:, BLK[n]], PSB[pi][:], 128.0 ** -0.5), reads=[f"ps{pi}"], writes=[f"B{RK}_{n}"])
                    for n in range(NBLK):
                        pi = pj(wv, n)
                        P.op("act", lambda e, pi=pi, n=n: e.copy(Trow[:, BLK[n]], PSB[pi][:]), reads=[f"ps{pi}"], writes=[f"B{RT}_{n}"])
                        pt = next_ps()
                        for c in range(4):
                            P.op("pe", lambda e, pt=pt, c=c, n=n: e.transpose(PSB[pt][:, c * 128:(c + 1) * 128],
                                                                             Trow[:, n * 512 + c * 128:n * 512 + (c + 1) * 128].bitcast(F32), ident[:]),
                                 reads=[f"B{RT}_{n}", "ident"], writes=[f"ps{pt}"])
                        P.op("act", lambda e, pt=pt, n=n: e.copy(BIG[:, RV, BLK[n]], PSB[pt][:]), reads=[f"ps{pt}"], writes=[f"B{RV}_{n}"])
                    for n in range(NBLK):
                        pi = pj(wg, n)
                        P.op("act", lambda e, pi=pi, n=n: e.activation(GrowW[:, BLK[n]], PSB[pi][:], AF.Silu), reads=[f"ps{pi}"], writes=[f"B{RG}_{n}"])
                    P.dma([(BIG[:, 8:10, :].rearrange("p a t -> p (a t)")[:, 0:2 * L], ramp_d[1 if latent else 0])], "ramp", writes=RTR, eng="pool")
                    for pc in range(2 * L // 512):
                        ta = next_tmp()
                        sl = slice(pc * 512, (pc + 1) * 512)
                        P.op("act", lambda e, ta=ta, sl=sl, nlgb=nlgb: e.activation(TMP[ta][:], Rtab[:, sl], AF.Exp, scale=nlgb),
                             reads=RTR + ["NLG"], writes=[f"TMP{ta}"])
                        P.op("act", lambda e, sl=sl, lgf=lgf: e.activation(RtabW[:, sl], Rtab[:, sl], AF.Exp, scale=lgf),
                             reads=RTR + ["LG"], writes=RTR)
                        P.op("dve", lambda e, ta=ta, sl=sl: e.tensor_tensor(RtabW[:, sl], Rtab[:, sl], TMP[ta][:], ALU.min),
                             reads=RTR + [f"TMP{ta}"], writes=RTR)
                    if latent:
                        P.dma([(S0T[:, d, :], sret_d[l, d, h]) for d in range(2)], "s0t", writes=["S0T"], eng="pool")
                    if not latent:
                        P.op("act", lambda e, lgf=lgf: e.activation(KD[:, 0, :], posT[:, 0, :], AF.Exp, scale=lgf), reads=["posT", "LG"], writes=["KD"])
                        P.op("act", lambda e, lgb=lgb: e.activation(KD[:, 1, :], posT[:, 1, :], AF.Exp, scale=lgb), reads=["posT", "LG"], writes=["KD"])
                        def st_stages(sq, b):
                            n = sq // 2
                            stt = {}
                            KTSb = PT[b][:, :].rearrange("p (d c e) -> p d c e", d=2, c=2)

                            def s0():
                                stt["pt"] = next_ps()
                                for cl in range(2):
                                    c = sq * 2 + cl
                                    P.op("pe", lambda e: e.transpose(PSB[stt["pt"]][:, cl * 128:(cl + 1) * 128], Krow[:, c * 128:(c + 1) * 128].bitcast(F32), ident[:]),
                                         reads=[f"B{RK}_{n}", "ident"], writes=[f"ps{stt['pt']}"])

                            def s1():
                                for d in range(2):
                                    for cl in range(2):
                                        P.op("dve", lambda e: e.tensor_scalar(KTSb[:, d, cl, :], PSB[stt["pt"]][:, cl * 128:(cl + 1) * 128], KD[:, d, cl:cl + 1], None, ALU.mult),
                                             reads=[f"ps{stt['pt']}", "KD"], writes=[f"PT{b}"])

                            def s2():
                                for d in range(2):
                                    stt[("ps", d)] = next_ps()
                                    for cl in range(2):
                                        c = sq * 2 + cl
                                        P.op("pe", lambda e: e.matmul(PSB[stt[("ps", d)]][:, 0:128], KTSb[:, d, cl, :], Vrow[:, c, :], start=(cl == 0), stop=(cl == 1)),
                                             reads=[f"PT{b}", f"B{RV}_{n}"], writes=[f"ps{stt[('ps', d)]}"])

                            def s3():
                                for d in range(2):
                                    so = 2 * b + d
                                    P.op("act", lambda e: e.copy(TMP[so][:, 0:128], PSB[stt[("ps", d)]][:, 0:128]), reads=[f"ps{stt[('ps', d)]}"], writes=[f"TMP{so}"])
                                    P.dma([(nsret_d[sq, l, d, h], TMP[so][:, 0:128])], f"sto{so}", reads=[f"TMP{so}"])
                            return [s0, s1, s2, s3]

                        for pr_ in range(2):
                            for fa, fb in zip(st_stages(2 * pr_, 0), st_stages(2 * pr_ + 1, 1)):
                                fa()
                                fb()
                    if latent:
                        groups = [(0, 512, list(range(8)), 0), (512, 512, list(range(8)), 0)]
                    else:
                        groups = []

                        def grp_stages(sq, b):
                            i0 = sq * 256
                            n = i0 // 512
                            jcs = [2 * sq, 2 * sq + 1]
                            acc = 6 + b
                            ta, tb = 2 * b, 2 * b + 1
                            stt = {}
                            TA = TMP[ta][:, 0:256]
                            TB = TMP[tb][:, 0:256]
                            O1 = OS[b][:, 0:256]
                            O2 = OS[b][:, 256:512]

                            def s0():
                                stt["ps"] = next_ps()
                                for ji, jc in enumerate(jcs):
                                    P.op("pe", lambda e: e.matmul(PSB[stt["ps"]][:, ji * 256:(ji + 1) * 256], Krow[:, jc * 128:(jc + 1) * 128], Qrow[:, i0:i0 + 256],
                                                                  start=True, stop=True), reads=[f"B{RK}_{n}", f"B{RQ}_{n}"], writes=[f"ps{stt['ps']}"])

                            def s1():
                                for ji, jc in enumerate(jcs):
                                    off = i0 - jc * 128 + L
                                    P.op("dve", lambda e: e.tensor_tensor(PT[b][:, ji * 256:(ji + 1) * 256], PSB[stt["ps"]][:, ji * 256:(ji + 1) * 256], Rtab[:, off:off + 256], ALU.mult),
                                         reads=[f"ps{stt['ps']}"] + RTR, writes=[f"PT{b}"])

                            def s2():
                                for ji, jc in enumerate(jcs):
                                    P.op("pe", lambda e: e.matmul(PSB[acc][:, 0:256], Vrow[:, jc, :], PT[b][:, ji * 256:(ji + 1) * 256], start=(ji == 0), stop=(ji == 1)),
                                         reads=[f"B{RV}_{n}", f"PT{b}"], writes=[f"ps{acc}"])

                            def s3():
                                P.op("act", lambda e: e.copy(O1, PSB[acc][:, 0:256]), reads=[f"ps{acc}"], writes=[f"OS{b}"])
                                P.op("act", lambda e: e.activation(O2, PSB[acc][:, 0:256], AF.Square), reads=[f"ps{acc}"], writes=[f"OS{b}"])

                            def s4():
                                stt["pm"] = next_ps()
                                P.op("pe", lambda e: e.matmul(PSB[stt["pm"]][:, 0:256], ones[:], O1, start=True, stop=True), reads=["ones", f"OS{b}"], writes=[f"ps{stt['pm']}"])
                                P.op("pe", lambda e: e.matmul(PSB[stt["pm"]][:, 256:512], ones[:], O2, start=True, stop=True), reads=["ones", f"OS{b}"], writes=[f"ps{stt['pm']}"])

                            def s5():
                                P.op("dve", lambda e: e.tensor_scalar(TA, PSB[stt["pm"]][:, 0:256], 1.0 / 128, None, ALU.mult), reads=[f"ps{stt['pm']}"], writes=[f"TMP{ta}"])

                            def s6():
                                P.op("dve", lambda e: e.tensor_tensor(TB, TA, TA, ALU.mult), reads=[f"TMP{ta}"], writes=[f"TMP{tb}"])

                            def s7():
                                P.op("dve", lambda e: e.scalar_tensor_tensor(TB, PSB[stt["pm"]][:, 256:512], 1.0 / 128, TB, ALU.mult, ALU.subtract),
                                     reads=[f"ps{stt['pm']}", f"TMP{tb}"], writes=[f"TMP{tb}"])

                            def s8():
                                P.op("act", lambda e: e.activation(TB, TB, AF.Sqrt, bias=cst[:, 1:2], scale=1.0), reads=[f"TMP{tb}", "cst"], writes=[f"TMP{tb}"])

                            def s9():
                                P.op("dve", lambda e: e.reciprocal(TB, TB), reads=[f"TMP{tb}"], writes=[f"TMP{tb}"])

                            def s10():
                                P.op("dve", lambda e: e.tensor_tensor(TA, O1.bitcast(F32), TA, ALU.subtract), reads=[f"OS{b}", f"TMP{ta}"], writes=[f"TMP{ta}"])

                            def s11():
                                P.op("dve", lambda e: e.tensor_tensor(TA, TA, TB, ALU.mult), reads=[f"TMP{ta}", f"TMP{tb}"], writes=[f"TMP{ta}"])

                            def s12():
                                P.op("dve", lambda e: e.tensor_tensor(RET[:, h, i0:i0 + 256], TA, Grow[:, i0:i0 + 256], ALU.mult),
                                     reads=[f"TMP{ta}", f"B{RG}_{n}"], writes=xr("Y", [h], [n]))
                            return [s0, s1, s2, s3, s4, s5, s6, s7, s8, s9, s10, s11, s12]

                        for pr_ in range(2):
                            sa = grp_stages(2 * pr_, 0)
                            sb_ = grp_stages(2 * pr_ + 1, 1)
                            for fa, fb in zip(sa, sb_):
                                fa()
                                fb()
                    deferred_norms = []
                    for (i0, N, jcs, seq0) in groups:
                        n = i0 // 512
                        acc = 6 + (acc_i[0] % 2)
                        acc_i[0] += 1
                        for ji, jc in enumerate(jcs):
                            nj = (jc * 128) // 512
                            pi = next_ps()
                            P.op("pe", lambda e, pi=pi, jc=jc, i0=i0, N=N: e.matmul(PSB[pi][:, 0:N], Krow[:, jc * 128:(jc + 1) * 128], Qrow[:, i0:i0 + N],
                                                                                   start=True, stop=True),
                                 reads=[f"B{RK}_{nj}", f"B{RQ}_{n}"], writes=[f"ps{pi}"])
                            off = (i0 - seq0) - (jc * 128 - seq0) + L
                            pt_i = (acc_i[0] + ji) % 2
                            P.op("dve", lambda e, pi=pi, pt_i=pt_i, off=off, N=N: e.tensor_tensor(PT[pt_i][:, 0:N], PSB[pi][:, 0:N], Rtab[:, off:off + N], ALU.mult),
                                 reads=[f"ps{pi}"] + RTR, writes=[f"PT{pt_i}"])
                            last = (ji == len(jcs) - 1) and not latent
                            P.op("pe", lambda e, acc=acc, pt_i=pt_i, jc=jc, N=N, ji=ji, last=last: e.matmul(PSB[acc][:, 0:N], Vrow[:, jc, :], PT[pt_i][:, 0:N],
                                                                                                         start=(ji == 0), stop=last),
                                 reads=[f"B{RV}_{nj}", f"PT{pt_i}"], writes=[f"ps{acc}"])
                        if latent:
                            for d in range(2):
                                pb = next_tmp()
                                P.dma([(TMP[pb][:], pos_d[d][:, i0:i0 + N])], f"posb{pb}", writes=[f"TMP{pb}"])
                                lgd = lgf if d == 0 else lgb
                                P.op("act", lambda e, lgd=lgd, pb=pb: e.activation(TMP[pb][:], TMP[pb][:], AF.Exp, scale=lgd), reads=[f"TMP{pb}", "LG"], writes=[f"TMP{pb}"])
                                pt_i = d
                                P.op("dve", lambda e, pt_i=pt_i, i0=i0, N=N, pb=pb: e.tensor_tensor(PT[pt_i][:, 0:N], Q0row[:, i0:i0 + N].bitcast(F32), TMP[pb][:, 0:N], ALU.mult),
                                     reads=[f"B{RQ0}_{n}", f"TMP{pb}"], writes=[f"PT{pt_i}"])
                                P.op("pe", lambda e, acc=acc, pt_i=pt_i, d=d, N=N: e.matmul(PSB[acc][:, 0:N], S0T[:, d, :], PT[pt_i][:, 0:N],
                                                                                        start=False, stop=(d == 1)),
                                     reads=["S0T", f"PT{pt_i}"], writes=[f"ps{acc}"])
                        def _norm_part(acc=acc, i0=i0, N=N, n=n):
                            P.op("act", lambda e, acc=acc, N=N: e.copy(OS[0][:, 0:N], PSB[acc][:, 0:N]), reads=[f"ps{acc}"], writes=["OS0"])
                            P.op("act", lambda e, acc=acc, N=N: e.activation(OS[1][:, 0:N], PSB[acc][:, 0:N], AF.Square), reads=[f"ps{acc}"], writes=["OS1"])
                            p1 = next_ps()
                            P.op("pe", lambda e, p1=p1, N=N: e.matmul(PSB[p1][:, 0:N], ones[:], OS[0][:, 0:N], start=True, stop=True),
                                 reads=["ones", "OS0"], writes=[f"ps{p1}"])
                            p2 = next_ps()
                            P.op("pe", lambda e, p2=p2, N=N: e.matmul(PSB[p2][:, 0:N], ones[:], OS[1][:, 0:N], start=True, stop=True),
                                 reads=["ones", "OS1"], writes=[f"ps{p2}"])
                            ta = next_tmp()
                            tb = next_tmp()
                            P.op("dve", lambda e, ta=ta, p1=p1, N=N: e.tensor_scalar(TMP[ta][:, 0:N], PSB[p1][:, 0:N], 1.0 / 128, None, ALU.mult),
                                 reads=[f"ps{p1}"], writes=[f"TMP{ta}"])
                            P.op("dve", lambda e, ta=ta, tb=tb, N=N: e.tensor_tensor(TMP[tb][:, 0:N], TMP[ta][:, 0:N], TMP[ta][:, 0:N], ALU.mult),
                                 reads=[f"TMP{ta}"], writes=[f"TMP{tb}"])
                            P.op("dve", lambda e, tb=tb, p2=p2, N=N: e.scalar_tensor_tensor(TMP[tb][:, 0:N], PSB[p2][:, 0:N], 1.0 / 128, TMP[tb][:, 0:N], ALU.mult, ALU.subtract),
                                 reads=[f"ps{p2}", f"TMP{tb}"], writes=[f"TMP{tb}"])
                            P.op("act", lambda e, tb=tb, N=N: e.activation(TMP[tb][:, 0:N], TMP[tb][:, 0:N], AF.Sqrt, bias=cst[:, 1:2], scale=1.0),
                                 reads=[f"TMP{tb}", "cst"], writes=[f"TMP{tb}"])
                            P.op("dve", lambda e, tb=tb, N=N: e.reciprocal(TMP[tb][:, 0:N], TMP[tb][:, 0:N]), reads=[f"TMP{tb}"], writes=[f"TMP{tb}"])
                            P.op("dve", lambda e, ta=ta, N=N: e.tensor_tensor(TMP[ta][:, 0:N], OS[0][:, 0:N].bitcast(F32), TMP[ta][:, 0:N], ALU.subtract),
                                 reads=["OS0", f"TMP{ta}"], writes=[f"TMP{ta}"])
                            P.op("dve", lambda e, ta=ta, tb=tb, N=N: e.tensor_tensor(TMP[ta][:, 0:N], TMP[ta][:, 0:N], TMP[tb][:, 0:N], ALU.mult),
                                 reads=[f"TMP{ta}", f"TMP{tb}"], writes=[f"TMP{ta}"])
                            P.op("dve", lambda e, ta=ta, h=h, i0=i0, N=N: e.tensor_tensor(RET[:, h, i0:i0 + N], TMP[ta][:, 0:N], Grow[:, i0:i0 + N], ALU.mult),
                                 reads=[f"TMP{ta}", f"B{RG}_{n}"], writes=xr("Y", [h], [n]))
                        deferred_norms.append(_norm_part)
                    for _f in deferred_norms:
                        _f()
                if tile == 0 and l == 0:
                    dump("ret", RET.bitcast(F32), xr("Y", range(4), range(NBLK)))
                branch_out(w_ret_o[l], 4, RET, "Y", 1, nb == 0, False)
                nb += 1
            if "hy" in branches:
                latent = (tile == 1)
                L = 1024 if latent else 256
                li = 1 if latent else 0
                nseq = T // L
                nfc = L // 128
                NB = 256
                YOUT = SCR[:, 12:16, :]
                TWO_PI = 2.0 * math.pi
                MAGIC = 12582912.0

                def rr(r, ns=range(NBLK)):
                    return [f"B{r}_{n}" for n in ns]

                def rowpair_tm(r0):
                    return BIG[:, r0:r0 + 2, :].rearrange("p a t -> p (a t)").rearrange("p (c e) -> p c e", c=8)

                nzb = (L + 511) // 512
                P.dma([(PT[zb][0:33, 0:min(512, L)], zT_d[li][:, zb * 512:zb * 512 + min(512, L)]) for zb in range(nzb)],
                      "zt", writes=[f"PT{zb}" for zb in range(nzb)], eng="pool")
                shw = load_w(lambda w: [(w[0:33, 0:64], hw1_d[:, l, :]), (w[0:64, 64:128], hw2_d[:, l, :])], half=True)
                for layer_i in range(2):
                    src = PT if layer_i == 0 else OS
                    dst = OS if layer_i == 0 else PT
                    sn = "PT" if layer_i == 0 else "OS"
                    dn = "OS" if layer_i == 0 else "PT"
                    kk = 33 if layer_i == 0 else 64
                    wmat = WS[shw][0:33, 0:64] if layer_i == 0 else WS[shw][0:64, 64:128]
                    fq = HYS[:, l, 2 + layer_i:3 + layer_i]
                    fb = HYF[:, l, layer_i:layer_i + 1]
                    for zb in range(nzb):
                        wdt = min(512, L)
                        pi = next_ps()
                        P.op("pe", lambda e, pi=pi, wmat=wmat, src=src, zb=zb, kk=kk, wdt=wdt: e.matmul(
                            PSB[pi][0:64, 0:wdt], wmat, src[zb][0:kk, 0:wdt], start=True, stop=True),
                            reads=[f"WS{shw}", f"{sn}{zb}"], writes=[f"ps{pi}"])
                        P.op("dve", lambda e, pi=pi, fq=fq, fb=fb, wdt=wdt: e.tensor_scalar(TMP[0][0:64, 0:wdt], PSB[pi][0:64, 0:wdt], fq, fb, ALU.mult, ALU.add),
                             reads=[f"ps{pi}", "HYS", "HYF"], writes=["TMP0"])
                        P.op("dve", lambda e, wdt=wdt: e.tensor_scalar(TMP[1][0:64, 0:wdt], TMP[0][0:64, 0:wdt], 1.0 / TWO_PI, MAGIC, ALU.mult, ALU.add),
                             reads=["TMP0"], writes=["TMP1"])
                        P.op("dve", lambda e, wdt=wdt: e.tensor_scalar(TMP[1][0:64, 0:wdt], TMP[1][0:64, 0:wdt], -MAGIC, None, ALU.add),
                             reads=["TMP1"], writes=["TMP1"])
                        P.op("dve", lambda e, wdt=wdt: e.scalar_tensor_tensor(TMP[0][0:64, 0:wdt], TMP[1][0:64, 0:wdt], -TWO_PI, TMP[0][0:64, 0:wdt], ALU.mult, ALU.add),
                             reads=["TMP0", "TMP1"], writes=["TMP0"])
                        P.op("dve", lambda e, wdt=wdt: e.tensor_scalar(TMP[0][0:64, 0:wdt], TMP[0][0:64, 0:wdt], 3.141592, -3.141592, ALU.min, ALU.max),
                             reads=["TMP0"], writes=["TMP0"])
                        P.op("act", lambda e, dst=dst, zb=zb, wdt=wdt: e.activation(dst[zb][0:64, 0:wdt], TMP[0][0:64, 0:wdt], AF.Sin),
                             reads=["TMP0"], writes=[f"{dn}{zb}"])
                HID = PT

                def hy_proj_conv(col0, jch0, dst_rows, raw_row):
                    s_ = load_w(lambda w: [(w[:, 0:2048].rearrange("p (k n) -> p k n", k=8),
                                            wl[:, col0:col0 + 256].rearrange("(k p) n -> p k n", p=128))], half=True)
                    wv_ = wview(s_, 8, 256)
                    raw = BIG[:, raw_row, :]
                    rawf = raw.bitcast(F32)
                    for cc in range(2):
                        j = jch0 + cc
                        zrow = BIG[:, dst_rows + cc, :]
                        zrowf = zrow.bitcast(F32)
                        for n in range(NBLK):
                            pi = next_ps()
                            for k in R8:
                                P.op("pe", lambda e, pi=pi, wv_=wv_, k=k, n=n, cc=cc: e.matmul(PSB[pi][:], wv_[:, k, cc * 128:(cc + 1) * 128], H[:, k, BLK[n]],
                                                                                           start=(k == 0), stop=(k == 7)),
                                     reads=[f"WS{s_}"] + xr("H", [k], [n]), writes=[f"ps{pi}"])
                            P.op("act", lambda e, pi=pi, n=n, raw=raw: e.copy(raw[:, BLK[n]], PSB[pi][:]), reads=[f"ps{pi}"], writes=[f"B{raw_row}_{n}"])
                        w0 = HYC[:, l, 0, j:j + 1]
                        w1 = HYC[:, l, 1, j:j + 1]
                        w2 = HYC[:, l, 2, j:j + 1]
                        bb = HYC[:, l, 3, j:j + 1]
                        P.op("act", lambda e, zrow=zrow, rawf=rawf, w1=w1, bb=bb: e.activation(zrow, rawf, AF.Identity, bias=bb, scale=w1),
                             reads=rr(raw_row) + ["HYC"], writes=rr(dst_rows + cc))
                        for sq in range(nseq):
                            a0 = sq * L
                            P.op("dve", lambda e, zrow=zrow, zrowf=zrowf, rawf=rawf, w0=w0, a0=a0: e.scalar_tensor_tensor(
                                zrow[:, a0 + 1:a0 + L], rawf[:, a0:a0 + L - 1], w0, zrowf[:, a0 + 1:a0 + L], ALU.mult, ALU.add),
                                reads=rr(raw_row) + rr(dst_rows + cc) + ["HYC"], writes=rr(dst_rows + cc))
                            P.op("dve", lambda e, zrow=zrow, zrowf=zrowf, rawf=rawf, w2=w2, a0=a0: e.scalar_tensor_tensor(
                                zrow[:, a0:a0 + L - 1], rawf[:, a0 + 1:a0 + L], w2, zrowf[:, a0:a0 + L - 1], ALU.mult, ALU.add),
                                reads=rr(raw_row) + rr(dst_rows + cc) + ["HYC"], writes=rr(dst_rows + cc))

                def to_tm(src_rows, tm_r0):
                    tmv = rowpair_tm(tm_r0)
                    for cc in range(2):
                        srcf = BIG[:, src_rows + cc, :].bitcast(F32)
                        for n in range(NBLK):
                            pt = next_ps()
                            for c in range(4):
                                P.op("pe", lambda e, pt=pt, c=c, n=n, srcf=srcf: e.transpose(PSB[pt][:, c * 128:(c + 1) * 128],
                                                                                          srcf[:, n * 512 + c * 128:n * 512 + (c + 1) * 128], ident[:]),
                                     reads=[f"B{src_rows + cc}_{n}", "ident"], writes=[f"ps{pt}"])
                            P.op("act", lambda e, pt=pt, n=n, cc=cc, tmv=tmv: e.copy(tmv[:, n * 4:(n + 1) * 4, cc * 128:(cc + 1) * 128],
                                                                                   PSB[pt][:].rearrange("p (c e) -> p c e", c=4)),
                                 reads=[f"ps{pt}"], writes=rr(tm_r0) + rr(tm_r0 + 1))

                for cb in range(2):
                    pairs3 = [(16, 18, 20), (20, 16, 18)]
                    hy_proj_conv(2560 + 1024 + cb * 256, 8 + cb * 2, 18, 20)
                    to_tm(18, 16)
                    for o in range(2):
                        tm_in, r_sum, r_diff = pairs3[o]
                        tm_out = 20
                        tmv = rowpair_tm(tm_in)
                        tsum = rowpair_tm(r_sum)
                        tdiff = rowpair_tm(r_diff)
                        TM_IN = rr(tm_in) + rr(tm_in + 1)
                        TSUM = rr(r_sum) + rr(r_sum + 1)
                        TDIFF = rr(r_diff) + rr(r_diff + 1)
                        YHre = rowpair_tm(8)
                        YHim = rowpair_tm(10)
                        YRE = rr(8) + rr(9)
                        YIM = rr(10) + rr(11)
                        s3 = load_w(lambda w, o=o, cb=cb: [(w[0:64, 0:256], hy_w3[l][:, o * 512 + cb * 256:o * 512 + (cb + 1) * 256]),
                                                            (w[0:64, 256:512], hy_w3[l][:, 1024 + o * 512 + cb * 256:1024 + o * 512 + (cb + 1) * 256])], half=True)
                        P.dma([(TMP[3][:, 0:256], rate_d[:, cb * 256:(cb + 1) * 256])], "rate", writes=["TMP3"])
                        NRM = RSTD[:, 0:256]
                        def tap_stages(tc, b):
                            cs_ = slice(b * 256, (b + 1) * 256)
                            stt = {}
                            F_, G_, Wn_ = TMP[0][:, cs_], TMP[1][:, cs_], TMP[2][:, cs_]
                            nF, nG, nW = f"TMP0h{b}", f"TMP1h{b}", f"TMP2h{b}"

                            def s0():
                                stt["pi"] = next_ps()
                                P.op("pe", lambda e: e.matmul(PSB[stt["pi"]][:], HID[tc // 4][0:64, (tc % 4) * 128:(tc % 4 + 1) * 128], WS[s3][0:64, 0:512], start=True, stop=True),
                                     reads=[f"PT{tc // 4}", f"WS{s3}"], writes=[f"ps{stt['pi']}"])
                                P.op("act", lambda e: e.activation(Wn_, TMP[3][:, 0:256], AF.Exp, scale=NTN[:, li, tc:tc + 1]), reads=["TMP3", "NTN"], writes=[nW])

                            def s1():
                                P.op("dve", lambda e: e.tensor_tensor(F_, PSB[stt["pi"]][:, 0:256], Wn_, ALU.mult), reads=[f"ps{stt['pi']}", nW], writes=[nF])
                                P.op("dve", lambda e: e.tensor_tensor(G_, PSB[stt["pi"]][:, 256:512], Wn_, ALU.mult), reads=[f"ps{stt['pi']}", nW], writes=[nG])
                                if tc == 0:
                                    P.op("dve", lambda e: e.memset(TMP[1][0:1, cs_], 0.0), reads=[nG], writes=[nG])

                            def s2():
                                P.op("dve", lambda e: e.tensor_tensor(tsum[:, tc, :], F_, G_, ALU.add), reads=[nF, nG], writes=TSUM)
                                P.op("dve", lambda e: e.tensor_tensor(tdiff[:, tc, :], F_, G_, ALU.subtract), reads=[nF, nG], writes=TDIFF)

                            def s3_():
                                P.op("act", lambda e: e.activation(OS[b][:, 0:256], F_, AF.Square), reads=[nF], writes=[f"OS{b}"])
                                P.op("act", lambda e: e.activation(OS[b][:, 256:512], G_, AF.Square), reads=[nG], writes=[f"OS{b}"])

                            def s4():
                                P.op("pe", lambda e: e.matmul(PSB[6][:], ones[:], OS[b][:], start=(tc == 0), stop=(tc == nfc - 1)), reads=["ones", f"OS{b}"], writes=["ps6"])
                            return [s0, s1, s2, s3_, s4]

                        for tp_ in range(nfc // 2):
                            for fa, fb in zip(tap_stages(2 * tp_, 0), tap_stages(2 * tp_ + 1, 1)):
                                fa()
                                fb()
                        P.op("dve", lambda e: e.tensor_copy(TMP[0][:, 0:256], PSB[6][:, 0:256]), reads=["ps6"], writes=["TMP0"])
                        P.op("dve", lambda e: e.tensor_tensor(TMP[0][:, 0:256], TMP[0][:, 0:256], PSB[6][:, 256:512], ALU.add), reads=["ps6", "TMP0"], writes=["TMP0"])
                        P.op("act", lambda e: e.activation(NRM, TMP[0][:, 0:256], AF.Sqrt, bias=cst[:, 0:1], scale=1.0), reads=["TMP0", "cst"], writes=["RSTD"])
                        P.op("dve", lambda e: e.reciprocal(NRM, NRM), reads=["RSTD"], writes=["RSTD"])
                        fpg = 2
                        for fg in range(nfc // fpg):
                            ncol = fpg * 128
                            sF = []
                            for ri in range(2):
                                sF.append(load_w(lambda w, ri=ri, fg=fg, ncol=ncol: [(w[:, 0:nfc * ncol].rearrange("p (k n) -> p k n", k=nfc),
                                                                                      dftF_d[li][ri][:, fg * ncol:(fg + 1) * ncol].rearrange("(k p) n -> p k n", p=128))], half=True))
                            Fv = [wview(sF[ri], nfc, ncol) for ri in range(2)]
                            for fl in range(fpg):
                                fc = fg * fpg + fl
                                fsl = slice(fl * 128, (fl + 1) * 128)
                                pk = [next_ps(), next_ps()]
                                for ri, (tab, TAB) in enumerate(((tsum, TSUM), (tdiff, TDIFF))):
                                    for tc in range(nfc):
                                        P.op("pe", lambda e, ri=ri, tc=tc, tab=tab, fsl=fsl, pk=pk: e.matmul(PSB[pk[ri]][:, 0:256], Fv[ri][:, tc, fsl], tab[:, tc, :],
                                                                                                          start=(tc == 0), stop=(tc == nfc - 1)),
                                             reads=[f"WS{sF[ri]}"] + TAB, writes=[f"ps{pk[ri]}"])
                                P.op("dve", lambda e, pk=pk: e.tensor_tensor(TMP[0][:, 0:256], PSB[pk[0]][:, 0:256], NRM, ALU.mult), reads=[f"ps{pk[0]}", "RSTD"], writes=["TMP0"])
                                P.op("dve", lambda e, pk=pk: e.tensor_tensor(TMP[1][:, 0:256], PSB[pk[1]][:, 0:256], NRM, ALU.mult), reads=[f"ps{pk[1]}", "RSTD"], writes=["TMP1"])
                                for sq in range(nseq):
                                    q = sq * nfc + fc
                                    px = [next_ps(), next_ps()]
                                    for ri in range(2):
                                        for sc in range(nfc):
                                            P.op("pe", lambda e, ri=ri, sc=sc, sq=sq, fsl=fsl, px=px: e.matmul(PSB[px[ri]][:, 0:256], Fv[ri][:, sc, fsl], tmv[:, sq * nfc + sc, :],
                                                                                                            start=(sc == 0), stop=(sc == nfc - 1)),
                                                 reads=[f"WS{sF[ri]}"] + TM_IN, writes=[f"ps{px[ri]}"])
                                    P.op("dve", lambda e, px=px: e.tensor_tensor(TMP[2][:, 0:256], PSB[px[0]][:, 0:256], TMP[0][:, 0:256], ALU.mult), reads=[f"ps{px[0]}", "TMP0"], writes=["TMP2"])
                                    P.op("dve", lambda e, px=px: e.tensor_tensor(TMP[3][:, 0:256], PSB[px[1]][:, 0:256], TMP[1][:, 0:256], ALU.mult), reads=[f"ps{px[1]}", "TMP1"], writes=["TMP3"])
                                    P.op("dve", lambda e, q=q: e.tensor_tensor(YHre[:, q, :], TMP[2][:, 0:256], TMP[3][:, 0:256], ALU.subtract), reads=["TMP2", "TMP3"], writes=YRE)
                                    P.op("dve", lambda e, px=px: e.tensor_tensor(TMP[2][:, 0:256], PSB[px[0]][:, 0:256], TMP[1][:, 0:256], ALU.mult), reads=[f"ps{px[0]}", "TMP1"], writes=["TMP2"])
                                    P.op("dve", lambda e, px=px: e.tensor_tensor(TMP[3][:, 0:256], PSB[px[1]][:, 0:256], TMP[0][:, 0:256], ALU.mult), reads=[f"ps{px[1]}", "TMP0"], writes=["TMP3"])
                                    P.op("dve", lambda e, q=q: e.tensor_tensor(YHim[:, q, :], TMP[2][:, 0:256], TMP[3][:, 0:256], ALU.add), reads=["TMP2", "TMP3"], writes=YIM)
                        hy_proj_conv(2560 + o * 512 + cb * 256, o * 4 + cb * 2, r_sum, r_diff)
                        for sq in range(nseq):
                            for tb in range(L // NB):
                                t0 = sq * L + tb * NB
                                n = t0 // 512
                                if not (L == NB and sq > 0):
                                    sG = []
                                    for ri in range(2):
                                        sG.append(load_w(lambda w, ri=ri, tb=tb: [(w[:, 0:nfc * NB].rearrange("p (k n) -> p k n", k=nfc),
                                                                                    dftG_d[li][ri][:, tb * NB:(tb + 1) * NB].rearrange("(k p) n -> p k n", p=128))], half=True))
                                    Gv = [wview(sG[ri], nfc, NB) for ri in range(2)]
                                def inv_stages(cc):
                                    csl = slice(cc * 128, (cc + 1) * 128)
                                    T0i, T1i = 2 * cc, 2 * cc + 1
                                    TA = TMP[T0i][:, 0:NB]
                                    TB = TMP[T1i][:, 0:NB]
                                    stt = {}
                                    hb = HYB[:, l, o, cb * 2 + cc:cb * 2 + cc + 1]
                                    grow = BIG[:, r_sum + cc, t0:t0 + NB].bitcast(F32)

                                    def s0():
                                        stt["pc"] = next_ps()
                                        cnt = 0
                                        for ri, (YH, YN) in enumerate(((YHre, YRE), (YHim, YIM))):
                                            for fc in range(nfc):
                                                P.op("pe", lambda e: e.matmul(PSB[stt["pc"]][:, 0:NB], YH[:, sq * nfc + fc, csl], Gv[ri][:, fc, :],
                                                                              start=(cnt == 0), stop=(cnt == 2 * nfc - 1)),
                                                     reads=[f"WS{sG[ri]}"] + YN, writes=[f"ps{stt['pc']}"])
                                                cnt += 1

                                    def s1():
                                        stt["pv"] = next_ps()
                                        for c in range(NB // 128):
                                            tcg = t0 // 128 + c
                                            P.op("pe", lambda e: e.transpose(PSB[stt["pv"]][:, c * 128:(c + 1) * 128], tmv[:, tcg, csl].bitcast(F32), ident[:]),
                                                 reads=TM_IN + ["ident"], writes=[f"ps{stt['pv']}"])

                                    def s2():
                                        P.op("act", lambda e: e.activation(TA, PSB[stt["pv"]][:, 0:NB], AF.Identity, scale=hb), reads=[f"ps{stt['pv']}", "HYB"], writes=[f"TMP{T0i}"])

                                    def s3():
                                        P.op("dve", lambda e: e.tensor_tensor(TA, PSB[stt["pc"]][:, 0:NB], TA, ALU.add), reads=[f"ps{stt['pc']}", f"TMP{T0i}"], writes=[f"TMP{T0i}"])

                                    def s4():
                                        if o == 1:
                                            P.op("dve", lambda e: e.tensor_tensor(YOUT[:, cb * 2 + cc, t0:t0 + NB], TA, grow, ALU.mult),
                                                 reads=[f"TMP{T0i}", f"B{r_sum + cc}_{n}"], writes=xr("Y", [cb * 2 + cc], [n]))
                                        else:
                                            P.op("dve", lambda e: e.tensor_tensor(TB, TA, grow, ALU.mult),
                                                 reads=[f"TMP{T0i}", f"B{r_sum + cc}_{n}"], writes=[f"TMP{T1i}"])

                                    def s5():
                                        if o == 0:
                                            stt["pz"] = next_ps()
                                            for c in range(NB // 128):
                                                P.op("pe", lambda e: e.transpose(PSB[stt["pz"]][:, c * 128:(c + 1) * 128], TMP[T1i][:, c * 128:(c + 1) * 128], ident[:]),
                                                     reads=[f"TMP{T1i}", "ident"], writes=[f"ps{stt['pz']}"])

                                    def s6():
                                        if o == 0:
                                            tmo = rowpair_tm(tm_out)
                                            tc0 = t0 // 128
                                            nt = NB // 128
                                            P.op("act", lambda e: e.copy(tmo[:, tc0:tc0 + nt, csl], PSB[stt["pz"]][:, 0:nt * 128].rearrange("p (c e) -> p c e", c=nt)),
                                                 reads=[f"ps{stt['pz']}"], writes=rr(tm_out) + rr(tm_out + 1))
                                    return [s0, s1, s2, s3, s4, s5, s6]

                                for fa, fb in zip(inv_stages(0), inv_stages(1)):
                                    fa()
                                    fb()
                if tile == 0 and l == 0:
                    dump("hy", YOUT.bitcast(F32), xr("Y", range(4), range(NBLK)))
                branch_out(w_hy_o[l], 4, YOUT, "Y", 2, nb == 0, False)
                nb += 1
            if nb == 0:
                for m in R8:
                    P.op("dve", lambda e, m=m: e.memset(MERGED[:, m, :], 0.0), writes=xr("MG", [m], range(NBLK)))

            def epi_res(gi, mp=mp):
                gap = mp(gi)

                def f(m, n, pi):
                    P.op("dve", lambda e, m=m, n=n, pi=pi: e.scalar_tensor_tensor(
                        X[:, m, BLK[n]], PSB[pi][:], gap[:, m:m + 1], X[:, m, BLK[n]], ALU.mult, ALU.add),
                        reads=[f"ps{pi}", "MODP"] + xr("X", [m], [n]), writes=xr("X", [m], [n]))
                return f
            proj_fm(w_out[l], 0, 1024, 8, MERGED, "MG", epi_res(2))
            if tile == 0 and l == 0:
                dump("x1", X[:], xr("X", R8, range(NBLK)))

            rms_norm(H, "H", mp(3), mp(4))
            GA = BIG
            for i in range(11):
                def pairs(w, i=i, l=l):
                    return [(w[:, 0:2048].rearrange("p (k n) -> p k n", k=8),
                             w_ffn_in[l][:, i * 256:(i + 1) * 256].rearrange("(k p) n -> p k n", p=128)),
                            (w[:, 2048:4096].rearrange("p (k n) -> p k n", k=8),
                             w_ffn_in[l][:, DFF + i * 256:DFF + (i + 1) * 256].rearrange("(k p) n -> p k n", p=128))]
                s = load_w(pairs)
                wa = wview(s, 8, 256, 0)
                wb = wview(s, 8, 256, 2048)
                for mm in range(2):
                    for n in range(NBLK):
                        pa = next_ps()
                        for k in R8:
                            P.op("pe", lambda e, pa=pa, wa=wa, k=k, n=n, mm=mm: e.matmul(PSB[pa][:], wa[:, k, mm * 128:(mm + 1) * 128], H[:, k, BLK[n]],
                                                                                     start=(k == 0), stop=(k == 7)),
                                 reads=[f"WS{s}"] + xr("H", [k], [n]), writes=[f"ps{pa}"])
                        pb = next_ps()
                        for k in R8:
                            P.op("pe", lambda e, pb=pb, wb=wb, k=k, n=n, mm=mm: e.matmul(PSB[pb][:], wb[:, k, mm * 128:(mm + 1) * 128], H[:, k, BLK[n]],
                                                                                     start=(k == 0), stop=(k == 7)),
                                 reads=[f"WS{s}"] + xr("H", [k], [n]), writes=[f"ps{pb}"])
                        t0 = next_tmp()
                        P.op("act", lambda e, t0=t0, pa=pa: e.activation(TMP[t0][:], PSB[pa][:], AF.Silu),
                             reads=[f"ps{pa}"], writes=[f"TMP{t0}"])
                        j = i * 2 + mm
                        P.op("dve", lambda e, t0=t0, pb=pb, j=j, n=n: e.tensor_tensor(GA[:, j, BLK[n]], TMP[t0][:], PSB[pb][:], ALU.mult),
                             reads=[f"TMP{t0}", f"ps{pb}"], writes=xr("GA", [j], [n]))
            for m in R8:
                def pairs(w, m=m, l=l):
                    return [(w[:, 0:22 * 128].rearrange("p (k n) -> p k n", k=22),
                             w_ffn_out[l][:, m * 128:(m + 1) * 128].rearrange("(k p) n -> p k n", p=128))]
                s = load_w(pairs)
                wv = wview(s, 22, 128)
                for n in range(NBLK):
                    pi = next_ps()
                    for k in range(22):
                        P.op("pe", lambda e, pi=pi, wv=wv, k=k, n=n: e.matmul(PSB[pi][:], wv[:, k, :], GA[:, k, BLK[n]],
                                                                              start=(k == 0), stop=(k == 21)),
                             reads=[f"WS{s}"] + xr("GA", [k], [n]), writes=[f"ps{pi}"])
                    epi_res(5)(m, n, pi)
            if tile == 0 and l == 0:
                dump("x2", X[:], xr("X", R8, range(NBLK)))
        rms_norm(H, "H", nf, None)
        P.dma([(yT[tile][:, k, :], H[:, k, :].bitcast(F32)) for k in R8], "y_out", reads=xr("H", R8, range(NBLK)))

    P.wait_all("sp")
    P.emit()
    P.close()
    return nc


def _fm(x2d):
    t = x2d.shape[0]
    return np.ascontiguousarray(x2d.T.reshape(8, 128, t).transpose(1, 0, 2))


def _unfm(y):
    t = y.shape[2]
    return np.ascontiguousarray(y.transpose(1, 0, 2).reshape(1024, t).T)


def _vec_fm(v, nch):
    return np.ascontiguousarray(v.reshape(nch, 128).T)


def prep_core_inputs(inp, core):
    f = np.float32
    b = core % 4
    m = {}
    m["xT_p"] = _fm(inp["x_prompt"][core * 4:(core + 1) * 4].reshape(1024, 1024))
    m["xT_s"] = _fm(inp["x_sample"][b])
    m["cond"] = np.ascontiguousarray(np.stack([_vec_fm(inp["c_ctx"], 8), _vec_fm(inp["c"][b], 8)], axis=-1))
    m["b_mod_t"] = np.ascontiguousarray(np.stack([_vec_fm(inp["b_mod"][l], 48) for l in range(DEPTH)], axis=1))
    m["norm1_t"] = np.ascontiguousarray(np.stack([_vec_fm(inp["norm1"][l], 8) for l in range(DEPTH)], axis=1))
    m["norm2_t"] = np.ascontiguousarray(np.stack([_vec_fm(inp["norm2"][l], 8) for l in range(DEPTH)], axis=1))
    m["normf_t"] = _vec_fm(inp["norm_f"], 8)
    m["ident"] = np.eye(128, dtype=f)
    m.update(_consts())
    m["hy_w3"] = inp["hy_w3"]
    hyc = np.zeros((128, DEPTH, 4, 12), f)
    for l in range(DEPTH):
        for tp in range(3):
            hyc[:, l, tp, :] = _vec_fm(inp["hy_conv_w"][l, tp], 12)
        hyc[:, l, 3, :] = _vec_fm(inp["hy_conv_b"][l], 12)
    m["hyc"] = hyc
    hyb = np.zeros((128, DEPTH, 2, 4), f)
    for l in range(DEPTH):
        for o in range(2):
            hyb[:, l, o, :] = _vec_fm(inp["hy_bias"][l, o], 4)
    m["hyb"] = hyb
    m["hw1"] = np.ascontiguousarray(inp["hy_w1"].transpose(1, 0, 2))
    m["hw2"] = np.ascontiguousarray(inp["hy_w2"].transpose(1, 0, 2))
    hys = np.zeros((64, DEPTH, 4), f)
    for l in range(DEPTH):
        hys[:, l, 0] = inp["hy_b1"][l]
        hys[:, l, 1] = inp["hy_b2"][l]
        hys[:, l, 2] = inp["hy_freq"][l, 0]
        hys[:, l, 3] = inp["hy_freq"][l, 1]
    m["hys"] = hys
    def sp_layout(a):
        return np.ascontiguousarray(a.reshape(2, 16, 2, 64).transpose(2, 3, 0, 1).reshape(128, 32))
    s5sp = np.zeros((128, DEPTH, 3, 32), f)
    for l in range(DEPTH):
        s5sp[:, l, 0] = sp_layout(inp["s5_lam_re"][l])
        s5sp[:, l, 1] = sp_layout(inp["s5_lam_im"][l])
        s5sp[:, l, 2] = sp_layout(np.broadcast_to(inp["s5_log_dt"][l][:, :, None], (2, 32, 64)))
    m["s5sp"] = s5sp
    h0 = inp["state_s5"][b]
    s5h0 = np.zeros((128, DEPTH, 32, 2), f)
    for l in range(DEPTH):
        for ri in range(2):
            s5h0[:, l, :, ri] = sp_layout(h0[l, :, :, :, ri])
    m["s5h0"] = s5h0
    s5bz = np.zeros((DEPTH, 128, 2, 32, 2, 16), f)
    s5cz = np.zeros((DEPTH, 128, 32, 2, 2, 16), f)
    for l in range(DEPTH):
        for ri, key in enumerate(("s5_b_re", "s5_b_im")):
            Bq = inp[key][l].reshape(2, 16, 2, 64, 16)
            for gl in range(2):
                s5bz[l, gl * 64:(gl + 1) * 64, ri, :, gl, :] = Bq[:, :, gl].transpose(2, 0, 1, 3).reshape(64, 32, 16)
        for ri, key in enumerate(("s5_c_re", "s5_c_im")):
            Cq = inp[key][l].reshape(2, 16, 2, 16, 64)
            for gl in range(2):
                s5cz[l, gl * 64:(gl + 1) * 64, :, ri, gl, :] = Cq[:, :, gl].transpose(3, 0, 1, 2).reshape(64, 32, 16)
    m["s5bz"] = s5bz.reshape(DEPTH, 128, 2048)
    m["s5cz"] = s5cz.reshape(DEPTH, 128, 2048)
    m["s5d"] = np.ascontiguousarray(np.stack([_vec_fm(inp["s5_d"][l], 4) for l in range(DEPTH)], axis=1))
    m["ret_decay_bc"] = np.ascontiguousarray(np.broadcast_to(inp["ret_decay"].reshape(1, DEPTH * 8), (128, DEPTH * 8)))
    m["state_ret_c"] = np.ascontiguousarray(inp["state_ret"][b])
    for k in ("w_mod", "w_in", "w_s5_glu", "w_ret_o", "w_hy_o", "w_out", "w_ffn_in", "w_ffn_out"):
        m[k] = np.ascontiguousarray(inp[k])
    return {k: np.asarray(v, dtype=f) for k, v in m.items()}


_CONST_CACHE = {}


def _consts():
    if _CONST_CACHE:
        return _CONST_CACHE
    f = np.float32
    c = {}
    p = np.arange(128, dtype=f)[:, None]
    c["ramp_p"] = (np.arange(512, dtype=f)[None, :] - p - 256.0).astype(f)
    c["ramp_s"] = (np.arange(2048, dtype=f)[None, :] - p - 1024.0).astype(f)
    L = 1024
    rows = np.repeat(np.arange(L // 64, dtype=f), 64)
    cols = np.tile(np.arange(64, dtype=f), L // 64)
    inv = (f(10000.0) ** (-np.arange(32, dtype=f) / f(32))).astype(f)
    ang = np.concatenate([rows[:, None] * inv, cols[:, None] * inv], axis=-1).astype(f)
    cos = np.cos(ang).astype(f).T
    sin = np.sin(ang).astype(f).T
    c["rope_cs"] = np.ascontiguousarray(np.stack([np.concatenate([cos, cos], 0), np.concatenate([-sin, sin], 0)], 0))
    i = np.arange(1024, dtype=f)
    c["pos12"] = np.ascontiguousarray(np.stack([np.broadcast_to(i + 1.0, (128, 1024)), np.broadcast_to(1023.0 - i, (128, 1024))], 0)).astype(f)
    pp = np.arange(128, dtype=f)
    posT = np.zeros((128, 2, 2), f)
    for cl in range(2):
        posT[:, 0, cl] = 255.0 - (cl * 128 + pp)
        posT[:, 1, cl] = cl * 128 + pp
    c["posT"] = posT
    perm = np.zeros((128, 128), f)
    for mcol in range(128):
        perm[(mcol + 64) % 128, mcol] = 1.0
    c["perm"] = perm
    zT = np.zeros((2, 33, 1024), f)
    ntn = np.zeros((128, 2, 8), f)
    bands = np.linspace(1e-4, 15, 16, dtype=f)
    for li, Lh in enumerate((256, 1024)):
        t = np.arange(Lh, dtype=f)
        tn = (t / f(Lh)).astype(f)
        angz = (f(2.0 * math.pi / Lh) * t[:, None] * bands[None, :]).astype(f)
        z = np.concatenate([tn[:, None], np.cos(angz), -np.sin(angz)], axis=-1).astype(f)
        zT[li, :, :Lh] = z.T
        for tc in range(Lh // 128):
            ntn[:, li, tc] = -(tc * 128 + np.arange(128, dtype=f)) / f(Lh)
        s64 = np.arange(Lh, dtype=np.float64)
        th = 2.0 * np.pi * (s64[None, :] + 0.5) / (2.0 * Lh)
        ang = s64[:, None] * th
        F = np.stack([np.cos(ang), -np.sin(ang)], 0).astype(f)
        G = np.stack([np.cos(ang).T / Lh, -np.sin(ang).T / Lh], 0).astype(f)
        key = "p" if Lh == 256 else "s"
        c["dftF_" + key] = np.ascontiguousarray(F)
        c["dftG_" + key] = np.ascontiguousarray(G)
    c["zT"] = zT
    c["ntn"] = ntn
    dmin = -math.log(1e-2) / 1.5
    dmax = -math.log(1e-2) / 0.3
    rate = np.linspace(dmin, dmax, 512, dtype=f)
    c["tpos"] = np.ascontiguousarray(np.broadcast_to(np.arange(512, dtype=f), (128, 512))).astype(f)
    c["rate_bc"] = np.ascontiguousarray(np.broadcast_to(rate, (128, 512))).astype(f)
    _CONST_CACHE.update(c)
    return _CONST_CACHE


_NC_CACHE = {}


def kernel(**inputs):
    inp = {k: np.asarray(v) for k, v in inputs.items()}
    if "nc" not in _NC_CACHE:
        _NC_CACHE["nc"] = build_program()
    nc = _NC_CACHE["nc"]
    in_maps = [prep_core_inputs(inp, c) for c in range(8)]
    res = run_bass_kernel_spmd(nc, in_maps, core_ids=list(range(8)))
    y_prompt = np.zeros((32, 256, 1024), np.float32)
    y_sample = np.zeros((4, 1024, 1024), np.float32)
    for c in range(8):
        r = res.results[c]
        y_prompt[c * 4:(c + 1) * 4] = _unfm(r["yT_p"]).reshape(4, 256, 1024)
        if c < 4:
            y_sample[c] = _unfm(r["yT_s"])
    s5 = np.zeros((32, DEPTH, 2, 32, 64, 2), np.float32)
    ret = np.zeros((32, DEPTH, 2, 4, 128, 128), np.float32)
    for c in range(8):
        ret[c * 4:(c + 1) * 4] = res.results[c]["new_state_ret_c"]
        s5[c * 4:(c + 1) * 4] = res.results[c]["new_state_s5_c"]
    return (y_prompt, y_sample, s5, ret)
```
